# Optimizing a Trainium2 kernel written in Bass

```python
import math
import jax, jax.numpy as jnp
from jax import lax
import numpy as np

D_MODEL = 4096
BATCH = 8
SEQ = 2048
DEPTH = 2

GRID_W = 64
CTX_LEN = 256
N_MIXERS = 2
N_ATTN_LAYERS = (DEPTH + 1) // 2
N_CONV_LAYERS = DEPTH // 2
DIFF_HEAD_DIM = 128
DIFF_HEADS = D_MODEL // (2 * DIFF_HEAD_DIM)
V_HEAD_DIM = 2 * DIFF_HEAD_DIM
ATTN_SCALE = DIFF_HEAD_DIM ** -0.5
ROPE_DIM_PER_AXIS = DIFF_HEAD_DIM // 2
ROPE_BASE = 10000.0
CONV_WIDTH = 31
CONV_PAD = CONV_WIDTH // 2
D_FF = 4 * D_MODEL
Q_BLOCK = 128
RMS_EPS = 1e-6
LN_EPS = 1e-5

kernel_name = "hybrid_diffattn_conformer_dit_block"


def rms_norm(x, g):
    xf = x.astype(jnp.float32)
    y = xf * lax.rsqrt(jnp.mean(xf * xf, axis=-1, keepdims=True) + RMS_EPS)
    return (y * g.astype(jnp.float32)).astype(x.dtype)


def layer_norm(x, g, b):
    xf = x.astype(jnp.float32)
    mu = jnp.mean(xf, axis=-1, keepdims=True)
    var = jnp.mean(jnp.square(xf - mu), axis=-1, keepdims=True)
    y = (xf - mu) * lax.rsqrt(var + LN_EPS)
    return (y * g.astype(jnp.float32) + b.astype(jnp.float32)).astype(x.dtype)


def adaln(cvec, w, b):
    m = jax.nn.silu(cvec) @ w + b
    m = m.reshape(m.shape[:-1] + (1, 6, D_MODEL))
    return [m[..., j, :] for j in range(6)]


def modulate(h, shift, scale):
    return h * (1.0 + scale) + shift


def axial_rope_tables(length, dtype):
    rows = length // GRID_W
    row = jnp.repeat(jnp.arange(rows, dtype=jnp.float32), GRID_W)
    col = jnp.tile(jnp.arange(GRID_W, dtype=jnp.float32), rows)
    n = ROPE_DIM_PER_AXIS // 2
    inv = ROPE_BASE ** (-jnp.arange(n, dtype=jnp.float32) / n)
    ar = row[:, None] * inv
    ac = col[:, None] * inv
    ang = jnp.concatenate([ar, ar, ac, ac], axis=-1)
    return (jnp.cos(ang)[None, :, None, :].astype(dtype),
            jnp.sin(ang)[None, :, None, :].astype(dtype))


def apply_rope(x, cos, sin):
    h = ROPE_DIM_PER_AXIS // 2
    xr1, xr2 = x[..., :h], x[..., h:2 * h]
    xc1, xc2 = x[..., 2 * h:3 * h], x[..., 3 * h:]
    rot = jnp.concatenate([-xr2, xr1, -xc2, xc1], axis=-1)
    return x * cos + rot * sin


def diff_core(q, k, v, lam):
    s = jnp.einsum('bqhcd,bkhcd->bchqk', q, k).astype(jnp.float32) * ATTN_SCALE
    p = jax.nn.softmax(s, axis=-1)
    a = p[:, 0] - lam * p[:, 1]
    return jnp.einsum('bhqk,bkhe->bqhe', a.astype(v.dtype), v)


def diff_head_out(o, subln_g, lam_init, w_o):
    o = rms_norm(o, subln_g) * (1.0 - lam_init)
    return o.reshape(o.shape[:2] + (D_MODEL,)) @ w_o


def diff_attention(h_lat, h_ctx, w_qkv, w_o, lq1, lk1, lq2, lk2, subln_g, layer_idx, ctx_queries):
    B, L, D = h_lat.shape
    C = h_ctx.shape[1]
    H, dk, dv = DIFF_HEADS, DIFF_HEAD_DIM, V_HEAD_DIM
    lam_init = 0.8 - 0.6 * math.exp(-0.3 * layer_idx)
    f32 = jnp.float32
    lam = (jnp.exp(jnp.sum(lq1.astype(f32) * lk1.astype(f32)))
           - jnp.exp(jnp.sum(lq2.astype(f32) * lk2.astype(f32))) + lam_init)
    q, k, v = jnp.split(h_lat @ w_qkv, 3, axis=-1)
    cos, sin = axial_rope_tables(L, h_lat.dtype)
    q = apply_rope(q.reshape(B, L, 2 * H, dk), cos, sin).reshape(B, L, H, 2, dk)
    k = apply_rope(k.reshape(B, L, 2 * H, dk), cos, sin).reshape(B, L, H, 2, dk)
    v = v.reshape(B, L, H, dv)
    if ctx_queries:
        qc, kc, vc = jnp.split(h_ctx @ w_qkv, 3, axis=-1)
        qc = qc.reshape(B, C, H, 2, dk)
    else:
        kc, vc = jnp.split(h_ctx @ w_qkv[:, D:], 2, axis=-1)
    kc = kc.reshape(B, C, H, 2, dk)
    vc = vc.reshape(B, C, H, dv)
    k_all = jnp.concatenate([kc, k], axis=1)
    v_all = jnp.concatenate([vc, v], axis=1)
    nblk = L // Q_BLOCK
    q_blocks = q.reshape(B, nblk, Q_BLOCK, H, 2, dk).swapaxes(0, 1)
    o = lax.map(lambda qb: diff_core(qb, k_all, v_all, lam), q_blocks)
    o = o.swapaxes(0, 1).reshape(B, L, H, dv)
    y_lat = diff_head_out(o, subln_g, lam_init, w_o)
    y_ctx = None
    if ctx_queries:
        y_ctx = diff_head_out(diff_core(qc, kc, vc, lam), subln_g, lam_init, w_o)
    return y_lat, y_ctx


def conformer_conv(h, pw1_w, pw1_b, dw_w, dw_b, ln_g, ln_b, pw2_w, pw2_b):
    a, g = jnp.split(h @ pw1_w + pw1_b, 2, axis=-1)
    u = a * jax.nn.sigmoid(g)
    u = lax.conv_general_dilated(u, dw_w[:, None, :], window_strides=(1,),
                                 padding=((CONV_PAD, CONV_PAD),),
                                 dimension_numbers=('NWC', 'WIO', 'NWC'),
                                 feature_group_count=D_MODEL) + dw_b
    u = jax.nn.silu(layer_norm(u, ln_g, ln_b))
    return u @ pw2_w + pw2_b


def sq_relu_mlp(h, w1, w2):
    return jnp.square(jax.nn.relu(h @ w1)) @ w2


def setup_inputs(seed: int = 0) -> dict:
    key = jax.random.key(seed)
    ks = jax.random.split(key, 26)
    D = D_MODEL
    na, nc = N_ATTN_LAYERS, N_CONV_LAYERS

    def nrm(k, shape, s):
        return jax.random.normal(k, shape, jnp.float32) * s

    return {
        "x": nrm(ks[0], (BATCH, SEQ, D), 1.0),
        "c": nrm(ks[1], (BATCH, D), 1.0),
        "ctx": nrm(ks[2], (BATCH, CTX_LEN, D), 1.0),
        "c_ctx": nrm(ks[3], (D,), 1.0),
        "ada_w": nrm(ks[4], (DEPTH, D, 6 * D), D ** -0.5),
        "ada_b": nrm(ks[5], (DEPTH, 6 * D), 0.01),
        "norm_mix_g": 1.0 + nrm(ks[6], (DEPTH, D), 0.01),
        "norm_mlp_g": 1.0 + nrm(ks[7], (DEPTH, D), 0.01),
        "norm_final_g": 1.0 + nrm(ks[8], (D,), 0.01),
        "attn_w_qkv": nrm(ks[9], (na, D, 3 * D), D ** -0.5),
        "attn_w_o": nrm(ks[10], (na, D, D), D ** -0.5),
        "lambda_q1": nrm(ks[11], (na, DIFF_HEAD_DIM), 0.1),
        "lambda_k1": nrm(ks[12], (na, DIFF_HEAD_DIM), 0.1),
        "lambda_q2": nrm(ks[13], (na, DIFF_HEAD_DIM), 0.1),
        "lambda_k2": nrm(ks[14], (na, DIFF_HEAD_DIM), 0.1),
        "attn_subln_g": 1.0 + nrm(ks[15], (na, V_HEAD_DIM), 0.01),
        "conv_pw1_w": nrm(ks[16], (nc, D, 2 * D), D ** -0.5),
        "conv_pw1_b": nrm(ks[17], (nc, 2 * D), 0.01),
        "conv_dw_w": nrm(ks[18], (nc, CONV_WIDTH, D), CONV_WIDTH ** -0.5),
        "conv_dw_b": nrm(ks[19], (nc, D), 0.01),
        "conv_ln_g": 1.0 + nrm(ks[20], (nc, D), 0.01),
        "conv_ln_b": nrm(ks[21], (nc, D), 0.01),
        "conv_pw2_w": nrm(ks[22], (nc, D, D), D ** -0.5),
        "conv_pw2_b": nrm(ks[23], (nc, D), 0.01),
        "mlp_w1": nrm(ks[24], (DEPTH, D, D_FF), D ** -0.5),
        "mlp_w2": nrm(ks[25], (DEPTH, D_FF, D), D_FF ** -0.5),
    }


def reference(x, c, ctx, c_ctx, ada_w, ada_b, norm_mix_g, norm_mlp_g, norm_final_g,
              attn_w_qkv, attn_w_o, lambda_q1, lambda_k1, lambda_q2, lambda_k2, attn_subln_g,
              conv_pw1_w, conv_pw1_b, conv_dw_w, conv_dw_b, conv_ln_g, conv_ln_b, conv_pw2_w, conv_pw2_b,
              mlp_w1, mlp_w2):
    x_lat, x_ctx = x, ctx
    for i in range(DEPTH):
        kind = i % N_MIXERS
        idx = i // N_MIXERS
        ctx_update = any(j % N_MIXERS == 0 for j in range(i + 1, DEPTH))
        ctx_in = (kind == 0) or ctx_update
        sh1, sc1, g1, sh2, sc2, g2 = adaln(c, ada_w[i], ada_b[i])
        h_lat = modulate(rms_norm(x_lat, norm_mix_g[i]), sh1, sc1)
        if ctx_in:
            csh1, csc1, cg1, csh2, csc2, cg2 = adaln(c_ctx, ada_w[i], ada_b[i])
            h_ctx = modulate(rms_norm(x_ctx, norm_mix_g[i]), csh1, csc1)
        if kind == 0:
            y_lat, y_ctx = diff_attention(h_lat, h_ctx, attn_w_qkv[idx], attn_w_o[idx],
                                          lambda_q1[idx], lambda_k1[idx], lambda_q2[idx], lambda_k2[idx],
                                          attn_subln_g[idx], i, ctx_update)
        else:
            conv_p = (conv_pw1_w[idx], conv_pw1_b[idx], conv_dw_w[idx], conv_dw_b[idx],
                      conv_ln_g[idx], conv_ln_b[idx], conv_pw2_w[idx], conv_pw2_b[idx])
            y_lat = conformer_conv(h_lat, *conv_p)
            y_ctx = conformer_conv(h_ctx, *conv_p) if ctx_update else None
        x_lat = x_lat + g1 * y_lat
        x_lat = x_lat + g2 * sq_relu_mlp(modulate(rms_norm(x_lat, norm_mlp_g[i]), sh2, sc2),
                                         mlp_w1[i], mlp_w2[i])
        if ctx_update:
            x_ctx = x_ctx + cg1 * y_ctx
            x_ctx = x_ctx + cg2 * sq_relu_mlp(modulate(rms_norm(x_ctx, norm_mlp_g[i]), csh2, csc2),
                                              mlp_w1[i], mlp_w2[i])
    return rms_norm(x_lat, norm_final_g)
```

```python
import math
from contextlib import ExitStack

import numpy as np
import concourse.bass as bass
import concourse.mybir as mybir
from concourse.bass_utils import run_bass_kernel_spmd

F32 = mybir.dt.float32
BF16 = mybir.dt.bfloat16
AF = mybir.ActivationFunctionType
ALU = mybir.AluOpType
AX = mybir.AxisListType

RMS_EPS = 1e-6
LN_EPS = 1e-5
CONV_W = 31
CONV_PAD = 15
ROPE_BASE = 10000.0
DK = 128
DV = 256


class Cfg:
    def __init__(self, D=4096, T=2048, C=256, DFF=16384, NB=1, GRID_W=64, debug=(), nphase=9):
        self.D, self.T, self.C, self.DFF, self.NB, self.GRID_W = D, T, C, DFF, NB, GRID_W
        self.DC = D // 128
        self.H = D // (2 * DK)
        self.TT = C + T
        self.NBX = NB + 1
        self.FC = DFF // 128
        self.debug = tuple(debug)
        self.nphase = nphase
        self.G = min(1024, DFF)
        self.MT = min(512, T)
        self.TB = min(1024, T)


class Sched:
    STREAMS = ("pe", "act", "dve", "pool", "sp")

    def __init__(self):
        self.ops = []
        self.lastw = {}
        self.readers = {}
        self.seen = set()
        self.fence = {}
        self.local = set()

    def add(self, stream, emit, R=(), W=(), dma=None, inc=16):
        i = len(self.ops)
        W = tuple(W) + tuple(k for k in R if isinstance(k, tuple) and k[0] == "ps")
        deps = set()
        for k in tuple(R) + tuple(W):
            if k not in self.seen:
                self.seen.add(k)
                deps.update(self.fence.values())
            w = self.lastw.get(k)
            if w is not None:
                deps.add(w)
        for k in W:
            deps.update(self.readers.get(k, ()))
        for k in R:
            self.readers.setdefault(k, []).append(i)
        for k in W:
            self.lastw[k] = i
            self.readers[k] = []
        deps.discard(i)
        self.ops.append(dict(stream=stream, emit=emit, deps=deps, dma=dma, inc=inc))
        return i

    def mark_local(self, key):
        self.local.add(key)

    def phase_fence(self):
        for k in list(self.local):
            ids = []
            if k in self.lastw:
                ids.append(self.lastw.pop(k))
            ids.extend(self.readers.pop(k, ()))
            self.seen.discard(k)
            for i in ids:
                o = self.ops[i]
                fk = ("dma", o["dma"]) if o["dma"] is not None else ("eng", o["stream"])
                if self.fence.get(fk, -1) < i:
                    self.fence[fk] = i
        self.local = set()

    def finalize(self):
        ops = self.ops
        for o in ops:
            o["signal"] = o["dma"] is not None
        for o in ops:
            for d in o["deps"]:
                p = ops[d]
                if p["dma"] is None:
                    if p["stream"] == "pe" and o["stream"] == "pe" and o["dma"] is None:
                        continue
                    p["signal"] = True
        cnt = {}
        for o in ops:
            key = ("dma", o["dma"]) if o["dma"] is not None else ("eng", o["stream"])
            o["semkey"] = key
            if o["signal"]:
                cnt[key] = cnt.get(key, 0) + (o["inc"] if o["dma"] is not None else 1)
                o["val"] = cnt[key]
        self.semkeys = sorted(cnt.keys(), key=str)
        self.by_stream = {s: [o for o in ops if o["stream"] == s] for s in self.STREAMS}

    def emit_stream(self, stream, eng, sems):
        waited = {}
        ops = self.ops
        for o in self.by_stream[stream]:
            need = {}
            for d in o["deps"]:
                p = ops[d]
                if not p["signal"]:
                    continue
                if p["dma"] is None and p["stream"] == "pe" and stream == "pe" and o["dma"] is None:
                    continue
                k = p["semkey"]
                if need.get(k, 0) < p["val"]:
                    need[k] = p["val"]
            for k, v in need.items():
                if waited.get(k, 0) >= v:
                    continue
                eng.wait_ge(sems[k], v)
                waited[k] = v
            ins = o["emit"](eng) if o["emit"] is not None else None
            if o["signal"]:
                assert ins is not None
                ins.then_inc(sems[o["semkey"]], o["inc"] if o["dma"] is not None else 1)


class Ring:
    def __init__(self, sched, name, aps):
        self.name, self.aps, self.i = name, aps, 0
        for j in range(len(aps)):
            sched.mark_local((name, j))

    def next(self):
        j = self.i % len(self.aps)
        self.i += 1
        return self.aps[j], (self.name, j)


class Arena:
    def __init__(self, ap, words):
        self.ap, self.words, self.off = ap, words, 0

    def reset(self):
        self.off = 0

    def alloc(self, shape, dtype):
        n = 1
        for s in shape:
            n *= s
        words = n if dtype == F32 else (n + 1) // 2
        words = (words + 15) // 16 * 16
        assert self.off + words <= self.words, ("SBUF arena overflow", self.off, words, self.words)
        v = self.ap[:, self.off:self.off + words]
        self.off += words
        if dtype == BF16:
            v = v.bitcast(BF16)
        v = v[:, 0:n]
        if len(shape) == 2:
            v = v.rearrange("p (a b) -> p a b", a=shape[0])
        elif len(shape) == 3:
            v = v.rearrange("p (a b c) -> p a b c", a=shape[0], b=shape[1])
        return v

    def ring(self, sched, name, n, shape, dtype):
        return Ring(sched, name, [self.alloc(shape, dtype) for _ in range(n)])


class Builder:
    def __init__(self, cfg):
        self.cfg = cfg
        self.nc = bass.Bass("TRN2", target_bir_lowering=False)
        self.s = Sched()
        self.dram = {}

    def din(self, name, shape, dtype=F32):
        t = self.nc.dram_tensor(name, list(shape), dtype, kind="ExternalInput").ap()
        self.dram[name] = t
        return t

    def dscratch(self, name, shape, dtype):
        kind = "ExternalOutput" if name in self.cfg.debug else "Internal"
        t = self.nc.dram_tensor(name, list(shape), dtype, kind=kind).ap()
        self.dram[name] = t
        return t

    def psb(self, b, n=512, off=0):
        return self.ps[:, b * 512 + off:b * 512 + off + n]

    def psb16(self, b):
        return self.ps[:, b * 512:(b + 1) * 512].bitcast(BF16)

    def rstd_ops(self, ss, out, n, eps, R, W):
        self.s.add("dve", lambda e: e.tensor_scalar(out, ss, 1.0 / n, eps, ALU.mult, ALU.add), R=R, W=W)
        self.s.add("act", lambda e: e.activation(out, out, AF.Sqrt), R=W, W=W)
        self.s.add("dve", lambda e: e.reciprocal(out, out), R=W, W=W)

    def build(self):
        cfg, nc, s = self.cfg, self.nc, self.s
        D, T, C, DFF, NB, DC, H, TT, NBX = cfg.D, cfg.T, cfg.C, cfg.DFF, cfg.NB, cfg.DC, cfg.H, cfg.TT, cfg.NBX
        x = self.din("x", [NB, T, D])
        ctx = self.din("ctx", [NB, C, D])
        cvec = self.din("cvec", [NBX, D])
        ada_w = self.din("ada_w", [2, D, 6 * D])
        ada_b = self.din("ada_b", [2, 6 * D])
        norm_mix_g = self.din("norm_mix_g", [2, D])
        norm_mlp_g = self.din("norm_mlp_g", [2, D])
        norm_final_g = self.din("norm_final_g", [1, D])
        w_qkv = self.din("attn_w_qkv", [D, 3 * D])
        w_o = self.din("attn_w_o", [D, D])
        lam_in = self.din("lam_in", [4, DK])
        subln_g = self.din("attn_subln_g", [1, DV])
        pw1_w = self.din("conv_pw1_w", [D, 2 * D])
        pw1_b = self.din("conv_pw1_b", [2, D])
        dw_w = self.din("conv_dw_w", [CONV_W, D])
        dw_b = self.din("conv_dw_b", [1, D])
        ln_g = self.din("conv_ln_g", [1, D])
        ln_b = self.din("conv_ln_b", [1, D])
        pw2_w = self.din("conv_pw2_w", [D, D])
        pw2_b = self.din("conv_pw2_b", [1, D])
        mlp_w1 = self.din("mlp_w1", [2, D, DFF])
        mlp_w2 = self.din("mlp_w2", [2, DFF, D])
        rope_cos = self.din("rope_cos", [128, TT])
        rope_sin = self.din("rope_sin", [128, TT])
        rot_m = self.din("rot_m", [128, 128])
        ident_in = self.din("ident", [128, 128])
        out = self.nc.dram_tensor("out", [NB, T, D], F32, kind="ExternalOutput").ap()
        qT = self.dscratch("qT", [D, T], BF16)
        kT = self.dscratch("kT", [D, TT], BF16)
        vS = self.dscratch("vS", [TT, D], BF16)
        oTs = self.dscratch("oTs", [T // 128, 128, DC, 128], BF16)
        x1 = self.dscratch("x1", [T, D], F32)
        x2 = self.dscratch("x2", [T, D], F32)
        x3 = self.dscratch("x3", [T, D], F32)
        x4 = self.dscratch("x4", [T, D], F32)
        uT = self.dscratch("uT", [D, T], F32)
        cT = self.dscratch("cT", [D, T], BF16)
        vTs = self.dscratch("vTs", [T // 128, 128, DC, 128], BF16)
        w1s = self.dscratch("w1s", [DFF // 128, 128, DC * 128], BF16)
        w2s = self.dscratch("w2s", [(DFF // cfg.G) * (D // min(512, D)), 128, (cfg.G // 128) * min(512, D)], BF16)
        self.__dict__.update(locals())

        ARENA_WORDS = 48 * 1024
        with ExitStack() as es:
            arena_t = es.enter_context(nc.sbuf_tensor("arena", [128, ARENA_WORDS], F32))
            self.ar = Arena(arena_t[:, :], ARENA_WORDS)
            NCOLV = CONV_W + 3 + 4 + 2 + 1
            pers_words = 128 * 3 + 128 + 2 * (6 * DC * NBX) + 8 * NBX * DC + NCOLV * DC + 64 + DV + 64
            pers_t = es.enter_context(nc.sbuf_tensor("pers", [128, pers_words], F32))
            self.pers = Arena(pers_t[:, :], pers_words)
            self.ps = es.enter_context(nc.psum_tensor("ps", [128, 4096], F32))[:, :]

            NP = cfg.nphase
            self.setup_phase()
            for b in range(NB):
                if NP >= 2: self.layer0_qkv(b)
                if NP >= 3: self.attention(b)
                if NP >= 4: self.proj_tm(b, oTs, "oTs", w_o, None, x[b], ("xin", b), x1, "x1", self.gate_cols(0, 2, b), "wo")
                if NP >= 5: self.mlp(b, 0, x1, "x1", x2, "x2")
                if NP >= 6: self.conv_block(b)
                if NP >= 7: self.proj_tm(b, vTs, "vTs", pw2_w, pw2_b, x2, "x2", x3, "x3", self.gate_cols(1, 2, b), "pw2")
                if NP >= 8: self.mlp(b, 1, x3, "x3", x4, "x4")
                if NP >= 9: self.final_norm(b)
            lastdma = {}
            for i, o in enumerate(s.ops):
                if o["dma"] is not None:
                    lastdma[o["dma"]] = i
            fin = s.add("sp", None)
            s.ops[fin]["deps"].update(lastdma.values())
            s.finalize()
            sems = {}
            for k in s.semkeys:
                sems[k] = es.enter_context(nc.semaphore("s_" + "_".join(str(z) for z in (k if isinstance(k, tuple) else (k,))).replace(" ", "")[:40]))
            self.nsems = len(sems)
            with nc.Block() as block:
                @block.tensor
                def _(e):
                    s.emit_stream("pe", e, sems)

                @block.scalar
                def _(e):
                    s.emit_stream("act", e, sems)

                @block.vector
                def _(e):
                    s.emit_stream("dve", e, sems)

                @block.gpsimd
                def _(e):
                    s.emit_stream("pool", e, sems)

                @block.sync
                def _(e):
                    s.emit_stream("sp", e, sems)
        return nc

    def mod_col(self, l, v, j, k):
        return self.mods[l][:, v * self.cfg.DC + k, j:j + 1]

    def gate_cols(self, l, v, b):
        return (l, v, b)

    def colvec(self, idx, k):
        return self.cols[:, idx, k:k + 1]

    def setup_phase(self):
        cfg, s, P, ar = self.cfg, self.s, self.pers, self.ar
        D, DC, NBX, NB = cfg.D, cfg.DC, cfg.NBX, cfg.NB
        self.ident16 = P.alloc([128], BF16)
        self.ident32 = P.alloc([128], F32)
        self.rot16 = P.alloc([128], BF16)
        self.ones16 = P.alloc([128], BF16)
        self.ones32 = P.alloc([128], F32)
        self.mods = [P.alloc([6 * DC, NBX], F32) for _ in range(2)]
        self.modA = P.alloc([4 * NBX, DC], F32)
        NCOLV = CONV_W + 3 + 4 + 2 + 1
        self.cols = P.alloc([NCOLV, DC], F32)
        self.lamc = P.alloc([8], F32)
        self.gsub = P.alloc([DV], F32)
        K = lambda n: ("pers", n)
        s.add("sp", lambda e: e.dma_start(out=self.ident32, in_=self.ident_in), W=[K("id32")], dma="c_id32")
        s.add("pool", lambda e: e.dma_start(out=self.ident16, in_=self.ident_in), W=[K("id16")], dma="c_id16")
        s.add("pool", lambda e: e.dma_start(out=self.rot16, in_=self.rot_m), W=[K("rot16")], dma="c_rot")
        s.add("dve", lambda e: e.memset(self.ones16, 1.0), W=[K("ones16")])
        s.add("dve", lambda e: e.memset(self.ones32, 1.0), W=[K("ones32")])
        s.add("sp", lambda e: e.dma_start(out=self.gsub, in_=self.subln_g.partition_broadcast(128)), W=[K("gsub")], dma="c_gsub")
        lam_init0 = 0.8 - 0.6 * math.exp(-0.3 * 0)
        s.add("dve", lambda e: e.tensor_scalar(self.gsub, self.gsub, 1.0 - lam_init0, None, ALU.mult), R=[K("gsub")], W=[K("gsub")])

        VPT = max(1, 128 // DC)
        stage = ar.ring(s, "cl_stage", 2, [128], F32)
        vec_srcs = []
        def rowsrc(ap_row):
            return ap_row.rearrange("o (k p) -> (o k) p", p=128)
        colnames = {}
        ci = 0
        for j in range(CONV_W):
            vec_srcs.append((self.cols[:, ci, :], rowsrc(self.dw_w[j:j + 1, :]))); colnames[("dw", j)] = ci; ci += 1
        for nm, src in (("dwb", self.dw_b), ("lng", self.ln_g), ("lnb", self.ln_b)):
            vec_srcs.append((self.cols[:, ci, :], rowsrc(src))); colnames[nm] = ci; ci += 1
        for l in range(2):
            vec_srcs.append((self.cols[:, ci, :], rowsrc(self.norm_mix_g[l:l + 1, :]))); colnames[("gmix", l)] = ci; ci += 1
            vec_srcs.append((self.cols[:, ci, :], rowsrc(self.norm_mlp_g[l:l + 1, :]))); colnames[("gmlp", l)] = ci; ci += 1
        for h in range(2):
            vec_srcs.append((self.cols[:, ci, :], rowsrc(self.pw1_b[h:h + 1, :]))); colnames[("pw1b", h)] = ci; ci += 1
        self.colnames = colnames
        cs32 = ar.alloc([NBX, DC], F32)
        s.mark_local("cs32")
        for j in range(NBX):
            vec_srcs.append((cs32[:, j, :], rowsrc(self.cvec[j:j + 1, :])))
        for g0 in range(0, len(vec_srcs), VPT):
            grp = vec_srcs[g0:g0 + VPT]
            st, stk = stage.next()
            s.add("dve", lambda e, st=st: e.memset(st, 0.0), W=[stk])
            for i, (dst, src) in enumerate(grp):
                s.add("sp", lambda e, st=st, i=i, src=src: e.dma_start(out=st[i * DC:(i + 1) * DC, :], in_=src),
                      W=[stk], dma=f"cl{stk[1]}")
            s.mark_local(("ps", 0))
            pst = self.psb(0, 128)
            s.add("pe", lambda e, st=st, pst=pst: e.transpose(pst, st, self.ident32), R=[stk, K("id32")], W=[("ps", 0)])
            for i, (dst, src) in enumerate(grp):
                s.add("dve", lambda e, dst=dst, i=i, pst=pst: e.tensor_copy(dst, pst[:, i * DC:(i + 1) * DC]),
                      R=[("ps", 0)], W=[K("cols"), "cs32"])
        lamt = ar.alloc([8], F32)
        s.mark_local("lamt")
        s.add("sp", lambda e: e.dma_start(out=lamt[:, 0:4], in_=self.lam_in.rearrange("a d -> d a"), allow_slow_non_contiguous=True),
              W=["lamt"], dma="c_lam")
        def lam_a(e):
            e.tensor_tensor(lamt[:, 4:5], lamt[:, 0:1], lamt[:, 1:2], ALU.mult)
            return e.tensor_tensor(lamt[:, 5:6], lamt[:, 2:3], lamt[:, 3:4], ALU.mult)
        s.add("dve", lam_a, R=["lamt"], W=["lamt"])
        s.mark_local(("ps", 1))
        s.add("pe", lambda e: e.matmul(self.psb(1, 2), self.ones32, lamt[:, 4:6], start=True, stop=True),
              R=["lamt", K("ones32")], W=[("ps", 1)])
        s.add("act", lambda e: e.activation(lamt[:, 6:8], self.psb(1, 2), AF.Exp), R=[("ps", 1)], W=["lamt"])
        s.add("dve", lambda e: e.scalar_tensor_tensor(self.lamc[:, 0:1], lamt[:, 6:7], lam_init0, lamt[:, 7:8], ALU.add, ALU.subtract),
              R=["lamt"], W=[K("lam")])
        s.add("dve", lambda e: e.tensor_scalar(self.lamc[:, 1:2], self.lamc[:, 0:1], -1.0, None, ALU.mult), R=[K("lam")], W=[K("lam")])

        cs16 = ar.alloc([DC, NBX], BF16)
        s.mark_local("cs16")
        s.add("act", lambda e: e.activation(cs16.rearrange("p k j -> p j k"), cs32, AF.Silu), R=["cs32"], W=["cs16"])
        wr = ar.ring(s, "adaw", 2, [DC, 512], BF16)
        br = ar.ring(s, "adab", 2, [512], F32)
        pbank = [2, 3]
        for bk in pbank:
            s.mark_local(("ps", bk))
        it = 0
        for l in range(2):
            for cg in range(6 * D // 512):
                w, wk = wr.next()
                bt, bk_ = br.next()
                s.add("pool", lambda e, w=w, l=l, cg=cg: e.dma_start(
                    out=w, in_=self.ada_w[l, :, cg * 512:(cg + 1) * 512].rearrange("(k p) c -> p k c", p=128)),
                    W=[wk], dma=f"adaw{wk[1]}")
                s.add("sp", lambda e, bt=bt, l=l, cg=cg: e.dma_start(out=bt[0:1, :], in_=self.ada_b[l:l + 1, cg * 512:(cg + 1) * 512]),
                      W=[bk_], dma=f"adab{bk_[1]}")
                pb = pbank[it % 2]
                it += 1
                def mm(e, w=w, bt=bt, pb=pb):
                    for sub in range(4):
                        o = self.psb(pb, NBX, sub * NBX)
                        for k in range(DC):
                            e.matmul(o, w[:, k, sub * 128:(sub + 1) * 128], cs16[:, k, :], start=(k == 0), stop=False)
                        ins = e.matmul(o, bt[0:1, sub * 128:(sub + 1) * 128], self.ones32[0:1, 0:NBX], start=False, stop=True)
                    return ins
                s.add("pe", mm, R=[wk, bk_, "cs16", K("ones32")], W=[("ps", pb)])
                s.add("dve", lambda e, l=l, cg=cg, pb=pb: e.tensor_copy(
                    self.mods[l][:, cg * 4:(cg + 1) * 4, :], self.psb(pb, 4 * NBX).rearrange("p (a j) -> p a j", a=4)),
                    R=[("ps", pb)], W=[K("mods")])
        for l in range(2):
            for si, (gname, v) in enumerate((("gmix", 1), ("gmlp", 4))):
                for j in range(NBX):
                    dst = self.modA[:, (l * 2 + si) * NBX + j, :]
                    g = self.cols[:, colnames[(gname, l)], :]
                    sc = self.mods[l][:, v * DC:(v + 1) * DC, j]
                    s.add("dve", lambda e, dst=dst, g=g, sc=sc: e.scalar_tensor_tensor(dst, sc, 1.0, g, ALU.add, ALU.mult),
                          R=[K("mods"), K("cols")], W=[K("modA")])
        s.phase_fence()
        ar.reset()

    def A_col(self, l, si, j, k):
        return self.modA[:, (l * 2 + si) * self.cfg.NBX + j, k:k + 1]

    def make_prologue(self, psbanks, nbuf=2):
        cfg, s, ar = self.cfg, self.s, self.ar
        D, DC = cfg.D, cfg.DC
        xt = ar.ring(s, "pr_x", nbuf, [D], F32)
        xn = ar.ring(s, "pr_xn", nbuf, [D], BF16)
        st = ar.ring(s, "pr_st", 4, [2], F32)
        self.pr_xring = xt
        for bk in psbanks:
            s.mark_local(("ps", bk))
        state = dict(i=0)
        K = lambda n: ("pers", n)
        GS = min(8, DC)

        def run(src_ap, src_key, dst_fn, dst_key_fn, A_fn, sh_fn):
            x_, xk = xt.next()
            n_, nk = xn.next()
            t_, tk = st.next()
            s.add("sp", lambda e: e.dma_start(out=x_, in_=src_ap), R=[src_key], W=[xk], dma=f"prx{xk[1]}")
            s.add("act", lambda e: e.activation(n_, x_, AF.Square, accum_out=t_[:, 0:1]), R=[xk], W=[tk, nk])
            self.rstd_ops(t_[:, 0:1], t_[:, 1:2], D, RMS_EPS, R=[tk], W=[tk])
            s.add("dve", lambda e: e.tensor_scalar(n_, x_, t_[:, 1:2], None, ALU.mult), R=[xk, tk], W=[nk])
            for g in range(DC // GS):
                bk = psbanks[state["i"] % len(psbanks)]
                state["i"] += 1
                pv = self.psb16(bk)
                def tr(e, g=g, pv=pv):
                    for q in range(GS):
                        k = g * GS + q
                        ins = e.transpose(pv[:, q * 128:(q + 1) * 128], n_[:, k * 128:(k + 1) * 128], self.ident16)
                    return ins
                s.add("pe", tr, R=[nk, K("id16")], W=[("ps", bk)])
                def ev(e, g=g, pv=pv):
                    for q in range(GS):
                        k = g * GS + q
                        ins = e.activation(dst_fn(k), pv[:, q * 128:(q + 1) * 128], AF.Identity, bias=sh_fn(k), scale=A_fn(k))
                    return ins
                s.add("act", ev, R=[("ps", bk), K("mods"), K("modA")], W=[dst_key_fn(g)])
        return run, DC // GS

    def layer0_qkv(self, b):
        cfg, s, ar = self.cfg, self.s, self.ar
        D, T, C, DC, H, TT = cfg.D, cfg.T, cfg.C, cfg.DC, cfg.H, cfg.TT
        K = lambda n: ("pers", n)
        TB = cfg.TB
        hT = ar.alloc([DC, TB], BF16)
        pro, NG = self.make_prologue([0, 1])
        wr = ar.ring(s, "qkvw", 3, [DC, 256], BF16)
        cosb = ar.alloc([TB], F32)
        sinb = ar.alloc([TB], F32)
        s.mark_local("cosb"); s.mark_local("sinb")
        qs_r = ar.ring(s, "qs", 2, [512], BF16)
        t1_r = ar.ring(s, "t1", 2, [512], F32)
        t2_r = ar.ring(s, "t2", 2, [512], F32)
        so_r = ar.ring(s, "so", 3, [512], BF16)
        vo_r = ar.ring(s, "vo", 3, [256], BF16)
        mb = [2, 3, 4, 5]
        rb = [6, 7]
        for bk in mb + rb:
            s.mark_local(("ps", bk))
        mi = [0, 0]
        blocks = [(True, 0, C)] + [(False, t0, min(TB, T - t0)) for t0 in range(0, T, TB)]
        for (is_ctx, t0, nt) in blocks:
            pos0 = t0 if is_ctx else C + t0
            ntile = nt // 128
            for g in range(NG):
                for i in range(ntile):
                    s.mark_local(("hT", i, g))
            for i in range(ntile):
                if is_ctx:
                    src, skey, j = self.ctx[b, t0 + i * 128:t0 + (i + 1) * 128, :], ("ctxin", b), cfg.NB
                else:
                    src, skey, j = self.x[b, t0 + i * 128:t0 + (i + 1) * 128, :], ("xin", b), b
                pro(src, skey,
                    lambda k, i=i: hT[:, k, i * 128:(i + 1) * 128],
                    lambda g, i=i: ("hT", i, g),
                    lambda k, j=j: self.A_col(0, 0, j, k),
                    lambda k, j=j: self.mod_col(0, 0, j, k))
            import os as _os
            if _os.environ.get("KSTOP") == "pro":
                continue
            s.add("sp", lambda e, pos0=pos0, nt=nt: e.dma_start(out=cosb[:, 0:nt], in_=self.rope_cos[:, pos0:pos0 + nt]), W=["cosb"], dma="cosb")
            s.add("sp", lambda e, pos0=pos0, nt=nt: e.dma_start(out=sinb[:, 0:nt], in_=self.rope_sin[:, pos0:pos0 + nt]), W=["sinb"], dma="sinb")
            hkeys = lambda tl: [("hT", i, g) for i in tl for g in range(NG)]
            cgs = range(D // 256, 3 * D // 256) if is_ctx else range(3 * D // 256)
            for cg in cgs:
                w, wk = wr.next()
                s.add("pool", lambda e, w=w, cg=cg: e.dma_start(
                    out=w, in_=self.w_qkv[:, cg * 256:(cg + 1) * 256].rearrange("(k p) c -> p k c", p=128)),
                    W=[wk], dma=f"qkvw{wk[1]}")
                which = cg // (D // 256)
                if which < 2:
                    dstT = self.qT if which == 0 else self.kT
                    dname = "qT" if which == 0 else "kT"
                    p0 = t0 if which == 0 else pos0
                    for mc in range(2):
                        fr = (cg % (D // 256)) * 256 + mc * 128
                        for ts in range(0, nt, 512):
                            n = min(512, nt - ts)
                            pb = mb[mi[0] % len(mb)]; mi[0] += 1
                            pr = rb[mi[1] % len(rb)]; mi[1] += 1
                            tl = range(ts // 128, (ts + n) // 128)
                            def mm(e, w=w, mc=mc, ts=ts, n=n, pb=pb):
                                for k in range(DC):
                                    ins = e.matmul(self.psb(pb, n), w[:, k, mc * 128:(mc + 1) * 128], hT[:, k, ts:ts + n],
                                                   start=(k == 0), stop=(k == DC - 1))
                                return ins
                            s.add("pe", mm, R=[wk] + hkeys(tl), W=[("ps", pb)])
                            if _os.environ.get("KSTOP") == "mm":
                                continue
                            q_, qk = qs_r.next()
                            a_, ak = t1_r.next()
                            b_, bk2 = t2_r.next()
                            o_, ok = so_r.next()
                            s.add("act", lambda e, q_=q_, pb=pb, n=n: e.activation(q_[:, 0:n], self.psb(pb, n), AF.Identity), R=[("ps", pb)], W=[qk])
                            if _os.environ.get("KSTOP") == "rope_a":
                                continue
                            s.add("pe", lambda e, q_=q_, pr=pr, n=n: e.matmul(self.psb(pr, n), self.rot16, q_[:, 0:n], start=True, stop=True),
                                  R=[qk, K("rot16")], W=[("ps", pr)])
                            if _os.environ.get("KSTOP") == "rope_b":
                                continue
                            if _os.environ.get("KVAR") == "v1":
                                s.add("dve", lambda e, a_=a_, pb=pb, n=n, ts=ts: e.tensor_tensor(a_[:, 0:n], self.psb(pb, n), t2_r.aps[0][:, 0:n], ALU.mult),
                                      R=[("ps", pb)], W=[ak])
                            elif _os.environ.get("KVAR") == "v3":
                                s.add("act", lambda e, a_=a_, pb=pb, n=n, ts=ts: e.activation(a_[:, 0:n], self.psb(pb, n), AF.Identity),
                                      R=[("ps", pb), "cosb"], W=[ak])
                            elif _os.environ.get("KVAR") == "v5":
                                s.add("dve", lambda e, a_=a_, q_=q_, n=n, ts=ts: e.tensor_tensor(a_[:, 0:n], q_[:, 0:n], cosb[:, ts:ts + n], ALU.mult),
                                      R=[qk, "cosb"], W=[ak])
                            elif _os.environ.get("KVAR") == "v6":
                                s.add("dve", lambda e, a_=a_, pb=pb, n=n, ts=ts: e.tensor_tensor(a_[:, 0:n], self.psb(pb, n), cosb[:, ts:ts + n], ALU.mult),
                                      R=[("ps", pb), "cosb", qk], W=[ak])
                            elif _os.environ.get("KVAR") == "v2":
                                s.add("dve", lambda e, a_=a_, pb=pb, n=n, ts=ts: e.tensor_copy(a_[:, 0:n], self.psb(pb, n)),
                                      R=[("ps", pb), "cosb"], W=[ak])
                            else:
                                s.add("dve", lambda e, a_=a_, pb=pb, n=n, ts=ts: e.tensor_tensor(a_[:, 0:n], self.psb(pb, n), cosb[:, ts:ts + n], ALU.mult),
                                      R=[("ps", pb), "cosb"], W=[ak])
                            if _os.environ.get("KSTOP") == "rope_c":
                                continue
                            s.add("dve", lambda e, b_=b_, pr=pr, n=n, ts=ts: e.tensor_tensor(b_[:, 0:n], self.psb(pr, n), sinb[:, ts:ts + n], ALU.mult),
                                  R=[("ps", pr), "sinb"], W=[bk2])
                            if _os.environ.get("KSTOP") == "rope1":
                                continue
                            s.add("dve", lambda e, o_=o_, a_=a_, b_=b_, n=n: e.tensor_tensor(o_[:, 0:n], a_[:, 0:n], b_[:, 0:n], ALU.add),
                                  R=[ak, bk2], W=[ok])
                            s.add("sp", lambda e, o_=o_, dstT=dstT, fr=fr, p0=p0, ts=ts, n=n: e.dma_start(
                                out=dstT[fr:fr + 128, p0 + ts:p0 + ts + n], in_=o_[:, 0:n]),
                                R=[ok], W=[(dname, fr // 128)], dma=f"so{ok[1]}")
                else:
                    if _os.environ.get("KSTOP") in ("mm", "rope1", "nov", "rope_a", "rope_b", "rope_c"):
                        continue
                    c0 = (cg - 2 * (D // 256)) * 256
                    for i in range(ntile):
                        pb = mb[mi[0] % len(mb)]; mi[0] += 1
                        def mm(e, w=w, i=i, pb=pb):
                            for k in range(DC):
                                ins = e.matmul(self.psb(pb, 256), hT[:, k, i * 128:(i + 1) * 128], w[:, k, :],
                                               start=(k == 0), stop=(k == DC - 1))
                            return ins
                        s.add("pe", mm, R=[wk] + hkeys([i]), W=[("ps", pb)])
                        o_, ok = vo_r.next()
                        s.add("act", lambda e, o_=o_, pb=pb: e.activation(o_, self.psb(pb, 256), AF.Identity), R=[("ps", pb)], W=[ok])
                        s.add("sp", lambda e, o_=o_, i=i, c0=c0, pos0=pos0: e.dma_start(
                            out=self.vS[pos0 + i * 128:pos0 + (i + 1) * 128, c0:c0 + 256], in_=o_),
                            R=[ok], W=[("vS", c0 // 256)], dma=f"vo{ok[1]}")
        s.phase_fence()
        ar.reset()

    def attention(self, b):
        cfg, s, ar = self.cfg, self.s, self.ar
        D, T, C, DC, H, TT = cfg.D, cfg.T, cfg.C, cfg.DC, cfg.H, cfg.TT
        K = lambda n: ("pers", n)
        KT = TT // 128
        NQ = T // 128
        scale = DK ** -0.5
        PMAX = min(TT, 1024)
        kt_r = ar.ring(s, "at_k", 2, [2, TT], BF16)
        q_r = ar.ring(s, "at_q", 2, [2, T], BF16)
        v_r = ar.ring(s, "at_v", 2, [KT, DV], BF16)
        e_r = [ar.ring(s, f"at_e{c}", 2, [TT], BF16) for c in range(2)]
        tmp_r = ar.ring(s, "at_tmp", 2, [TT], BF16)
        a_r = ar.ring(s, "at_a", 2, [TT], BF16)
        aT_r = ar.ring(s, "at_aT", 2, [KT, 128], BF16)
        st_r = ar.ring(s, "at_st", 6, [16], F32)
        on_r = ar.ring(s, "at_on", 2, [DV], BF16)
        junk = ar.alloc([DV], F32)
        ot_r = ar.ring(s, "at_ot", 3, [2, 128], BF16)
        NSB = (TT + 511) // 512
        assert NSB <= 5
        sb = list(range(NSB))
        skeys = [("ps", bk) for bk in sb]
        tb = [5, 6]
        ob = 7
        for bk in range(8):
            s.mark_local(("ps", bk))
        s.mark_local(("ps", 7, 0)); s.mark_local(("ps", 7, 1))
        S = self.ps[:, 0:TT]
        ti = [0]
        oi = [0]
        GT = 6
        heads = {}

        def load_head(h):
            kt, kk = kt_r.next()
            qt, qk = q_r.next()
            vt, vk = v_r.next()
            for c in range(2):
                fr = h * 256 + c * 128
                s.add("sp", lambda e, kt=kt, c=c, fr=fr: e.dma_start(out=kt[:, c, :], in_=self.kT[fr:fr + 128, :]),
                      R=[("kT", fr // 128)], W=[kk], dma=f"atk{kk[1]}_{c}")
                s.add("sp", lambda e, qt=qt, c=c, fr=fr: e.dma_start(out=qt[:, c, :], in_=self.qT[fr:fr + 128, :]),
                      R=[("qT", fr // 128)], W=[qk], dma=f"atq{qk[1]}_{c}")
            s.add("sp", lambda e, vt=vt, h=h: e.dma_start(out=vt, in_=self.vS[:, h * DV:(h + 1) * DV].rearrange("(kt p) c -> p kt c", p=128)),
                  R=[("vS", h)], W=[vk], dma=f"atv{vk[1]}")
            heads[h] = (kt, kk, qt, qk, vt, vk)

        items = [dict(h=h, i=i) for h in range(H) for i in range(NQ)]

        def stage_S(it, c):
            h, i = it["h"], it["i"]
            if h not in heads:
                load_head(h)
            kt, kk, qt, qk, vt, vk = heads[h]
            if c == 0:
                it["st"], it["sk"] = st_r.next()
                it["es"] = []
            st, sk = it["st"], it["sk"]
            def mmS(e):
                for j0 in range(0, TT, 512):
                    n = min(512, TT - j0)
                    ins = e.matmul(S[:, j0:j0 + n], qt[:, c, i * 128:(i + 1) * 128], kt[:, c, j0:j0 + n], start=True, stop=True)
                return ins
            s.add("pe", mmS, R=[qk, kk], W=skeys)
            s.add("dve", lambda e: e.tensor_reduce(st[:, c:c + 1], S[:, 0:PMAX], AX.X, ALU.max), R=skeys, W=[sk])
            s.add("dve", lambda e: e.tensor_scalar(st[:, 2 + c:3 + c], st[:, c:c + 1], -scale, None, ALU.mult), R=[sk], W=[sk])
            ev, ek = e_r[c].next()
            it["es"].append((ev, ek))
            s.add("act", lambda e: e.activation(ev, S, AF.Exp, bias=st[:, 2 + c:3 + c], scale=scale, accum_out=st[:, 4 + c:5 + c]),
                  R=skeys + [sk], W=[ek, sk])

        def stage_comb(it):
            st, sk, es = it["st"], it["sk"], it["es"]
            s.add("dve", lambda e: e.reciprocal(st[:, 6:8], st[:, 4:6]), R=[sk], W=[sk])
            s.add("dve", lambda e: e.tensor_tensor(st[:, 8:9], st[:, 7:8], self.lamc[:, 1:2], ALU.mult), R=[sk, K("lam")], W=[sk])
            tm, tk = tmp_r.next()
            s.add("act", lambda e: e.activation(tm, es[1][0], AF.Identity, scale=st[:, 8:9]), R=[es[1][1], sk], W=[tk])
            av, ak = a_r.next()
            s.add("dve", lambda e: e.scalar_tensor_tensor(av, es[0][0], st[:, 6:7], tm, ALU.mult, ALU.add), R=[es[0][1], tk, sk], W=[ak])
            it["a"] = (av, ak)

        def stage_T(it):
            av, ak = it["a"]
            aT, aTk = aT_r.next()
            it["aT"] = (aT, aTk)
            for g0 in range(0, KT, GT):
                ng = min(GT, KT - g0)
                bk = tb[ti[0] % 2]; ti[0] += 1
                pv = self.psb16(bk)
                def tr(e, g0=g0, ng=ng, pv=pv):
                    for q in range(ng):
                        ins = e.transpose(pv[:, q * 128:(q + 1) * 128], av[:, (g0 + q) * 128:(g0 + q + 1) * 128], self.ident16)
                    return ins
                s.add("pe", tr, R=[ak, K("id16")], W=[("ps", bk)])
                s.add("act", lambda e, g0=g0, ng=ng, pv=pv: e.activation(
                    aT[:, g0:g0 + ng, :], pv[:, 0:ng * 128].rearrange("p (a t) -> p a t", a=ng), AF.Identity),
                    R=[("ps", bk)], W=[aTk])

        def stage_AV(it):
            h, i = it["h"], it["i"]
            kt, kk, qt, qk, vt, vk = heads[h]
            st, sk = it["st"], it["sk"]
            aT, aTk = it["aT"]
            oh = oi[0] % 2; oi[0] += 1
            O = self.psb(ob, DV, oh * DV)
            def mmO(e):
                for j in range(KT):
                    ins = e.matmul(O, aT[:, j, :], vt[:, j, :], start=(j == 0), stop=(j == KT - 1))
                return ins
            s.add("pe", mmO, R=[aTk, vk], W=[("ps", 7, oh)])
            it["O"] = (O, oh)

        def stage_F(it):
            h, i = it["h"], it["i"]
            st, sk = it["st"], it["sk"]
            O, oh = it["O"]
            s.add("act", lambda e: e.activation(junk, O, AF.Square, accum_out=st[:, 9:10]), R=[("ps", 7, oh)], W=[sk, ("ps", 7, oh)])
            self.rstd_ops(st[:, 9:10], st[:, 10:11], DV, RMS_EPS, R=[sk], W=[sk])
            on, onk = on_r.next()
            s.add("dve", lambda e: e.scalar_tensor_tensor(on, O, st[:, 10:11], self.gsub, ALU.mult, ALU.mult),
                  R=[("ps", 7, oh), sk, K("gsub")], W=[onk, ("ps", 7, oh)])
            bk = tb[ti[0] % 2]; ti[0] += 1
            pv = self.psb16(bk)
            def tr2(e):
                for q in range(2):
                    ins = e.transpose(pv[:, q * 128:(q + 1) * 128], on[:, q * 128:(q + 1) * 128], self.ident16)
                return ins
            s.add("pe", tr2, R=[onk, K("id16")], W=[("ps", bk)])
            ot, otk = ot_r.next()
            s.add("act", lambda e: e.activation(ot, pv[:, 0:256].rearrange("p (a t) -> p a t", a=2), AF.Identity),
                  R=[("ps", bk)], W=[otk])
            s.add("sp", lambda e: e.dma_start(out=self.oTs[i, :, 2 * h:2 * h + 2, :], in_=ot),
                  R=[otk], W=[("oTs", i)], dma=f"atot{otk[1]}")

        N = len(items)
        stage_S(items[0], 0); stage_S(items[0], 1); stage_comb(items[0])
        self.preconvert_w1(0)
        for n in range(N):
            if n + 1 < N:
                stage_S(items[n + 1], 0)
            stage_T(items[n])
            if n + 1 < N:
                stage_S(items[n + 1], 1)
                stage_comb(items[n + 1])
            stage_AV(items[n])
            stage_F(items[n])
        s.phase_fence()
        ar.reset()

    def bcast_row(self, dst, col_fn, key_w, nchunks, psbanks):
        s = self.s
        K = lambda n: ("pers", n)
        gt, gk = self._bc_g, "bc_g"
        for k0 in range(0, nchunks, 4):
            nk = min(4, nchunks - k0)
            bk = psbanks[(k0 // 4) % len(psbanks)]
            def mk(e, k0=k0, nk=nk):
                for q in range(nk):
                    ins = e.activation(gt[:, q, :], self._bc_z, AF.Identity, bias=col_fn(k0 + q), scale=1.0)
                return ins
            s.add("act", mk, R=[K("mods"), "bc_z"], W=[gk])
            def mm(e, nk=nk, bk=bk):
                for q in range(nk):
                    ins = e.matmul(self.psb(bk, 128, q * 128), gt[:, q, :], self.ident32, start=True, stop=True)
                return ins
            s.add("pe", mm, R=[gk, K("id32")], W=[("ps", bk)])
            s.add("dve", lambda e, k0=k0, nk=nk, bk=bk: e.tensor_copy(dst[:, k0 * 128:(k0 + nk) * 128], self.psb(bk, nk * 128)),
                  R=[("ps", bk)], W=[key_w])

    def preconvert_w1(self, l):
        cfg, s = self.cfg, self.s
        DC = cfg.DC
        for fi in range(cfg.DFF // 128):
            s.add("pool", lambda e, fi=fi: e.dma_start(
                out=self.w1s[fi].rearrange("p (k c) -> p k c", k=DC),
                in_=self.mlp_w1[l, :, fi * 128:(fi + 1) * 128].rearrange("(k p) c -> p k c", p=128)),
                W=[("w1s", fi), "w1s_chain"], dma="w1pc")

    def alloc_bcast(self):
        s, ar = self.s, self.ar
        self._bc_g = ar.alloc([4, 128], F32)
        self._bc_z = ar.alloc([128], F32)
        s.mark_local("bc_g"); s.mark_local("bc_z")
        s.add("dve", lambda e: e.memset(self._bc_z, 0.0), W=["bc_z"])

    def proj_tm(self, b, aTs, aname, w, bias, xin, xin_name, xout, xout_name, gate, tag):
        cfg, s, ar = self.cfg, self.s, self.ar
        D, T, DC = cfg.D, cfg.T, cfg.DC
        K = lambda n: ("pers", n)
        NT = T // 128
        l, v, bb = gate
        self.alloc_bcast()
        gbc = ar.alloc([D], F32)
        s.mark_local("gbc")
        for bk in range(8):
            s.mark_local(("ps", bk))
        self.bcast_row(gbc, lambda k: self.mod_col(l, v, bb, k), "gbc", DC, [6, 7])
        if bias is not None:
            bbc = ar.alloc([D], F32)
            s.mark_local("bbc")
            s.add("sp", lambda e: e.dma_start(out=bbc, in_=bias.partition_broadcast(128)), W=["bbc"], dma="bbc")
        CW = min(512, D)
        wr = ar.ring(s, "tmw", 2, [DC, CW], BF16)
        a_r = ar.ring(s, "tma", 4, [DC, 128], BF16)
        x_r = ar.ring(s, "tmx", 4, [CW], F32)
        o_r = ar.ring(s, "tmo", 3, [CW], F32)
        mb = [0, 1, 2, 3]
        mi = 0
        xk_of = (lambda i: xin_name) if isinstance(xin_name, tuple) else (lambda i: (xin_name, i))
        tiles = [(n0, i) for n0 in range(0, D, CW) for i in range(NT)]
        ld = {}
        wcur = {}
        def load(j):
            n0, i = tiles[j]
            if i == 0:
                wt, wk = wr.next()
                s.add("pool", lambda e, wt=wt, n0=n0: e.dma_start(out=wt, in_=w[:, n0:n0 + CW].rearrange("(k p) c -> p k c", p=128)),
                      W=[wk], dma=f"tmw{wk[1]}")
                wcur[n0] = (wt, wk)
            at, ak = a_r.next()
            s.add("sp", lambda e, at=at, i=i: e.dma_start(out=at, in_=aTs[i]), R=[(aname, i)], W=[ak], dma=f"tma{ak[1]}")
            xt, xk = x_r.next()
            s.add("sp", lambda e, xt=xt, i=i, n0=n0: e.dma_start(out=xt, in_=xin[i * 128:(i + 1) * 128, n0:n0 + CW]),
                  R=[xk_of(i)], W=[xk], dma=f"tmx{xk[1]}")
            ld[j] = (at, ak, xt, xk)
        PF = 2
        for j in range(min(PF, len(tiles))):
            load(j)
        for j, (n0, i) in enumerate(tiles):
            if j + PF < len(tiles):
                load(j + PF)
            at, ak, xt, xk = ld.pop(j)
            wt, wk = wcur[n0]
            pb = mb[mi % 4]; mi += 1
            def mm(e, at=at, wt=wt, pb=pb):
                for k in range(DC):
                    ins = e.matmul(self.psb(pb, CW), at[:, k, :], wt[:, k, :], start=(k == 0), stop=(k == DC - 1))
                return ins
            s.add("pe", mm, R=[ak, wk], W=[("ps", pb)])
            ot, ok = o_r.next()
            if bias is not None:
                s.add("dve", lambda e, ot=ot, pb=pb, n0=n0: e.tensor_tensor(ot, self.psb(pb, CW), bbc[:, n0:n0 + CW], ALU.add),
                      R=[("ps", pb), "bbc"], W=[ok])
                s.add("dve", lambda e, ot=ot, n0=n0: e.tensor_tensor(ot, ot, gbc[:, n0:n0 + CW], ALU.mult), R=[ok, "gbc"], W=[ok])
            else:
                s.add("dve", lambda e, ot=ot, pb=pb, n0=n0: e.tensor_tensor(ot, self.psb(pb, CW), gbc[:, n0:n0 + CW], ALU.mult),
                      R=[("ps", pb), "gbc"], W=[ok])
            s.add("dve", lambda e, ot=ot, xt=xt: e.tensor_tensor(ot, ot, xt, ALU.add), R=[ok, xk], W=[ok])
            s.add("sp", lambda e, ot=ot, i=i, n0=n0: e.dma_start(out=xout[i * 128:(i + 1) * 128, n0:n0 + CW], in_=ot),
                  R=[ok], W=[(xout_name, i)], dma=f"tmo{ok[1]}")
        s.phase_fence()
        ar.reset()

    def mlp(self, b, l, xin, xin_name, xout, xout_name):
        cfg, s, ar = self.cfg, self.s, self.ar
        D, T, DC, DFF, G, MT = cfg.D, cfg.T, cfg.DC, cfg.DFF, cfg.G, cfg.MT
        K = lambda n: ("pers", n)
        NG = DFF // G
        GK = G // 128
        NS = MT // 128
        CW = min(512, D)
        self.alloc_bcast()
        gbc = ar.alloc([D], F32)
        s.mark_local("gbc")
        for bk in range(8):
            s.mark_local(("ps", bk))
        self.bcast_row(gbc, lambda k: self.mod_col(l, 5, b, k), "gbc", DC, [6, 7])
        hT = ar.alloc([DC, MT], BF16)
        pro, NPG = self.make_prologue([0, 1], nbuf=1)
        hid_r = ar.ring(s, "hid", 2, [GK, MT], BF16)
        acc = ar.alloc([NS, D], F32)
        w1_r = ar.ring(s, "w1", 2, [DC, 128], BF16)
        sq_r = ar.ring(s, "msq", 2, [MT], F32)
        w2_r = ar.ring(s, "w2", 2, [GK, CW], BF16)
        for i in range(NS):
            for g in range(NPG):
                s.mark_local(("hT", i, g))
            for n in range(D // CW):
                s.mark_local(("acc", i, n))
        hkeys = [("hT", i, g) for i in range(NS) for g in range(NPG)]
        mb1 = [2, 3]
        mb2 = [4, 5, 6, 7]
        m1 = [0]; m2 = [0]
        w1 = self.mlp_w1
        w2 = self.mlp_w2
        xt_r = self.pr_xring
        for tt in range(T // MT):
            tok0 = tt * MT
            for i in range(NS):
                pro(xin[tok0 + i * 128:tok0 + (i + 1) * 128, :], (xin_name, (tok0 // 128) + i),
                    lambda k, i=i: hT[:, k, i * 128:(i + 1) * 128],
                    lambda g, i=i: ("hT", i, g),
                    lambda k: self.A_col(l, 1, b, k),
                    lambda k: self.mod_col(l, 3, b, k))
            def m1_group(g):
                hd, hk = hid_r.next()
                for fc in range(GK):
                    f0 = g * G + fc * 128
                    wt, wk = w1_r.next()
                    fi = f0 // 128
                    if True:
                        s.add("sp", lambda e, wt=wt, fi=fi: e.dma_start(out=wt.rearrange("p k c -> p (k c)"), in_=self.w1s[fi]),
                              R=[("w1s", fi)], W=[wk], dma=f"w1h_{wk[1]}")
                    pb = mb1[m1[0] % 2]; m1[0] += 1
                    def mm(e, wt=wt, pb=pb):
                        for k in range(DC):
                            ins = e.matmul(self.psb(pb, MT), wt[:, k, :], hT[:, k, :], start=(k == 0), stop=(k == DC - 1))
                        return ins
                    s.add("pe", mm, R=[wk] + hkeys, W=[("ps", pb)])
                    sq, sqk = sq_r.next()
                    s.add("act", lambda e, sq=sq, pb=pb: e.activation(sq, self.psb(pb, MT), AF.Square), R=[("ps", pb)], W=[sqk])
                    s.add("dve", lambda e, hd=hd, fc=fc, pb=pb, sq=sq: e.scalar_tensor_tensor(hd[:, fc, :], self.psb(pb, MT), 0.0, sq, ALU.is_gt, ALU.mult),
                          R=[("ps", pb), sqk], W=[hk])
                return hd, hk
            def m2_group(g, hd, hk):
                for n in range(D // CW):
                    wt, wk = w2_r.next()
                    wi = g * (D // CW) + n
                    if tt == 0:
                        s.add("pool", lambda e, wt=wt, g=g, n=n: e.dma_start(
                            out=wt, in_=w2[l, g * G:(g + 1) * G, n * CW:(n + 1) * CW].rearrange("(k p) c -> p k c", p=128)),
                            W=[wk], dma=f"w2_{wk[1]}")
                        if T // MT > 1:
                            s.add("sp", lambda e, wt=wt, wi=wi: e.dma_start(out=self.w2s[wi], in_=wt.rearrange("p k c -> p (k c)")),
                                  R=[wk], W=[("w2s", wi)], dma=f"w2st{wk[1]}")
                    else:
                        s.add("sp", lambda e, wt=wt, wi=wi: e.dma_start(out=wt.rearrange("p k c -> p (k c)"), in_=self.w2s[wi]),
                              R=[("w2s", wi)], W=[wk], dma=f"w2h_{wk[1]}")
                    for i in range(NS):
                        pb = mb2[m2[0] % 4]; m2[0] += 1
                        def mm(e, wt=wt, hd=hd, i=i, pb=pb):
                            for k in range(GK):
                                ins = e.matmul(self.psb(pb, CW), hd[:, k, i * 128:(i + 1) * 128], wt[:, k, :], start=(k == 0), stop=(k == GK - 1))
                            return ins
                        s.add("pe", mm, R=[wk, hk], W=[("ps", pb)])
                        dst = acc[:, i, n * CW:(n + 1) * CW]
                        if g == 0:
                            s.add("act", lambda e, dst=dst, pb=pb: e.activation(dst, self.psb(pb, CW), AF.Identity), R=[("ps", pb)], W=[("acc", i, n)])
                        else:
                            s.add("dve", lambda e, dst=dst, pb=pb: e.tensor_tensor(dst, dst, self.psb(pb, CW), ALU.add),
                                  R=[("ps", pb), ("acc", i, n)], W=[("acc", i, n)])
            prev = m1_group(0)
            for g in range(NG):
                nxt = m1_group(g + 1) if g + 1 < NG else None
                m2_group(g, *prev)
                prev = nxt
            for i in range(NS):
                ti = tok0 // 128 + i
                xt, xk = xt_r.next()
                s.add("sp", lambda e, xt=xt, ti=ti: e.dma_start(out=xt, in_=xin[ti * 128:(ti + 1) * 128, :]), R=[(xin_name, ti)], W=[xk], dma="ep_x")
                akeys = [("acc", i, n) for n in range(D // CW)]
                s.add("dve", lambda e, i=i: e.tensor_tensor(acc[:, i, :], acc[:, i, :], gbc, ALU.mult), R=akeys + ["gbc"], W=akeys)
                s.add("dve", lambda e, i=i, xt=xt: e.tensor_tensor(acc[:, i, :], acc[:, i, :], xt, ALU.add), R=akeys + [xk], W=akeys)
                s.add("sp", lambda e, i=i, ti=ti: e.dma_start(out=xout[ti * 128:(ti + 1) * 128, :], in_=acc[:, i, :]),
                      R=akeys, W=[(xout_name, ti)] + akeys, dma=f"ep_o{i}")
        s.phase_fence()
        ar.reset()

    def conv_block(self, b):
        cfg, s, ar = self.cfg, self.s, self.ar
        D, T, DC, TB = cfg.D, cfg.T, cfg.DC, cfg.TB
        K = lambda n: ("pers", n)
        cn = self.colnames
        hT = ar.alloc([DC, TB], BF16)
        pro, NG = self.make_prologue([0, 1])
        wr = ar.ring(s, "pw1w", 4, [DC, 128], BF16)
        sg_r = ar.ring(s, "sg", 2, [512], F32)
        u_r = ar.ring(s, "uo", 3, [512], F32)
        mb = [2, 3, 4, 5, 6, 7]
        for bk in mb:
            s.mark_local(("ps", bk))
        mi = 0
        for t0 in range(0, T, TB):
            nt = min(TB, T - t0)
            ntile = nt // 128
            for g in range(NG):
                for i in range(ntile):
                    s.mark_local(("hT", i, g))
            for i in range(ntile):
                pro(self.x2[t0 + i * 128:t0 + (i + 1) * 128, :], ("x2", t0 // 128 + i),
                    lambda k, i=i: hT[:, k, i * 128:(i + 1) * 128],
                    lambda g, i=i: ("hT", i, g),
                    lambda k: self.A_col(1, 0, b, k),
                    lambda k: self.mod_col(1, 0, b, k))
            hkeys = lambda tl: [("hT", i, g) for i in tl for g in range(NG)]
            for fc in range(DC):
                wa, wak = wr.next()
                wg, wgk = wr.next()
                s.add("pool", lambda e, wa=wa, fc=fc: e.dma_start(out=wa, in_=self.pw1_w[:, fc * 128:(fc + 1) * 128].rearrange("(k p) c -> p k c", p=128)),
                      W=[wak], dma=f"pw1w{wak[1]}")
                s.add("pool", lambda e, wg=wg, fc=fc: e.dma_start(out=wg, in_=self.pw1_w[:, D + fc * 128:D + (fc + 1) * 128].rearrange("(k p) c -> p k c", p=128)),
                      W=[wgk], dma=f"pw1w{wgk[1]}")
                for ts in range(0, nt, 512):
                    n = min(512, nt - ts)
                    tl = range(ts // 128, (ts + n) // 128)
                    pa = mb[mi % 6]; mi += 1
                    pg = mb[mi % 6]; mi += 1
                    for (wt, wk, pb) in ((wa, wak, pa), (wg, wgk, pg)):
                        def mm(e, wt=wt, pb=pb, ts=ts, n=n):
                            for k in range(DC):
                                ins = e.matmul(self.psb(pb, n), wt[:, k, :], hT[:, k, ts:ts + n], start=(k == 0), stop=(k == DC - 1))
                            return ins
                        s.add("pe", mm, R=[wk] + hkeys(tl), W=[("ps", pb)])
                    sg, sgk = sg_r.next()
                    uo, uk = u_r.next()
                    s.add("act", lambda e, sg=sg, pg=pg, n=n, fc=fc: e.activation(sg[:, 0:n], self.psb(pg, n), AF.Sigmoid, bias=self.colvec(cn[("pw1b", 1)], fc), scale=1.0),
                          R=[("ps", pg), K("cols")], W=[sgk])
                    s.add("dve", lambda e, uo=uo, sg=sg, pa=pa, n=n, fc=fc: e.scalar_tensor_tensor(uo[:, 0:n], self.psb(pa, n), self.colvec(cn[("pw1b", 0)], fc), sg[:, 0:n], ALU.add, ALU.mult),
                          R=[("ps", pa), sgk, K("cols")], W=[uk])
                    s.add("sp", lambda e, uo=uo, fc=fc, t0=t0, ts=ts, n=n: e.dma_start(out=self.uT[fc * 128:(fc + 1) * 128, t0 + ts:t0 + ts + n], in_=uo[:, 0:n]),
                          R=[uk], W=[("uT", fc)], dma=f"uo{uk[1]}")
        s.phase_fence()
        ar.reset()
        self.preconvert_w1(1)
        ub_r = ar.ring(s, "cv_u", 2, [T + 2 * CONV_PAD], F32)
        ac_r = ar.ring(s, "cv_a", 2, [T], F32)
        a2_r = ar.ring(s, "cv_a2", 2, [T], F32)
        cb_r = ar.ring(s, "cv_c", 2, [T], BF16)
        sq_r = ar.ring(s, "cv_s", 2, [T], BF16)
        for bk in range(8):
            s.mark_local(("ps", bk))
        NTB = (T + 511) // 512
        assert 2 * NTB <= 8
        for j in range(2):
            ub, ubk = ub_r.next()
            s.add("pool", lambda e, ub=ub: e.memset(ub, 0.0), W=[ubk])
        for fc in range(DC):
            ub, ubk = ub_r.next()
            s.add("sp", lambda e, ub=ub, fc=fc: e.dma_start(out=ub[:, CONV_PAD:CONV_PAD + T], in_=self.uT[fc * 128:(fc + 1) * 128, :]),
                  R=[("uT", fc)], W=[ubk], dma=f"cvu{ubk[1]}")
            ac, ack = ac_r.next()
            a2, a2k = a2_r.next()
            s.add("dve", lambda e, ub=ub, ac=ac, fc=fc: e.tensor_scalar(ac, ub[:, 0:T], self.colvec(cn[("dw", 0)], fc), self.colvec(cn["dwb"], fc), ALU.mult, ALU.add),
                  R=[ubk, K("cols")], W=[ack])
            s.add("dve", lambda e, ub=ub, a2=a2, fc=fc: e.tensor_scalar(a2, ub[:, 1:1 + T], self.colvec(cn[("dw", 1)], fc), None, ALU.mult),
                  R=[ubk, K("cols")], W=[a2k])
            for j in range(2, CONV_W):
                tgt, tgk = (ac, ack) if j % 2 == 0 else (a2, a2k)
                s.add("dve", lambda e, ub=ub, tgt=tgt, fc=fc, j=j: e.scalar_tensor_tensor(tgt, ub[:, j:j + T], self.colvec(cn[("dw", j)], fc), tgt, ALU.mult, ALU.add),
                      R=[ubk, K("cols"), tgk], W=[tgk])
            s.add("dve", lambda e, ac=ac, a2=a2: e.tensor_tensor(ac, ac, a2, ALU.add), R=[ack, a2k], W=[ack])
            cb, cbk = cb_r.next()
            sq, sqk = sq_r.next()
            s.add("act", lambda e, cb=cb, ac=ac: e.activation(cb, ac, AF.Identity), R=[ack], W=[cbk])
            s.add("act", lambda e, sq=sq, cb=cb: e.activation(sq, cb, AF.Square), R=[cbk], W=[sqk])
            def st(e, cb=cb, sq=sq, fc=fc):
                for tb in range(NTB):
                    n = min(512, T - tb * 512)
                    e.matmul(self.psb(tb, n), self.ones16, cb[:, tb * 512:tb * 512 + n], start=(fc == 0), stop=(fc == DC - 1))
                    ins = e.matmul(self.psb(NTB + tb, n), self.ones16, sq[:, tb * 512:tb * 512 + n], start=(fc == 0), stop=(fc == DC - 1))
                return ins
            s.add("pe", st, R=[cbk, sqk, K("ones16")], W=[("ps", bk) for bk in range(2 * NTB)])
            s.add("sp", lambda e, cb=cb, fc=fc: e.dma_start(out=self.cT[fc * 128:(fc + 1) * 128, :], in_=cb), R=[cbk], W=[("cT", fc)], dma=f"cvc{cbk[1]}")
        mean = ar.alloc([T], F32)
        rstd = ar.alloc([T], F32)
        s.mark_local("lnstat")
        tmpm = ar.alloc([T], F32)
        S1 = self.ps[:, 0:T]
        S2 = self.ps[:, NTB * 512:NTB * 512 + T]
        pk = [("ps", bk) for bk in range(2 * NTB)]
        s.mark_local("tmpm")
        s.add("dve", lambda e: e.tensor_scalar(mean, S1, 1.0 / D, None, ALU.mult), R=pk, W=["lnstat"])
        s.add("dve", lambda e: e.tensor_tensor(tmpm, mean, mean, ALU.mult), R=["lnstat"], W=["tmpm"])
        s.add("dve", lambda e: e.scalar_tensor_tensor(rstd, S2, 1.0 / D, tmpm, ALU.mult, ALU.subtract), R=pk + ["tmpm"], W=["lnstat"])
        s.add("dve", lambda e: e.tensor_scalar(rstd, rstd, LN_EPS, None, ALU.add), R=["lnstat"], W=["lnstat"])
        s.add("act", lambda e: e.activation(rstd, rstd, AF.Sqrt), R=["lnstat"], W=["lnstat"])
        s.add("dve", lambda e: e.reciprocal(rstd, rstd), R=["lnstat"], W=["lnstat"])
        cl_r = ar.ring(s, "ln_c", 2, [T], BF16)
        t_r = ar.ring(s, "ln_t", 2, [T], F32)
        v_r = ar.ring(s, "ln_v", 2, [T], BF16)
        for fc in range(DC):
            cl, clk = cl_r.next()
            s.add("sp", lambda e, cl=cl, fc=fc: e.dma_start(out=cl, in_=self.cT[fc * 128:(fc + 1) * 128, :]), R=[("cT", fc)], W=[clk], dma=f"lnc{clk[1]}")
            tt, tk = t_r.next()
            s.add("dve", lambda e, tt=tt, cl=cl: e.tensor_tensor(tt, cl, mean, ALU.subtract), R=[clk, "lnstat"], W=[tk])
            s.add("dve", lambda e, tt=tt: e.tensor_tensor(tt, tt, rstd, ALU.mult), R=[tk, "lnstat"], W=[tk])
            vv, vk = v_r.next()
            s.add("act", lambda e, vv=vv, tt=tt, fc=fc: e.activation(vv, tt, AF.Silu, bias=self.colvec(cn["lnb"], fc), scale=self.colvec(cn["lng"], fc)),
                  R=[tk, K("cols")], W=[vk])
            s.add("sp", lambda e, vv=vv, fc=fc: e.dma_start(out=self.vTs[:, :, fc, :].rearrange("i p t -> p i t"), in_=vv.rearrange("p (i t) -> p i t", t=128)),
                  R=[vk], W=[("vTs", i) for i in range(T // 128)], dma=f"lnv{vk[1]}")
        s.phase_fence()
        ar.reset()

    def final_norm(self, b):
        cfg, s, ar = self.cfg, self.s, self.ar
        D, T = cfg.D, cfg.T
        gf = ar.alloc([D], F32)
        s.mark_local("gf")
        s.add("sp", lambda e: e.dma_start(out=gf, in_=self.norm_final_g.partition_broadcast(128)), W=["gf"], dma="gf")
        x_r = ar.ring(s, "fn_x", 3, [D], F32)
        junk = ar.alloc([D], BF16)
        st_r = ar.ring(s, "fn_s", 3, [2], F32)
        for i in range(T // 128):
            xt, xk = x_r.next()
            st, sk = st_r.next()
            s.add("sp", lambda e, xt=xt, i=i: e.dma_start(out=xt, in_=self.x4[i * 128:(i + 1) * 128, :]), R=[("x4", i)], W=[xk], dma=f"fnx{xk[1]}")
            s.add("act", lambda e, xt=xt, st=st: e.activation(junk, xt, AF.Square, accum_out=st[:, 0:1]), R=[xk], W=[sk])
            self.rstd_ops(st[:, 0:1], st[:, 1:2], D, RMS_EPS, R=[sk], W=[sk])
            s.add("dve", lambda e, xt=xt, st=st: e.scalar_tensor_tensor(xt, xt, st[:, 1:2], gf, ALU.mult, ALU.mult), R=[xk, sk, "gf"], W=[xk])
            s.add("sp", lambda e, xt=xt, i=i: e.dma_start(out=self.out[b, i * 128:(i + 1) * 128, :], in_=xt), R=[xk], W=[("out", b, i)], dma=f"fno{xk[1]}")
        s.phase_fence()
        ar.reset()


def rope_tables(cfg):
    T, C, GW = cfg.T, cfg.C, cfg.GRID_W
    rows = T // GW
    row = np.repeat(np.arange(rows, dtype=np.float32), GW)
    col = np.tile(np.arange(GW, dtype=np.float32), rows)
    n = 32
    inv = (np.float32(ROPE_BASE) ** (-np.arange(n, dtype=np.float32) / np.float32(n))).astype(np.float32)
    ar_ = row[:, None] * inv
    ac_ = col[:, None] * inv
    ang = np.concatenate([ar_, ar_, ac_, ac_], axis=-1).astype(np.float32)
    cos = np.ones((128, C + T), np.float32)
    sin = np.zeros((128, C + T), np.float32)
    cos[:, C:] = np.cos(ang).T
    sin[:, C:] = np.sin(ang).T
    return cos, sin


def rot_matrix():
    R = np.zeros((128, 128), np.float32)
    for m in range(128):
        blk = (m // 32) % 2
        if blk == 0:
            R[m + 32, m] = -1.0
        else:
            R[m - 32, m] = 1.0
    return R


def make_in_maps(cfg, ncores, inp):
    NB = cfg.NB
    cos, sin = rope_tables(cfg)
    common = dict(
        ada_w=inp["ada_w"], ada_b=inp["ada_b"], norm_mix_g=inp["norm_mix_g"], norm_mlp_g=inp["norm_mlp_g"],
        norm_final_g=inp["norm_final_g"].reshape(1, -1), attn_w_qkv=inp["attn_w_qkv"][0], attn_w_o=inp["attn_w_o"][0],
        lam_in=np.concatenate([inp["lambda_q1"], inp["lambda_k1"], inp["lambda_q2"], inp["lambda_k2"]], 0),
        attn_subln_g=inp["attn_subln_g"], conv_pw1_w=inp["conv_pw1_w"][0], conv_pw1_b=inp["conv_pw1_b"][0].reshape(2, -1),
        conv_dw_w=inp["conv_dw_w"][0], conv_dw_b=inp["conv_dw_b"], conv_ln_g=inp["conv_ln_g"], conv_ln_b=inp["conv_ln_b"],
        conv_pw2_w=inp["conv_pw2_w"][0], conv_pw2_b=inp["conv_pw2_b"], mlp_w1=inp["mlp_w1"], mlp_w2=inp["mlp_w2"],
        rope_cos=cos, rope_sin=sin, rot_m=rot_matrix(), ident=np.eye(128, dtype=np.float32),
    )
    common = {k: np.ascontiguousarray(np.asarray(v, dtype=np.float32)) for k, v in common.items()}
    maps = []
    for c in range(ncores):
        sl = slice(c * NB, (c + 1) * NB)
        m = dict(common)
        m["x"] = np.ascontiguousarray(inp["x"][sl])
        m["ctx"] = np.ascontiguousarray(inp["ctx"][sl])
        m["cvec"] = np.ascontiguousarray(np.concatenate([inp["c"][sl], inp["c_ctx"].reshape(1, -1)], 0))
        maps.append(m)
    return maps


NCORES = 8


def kernel(**inputs):
    inp = {k: np.asarray(v) for k, v in inputs.items()}
    B = inp["x"].shape[0]
    cfg = Cfg(D=inp["x"].shape[2], T=inp["x"].shape[1], C=inp["ctx"].shape[1], DFF=inp["mlp_w1"].shape[2], NB=B // NCORES)
    bld = Builder(cfg)
    nc = bld.build()
    maps = make_in_maps(cfg, NCORES, inp)
    res = run_bass_kernel_spmd(nc, maps, core_ids=list(range(NCORES)))
    outs = [np.asarray(r["out"]) for r in res.results]
    return np.concatenate(outs, axis=0).astype(np.float32)
```

```python
import math
from contextlib import ExitStack

import numpy as np
import concourse.bass as bass
import concourse.mybir as mybir
from concourse.bass_utils import run_bass_kernel_spmd

F32 = mybir.dt.float32
BF16 = mybir.dt.bfloat16
AF = mybir.ActivationFunctionType
ALU = mybir.AluOpType
AX = mybir.AxisListType

RMS_EPS = 1e-6
LN_EPS = 1e-5
CONV_W = 31
CONV_PAD = 15
ROPE_BASE = 10000.0
DK = 128
DV = 256


class Cfg:
    def __init__(self, D=4096, T=2048, C=256, DFF=16384, NB=1, GRID_W=64, debug=(), nphase=9):
        self.D, self.T, self.C, self.DFF, self.NB, self.GRID_W = D, T, C, DFF, NB, GRID_W
        self.DC = D // 128
        self.H = D // (2 * DK)
        self.TT = C + T
        self.NBX = NB + 1
        self.FC = DFF // 128
        self.debug = tuple(debug)
        self.nphase = nphase
        self.G = min(1024, DFF)
        self.MT = min(512, T)
        self.TB = min(1024, T)


class Sched:
    STREAMS = ("pe", "act", "dve", "pool", "sp")

    def __init__(self):
        self.ops = []
        self.lastw = {}
        self.readers = {}
        self.seen = set()
        self.fence = {}
        self.local = set()

    def add(self, stream, emit, R=(), W=(), dma=None, inc=16):
        i = len(self.ops)
        W = tuple(W) + tuple(k for k in R if isinstance(k, tuple) and k[0] == "ps")
        deps = set()
        for k in tuple(R) + tuple(W):
            if k not in self.seen:
                self.seen.add(k)
                deps.update(self.fence.values())
            w = self.lastw.get(k)
            if w is not None:
                deps.add(w)
        for k in W:
            deps.update(self.readers.get(k, ()))
        for k in R:
            self.readers.setdefault(k, []).append(i)
        for k in W:
            self.lastw[k] = i
            self.readers[k] = []
        deps.discard(i)
        self.ops.append(dict(stream=stream, emit=emit, deps=deps, dma=dma, inc=inc))
        return i

    def mark_local(self, key):
        self.local.add(key)

    def phase_fence(self):
        for k in list(self.local):
            ids = []
            if k in self.lastw:
                ids.append(self.lastw.pop(k))
            ids.extend(self.readers.pop(k, ()))
            self.seen.discard(k)
            for i in ids:
                o = self.ops[i]
                fk = ("dma", o["dma"]) if o["dma"] is not None else ("eng", o["stream"])
                if self.fence.get(fk, -1) < i:
                    self.fence[fk] = i
        self.local = set()

    def finalize(self):
        ops = self.ops
        for o in ops:
            o["signal"] = o["dma"] is not None
        for o in ops:
            for d in o["deps"]:
                p = ops[d]
                if p["dma"] is None:
                    if p["stream"] == "pe" and o["stream"] == "pe" and o["dma"] is None:
                        continue
                    p["signal"] = True
        cnt = {}
        for o in ops:
            key = ("dma", o["dma"]) if o["dma"] is not None else ("eng", o["stream"])
            o["semkey"] = key
            if o["signal"]:
                cnt[key] = cnt.get(key, 0) + (o["inc"] if o["dma"] is not None else 1)
                o["val"] = cnt[key]
        self.semkeys = sorted(cnt.keys(), key=str)
        self.by_stream = {s: [o for o in ops if o["stream"] == s] for s in self.STREAMS}

    def emit_stream(self, stream, eng, sems):
        waited = {}
        ops = self.ops
        for o in self.by_stream[stream]:
            need = {}
            for d in o["deps"]:
                p = ops[d]
                if not p["signal"]:
                    continue
                if p["dma"] is None and p["stream"] == "pe" and stream == "pe" and o["dma"] is None:
                    continue
                k = p["semkey"]
                if need.get(k, 0) < p["val"]:
                    need[k] = p["val"]
            for k, v in need.items():
                if waited.get(k, 0) >= v:
                    continue
                eng.wait_ge(sems[k], v)
                waited[k] = v
            ins = o["emit"](eng) if o["emit"] is not None else None
            if o["signal"]:
                assert ins is not None
                ins.then_inc(sems[o["semkey"]], o["inc"] if o["dma"] is not None else 1)


class Ring:
    def __init__(self, sched, name, aps):
        self.name, self.aps, self.i = name, aps, 0
        for j in range(len(aps)):
            sched.mark_local((name, j))

    def next(self):
        j = self.i % len(self.aps)
        self.i += 1
        return self.aps[j], (self.name, j)


class Arena:
    def __init__(self, ap, words):
        self.ap, self.words, self.off = ap, words, 0

    def reset(self):
        self.off = 0

    def alloc(self, shape, dtype):
        n = 1
        for s in shape:
            n *= s
        words = n if dtype == F32 else (n + 1) // 2
        words = (words + 15) // 16 * 16
        assert self.off + words <= self.words, ("SBUF arena overflow", self.off, words, self.words)
        v = self.ap[:, self.off:self.off + words]
        self.off += words
        if dtype == BF16:
            v = v.bitcast(BF16)
        v = v[:, 0:n]
        if len(shape) == 2:
            v = v.rearrange("p (a b) -> p a b", a=shape[0])
        elif len(shape) == 3:
            v = v.rearrange("p (a b c) -> p a b c", a=shape[0], b=shape[1])
        return v

    def ring(self, sched, name, n, shape, dtype):
        return Ring(sched, name, [self.alloc(shape, dtype) for _ in range(n)])


class Builder:
    def __init__(self, cfg):
        self.cfg = cfg
        self.nc = bass.Bass("TRN2", target_bir_lowering=False)
        self.s = Sched()
        self.dram = {}

    def din(self, name, shape, dtype=F32):
        t = self.nc.dram_tensor(name, list(shape), dtype, kind="ExternalInput").ap()
        self.dram[name] = t
        return t

    def dscratch(self, name, shape, dtype):
        kind = "ExternalOutput" if name in self.cfg.debug else "Internal"
        t = self.nc.dram_tensor(name, list(shape), dtype, kind=kind).ap()
        self.dram[name] = t
        return t

    def psb(self, b, n=512, off=0):
        return self.ps[:, b * 512 + off:b * 512 + off + n]

    def psb16(self, b):
        return self.ps[:, b * 512:(b + 1) * 512].bitcast(BF16)

    def rstd_ops(self, ss, out, n, eps, R, W):
        self.s.add("dve", lambda e: e.tensor_scalar(out, ss, 1.0 / n, eps, ALU.mult, ALU.add), R=R, W=W)
        self.s.add("act", lambda e: e.activation(out, out, AF.Sqrt), R=W, W=W)
        self.s.add("dve", lambda e: e.reciprocal(out, out), R=W, W=W)

    def build(self):
        cfg, nc, s = self.cfg, self.nc, self.s
        D, T, C, DFF, NB, DC, H, TT, NBX = cfg.D, cfg.T, cfg.C, cfg.DFF, cfg.NB, cfg.DC, cfg.H, cfg.TT, cfg.NBX
        x = self.din("x", [NB, T, D])
        ctx = self.din("ctx", [NB, C, D])
        cvec = self.din("cvec", [NBX, D])
        ada_w = self.din("ada_w", [2, D, 6 * D])
        ada_b = self.din("ada_b", [2, 6 * D])
        norm_mix_g = self.din("norm_mix_g", [2, D])
        norm_mlp_g = self.din("norm_mlp_g", [2, D])
        norm_final_g = self.din("norm_final_g", [1, D])
        w_qkv = self.din("attn_w_qkv", [D, 3 * D])
        w_o = self.din("attn_w_o", [D, D])
        lam_in = self.din("lam_in", [4, DK])
        subln_g = self.din("attn_subln_g", [1, DV])
        pw1_w = self.din("conv_pw1_w", [D, 2 * D])
        pw1_b = self.din("conv_pw1_b", [2, D])
        dw_w = self.din("conv_dw_w", [CONV_W, D])
        dw_b = self.din("conv_dw_b", [1, D])
        ln_g = self.din("conv_ln_g", [1, D])
        ln_b = self.din("conv_ln_b", [1, D])
        pw2_w = self.din("conv_pw2_w", [D, D])
        pw2_b = self.din("conv_pw2_b", [1, D])
        mlp_w1 = self.din("mlp_w1", [2, D, DFF])
        mlp_w2 = self.din("mlp_w2", [2, DFF, D])
        rope_cos = self.din("rope_cos", [128, TT])
        rope_sin = self.din("rope_sin", [128, TT])
        rot_m = self.din("rot_m", [128, 128])
        ident_in = self.din("ident", [128, 128])
        out = self.nc.dram_tensor("out", [NB, T, D], F32, kind="ExternalOutput").ap()
        qT = self.dscratch("qT", [D, T], BF16)
        kT = self.dscratch("kT", [D, TT], BF16)
        vS = self.dscratch("vS", [TT, D], BF16)
        oTs = self.dscratch("oTs", [T // 128, 128, DC, 128], BF16)
        x1 = self.dscratch("x1", [T, D], F32)
        x2 = self.dscratch("x2", [T, D], F32)
        x3 = self.dscratch("x3", [T, D], F32)
        x4 = self.dscratch("x4", [T, D], F32)
        uT = self.dscratch("uT", [D, T], F32)
        cT = self.dscratch("cT", [D, T], BF16)
        vTs = self.dscratch("vTs", [T // 128, 128, DC, 128], BF16)
        w1s = self.dscratch("w1s", [DFF // 128, 128, DC * 128], BF16)
        w2s = self.dscratch("w2s", [(DFF // cfg.G) * (D // min(512, D)), 128, (cfg.G // 128) * min(512, D)], BF16)
        self.__dict__.update(locals())

        ARENA_WORDS = 48 * 1024
        with ExitStack() as es:
            arena_t = es.enter_context(nc.sbuf_tensor("arena", [128, ARENA_WORDS], F32))
            self.ar = Arena(arena_t[:, :], ARENA_WORDS)
            NCOLV = CONV_W + 3 + 4 + 2 + 1
            pers_words = 128 * 3 + 128 + 2 * (6 * DC * NBX) + 8 * NBX * DC + NCOLV * DC + 64 + DV + 64
            pers_t = es.enter_context(nc.sbuf_tensor("pers", [128, pers_words], F32))
            self.pers = Arena(pers_t[:, :], pers_words)
            self.ps = es.enter_context(nc.psum_tensor("ps", [128, 4096], F32))[:, :]

            NP = cfg.nphase
            self.setup_phase()
            for b in range(NB):
                if NP >= 2: self.layer0_qkv(b)
                if NP >= 3: self.attention(b)
                if NP >= 4: self.proj_tm(b, oTs, "oTs", w_o, None, x[b], ("xin", b), x1, "x1", self.gate_cols(0, 2, b), "wo")
                if NP >= 5: self.mlp(b, 0, x1, "x1", x2, "x2")
                if NP >= 6: self.conv_block(b)
                if NP >= 7: self.proj_tm(b, vTs, "vTs", pw2_w, pw2_b, x2, "x2", x3, "x3", self.gate_cols(1, 2, b), "pw2")
                if NP >= 8: self.mlp(b, 1, x3, "x3", x4, "x4")
                if NP >= 9: self.final_norm(b)
            lastdma = {}
            for i, o in enumerate(s.ops):
                if o["dma"] is not None:
                    lastdma[o["dma"]] = i
            fin = s.add("sp", None)
            s.ops[fin]["deps"].update(lastdma.values())
            s.finalize()
            sems = {}
            for k in s.semkeys:
                sems[k] = es.enter_context(nc.semaphore("s_" + "_".join(str(z) for z in (k if isinstance(k, tuple) else (k,))).replace(" ", "")[:40]))
            self.nsems = len(sems)
            with nc.Block() as block:
                @block.tensor
                def _(e):
                    s.emit_stream("pe", e, sems)

                @block.scalar
                def _(e):
                    s.emit_stream("act", e, sems)

                @block.vector
                def _(e):
                    s.emit_stream("dve", e, sems)

                @block.gpsimd
                def _(e):
                    s.emit_stream("pool", e, sems)

                @block.sync
                def _(e):
                    s.emit_stream("sp", e, sems)
        return nc

    def mod_col(self, l, v, j, k):
        return self.mods[l][:, v * self.cfg.DC + k, j:j + 1]

    def gate_cols(self, l, v, b):
        return (l, v, b)

    def colvec(self, idx, k):
        return self.cols[:, idx, k:k + 1]

    def setup_phase(self):
        cfg, s, P, ar = self.cfg, self.s, self.pers, self.ar
        D, DC, NBX, NB = cfg.D, cfg.DC, cfg.NBX, cfg.NB
        self.ident16 = P.alloc([128], BF16)
        self.ident32 = P.alloc([128], F32)
        self.rot16 = P.alloc([128], BF16)
        self.ones16 = P.alloc([128], BF16)
        self.ones32 = P.alloc([128], F32)
        self.mods = [P.alloc([6 * DC, NBX], F32) for _ in range(2)]
        self.modA = P.alloc([4 * NBX, DC], F32)
        NCOLV = CONV_W + 3 + 4 + 2 + 1
        self.cols = P.alloc([NCOLV, DC], F32)
        self.lamc = P.alloc([8], F32)
        self.gsub = P.alloc([DV], F32)
        K = lambda n: ("pers", n)
        s.add("sp", lambda e: e.dma_start(out=self.ident32, in_=self.ident_in), W=[K("id32")], dma="c_id32")
        s.add("pool", lambda e: e.dma_start(out=self.ident16, in_=self.ident_in), W=[K("id16")], dma="c_id16")
        s.add("pool", lambda e: e.dma_start(out=self.rot16, in_=self.rot_m), W=[K("rot16")], dma="c_rot")
        s.add("dve", lambda e: e.memset(self.ones16, 1.0), W=[K("ones16")])
        s.add("dve", lambda e: e.memset(self.ones32, 1.0), W=[K("ones32")])
        s.add("sp", lambda e: e.dma_start(out=self.gsub, in_=self.subln_g.partition_broadcast(128)), W=[K("gsub")], dma="c_gsub")
        lam_init0 = 0.8 - 0.6 * math.exp(-0.3 * 0)
        s.add("dve", lambda e: e.tensor_scalar(self.gsub, self.gsub, 1.0 - lam_init0, None, ALU.mult), R=[K("gsub")], W=[K("gsub")])

        VPT = max(1, 128 // DC)
        stage = ar.ring(s, "cl_stage", 2, [128], F32)
        vec_srcs = []
        def rowsrc(ap_row):
            return ap_row.rearrange("o (k p) -> (o k) p", p=128)
        colnames = {}
        ci = 0
        for j in range(CONV_W):
            vec_srcs.append((self.cols[:, ci, :], rowsrc(self.dw_w[j:j + 1, :]))); colnames[("dw", j)] = ci; ci += 1
        for nm, src in (("dwb", self.dw_b), ("lng", self.ln_g), ("lnb", self.ln_b)):
            vec_srcs.append((self.cols[:, ci, :], rowsrc(src))); colnames[nm] = ci; ci += 1
        for l in range(2):
            vec_srcs.append((self.cols[:, ci, :], rowsrc(self.norm_mix_g[l:l + 1, :]))); colnames[("gmix", l)] = ci; ci += 1
            vec_srcs.append((self.cols[:, ci, :], rowsrc(self.norm_mlp_g[l:l + 1, :]))); colnames[("gmlp", l)] = ci; ci += 1
        for h in range(2):
            vec_srcs.append((self.cols[:, ci, :], rowsrc(self.pw1_b[h:h + 1, :]))); colnames[("pw1b", h)] = ci; ci += 1
        self.colnames = colnames
        cs32 = ar.alloc([NBX, DC], F32)
        s.mark_local("cs32")
        for j in range(NBX):
            vec_srcs.append((cs32[:, j, :], rowsrc(self.cvec[j:j + 1, :])))
        for g0 in range(0, len(vec_srcs), VPT):
            grp = vec_srcs[g0:g0 + VPT]
            st, stk = stage.next()
            s.add("dve", lambda e, st=st: e.memset(st, 0.0), W=[stk])
            for i, (dst, src) in enumerate(grp):
                s.add("sp", lambda e, st=st, i=i, src=src: e.dma_start(out=st[i * DC:(i + 1) * DC, :], in_=src),
                      W=[stk], dma=f"cl{stk[1]}")
            s.mark_local(("ps", 0))
            pst = self.psb(0, 128)
            s.add("pe", lambda e, st=st, pst=pst: e.transpose(pst, st, self.ident32), R=[stk, K("id32")], W=[("ps", 0)])
            for i, (dst, src) in enumerate(grp):
                s.add("dve", lambda e, dst=dst, i=i, pst=pst: e.tensor_copy(dst, pst[:, i * DC:(i + 1) * DC]),
                      R=[("ps", 0)], W=[K("cols"), "cs32"])
        lamt = ar.alloc([8], F32)
        s.mark_local("lamt")
        s.add("sp", lambda e: e.dma_start(out=lamt[:, 0:4], in_=self.lam_in.rearrange("a d -> d a"), allow_slow_non_contiguous=True),
              W=["lamt"], dma="c_lam")
        def lam_a(e):
            e.tensor_tensor(lamt[:, 4:5], lamt[:, 0:1], lamt[:, 1:2], ALU.mult)
            return e.tensor_tensor(lamt[:, 5:6], lamt[:, 2:3], lamt[:, 3:4], ALU.mult)
        s.add("dve", lam_a, R=["lamt"], W=["lamt"])
        s.mark_local(("ps", 1))
        s.add("pe", lambda e: e.matmul(self.psb(1, 2), self.ones32, lamt[:, 4:6], start=True, stop=True),
              R=["lamt", K("ones32")], W=[("ps", 1)])
        s.add("act", lambda e: e.activation(lamt[:, 6:8], self.psb(1, 2), AF.Exp), R=[("ps", 1)], W=["lamt"])
        s.add("dve", lambda e: e.scalar_tensor_tensor(self.lamc[:, 0:1], lamt[:, 6:7], lam_init0, lamt[:, 7:8], ALU.add, ALU.subtract),
              R=["lamt"], W=[K("lam")])
        s.add("dve", lambda e: e.tensor_scalar(self.lamc[:, 1:2], self.lamc[:, 0:1], -1.0, None, ALU.mult), R=[K("lam")], W=[K("lam")])

        cs16 = ar.alloc([DC, NBX], BF16)
        s.mark_local("cs16")
        s.add("act", lambda e: e.activation(cs16.rearrange("p k j -> p j k"), cs32, AF.Silu), R=["cs32"], W=["cs16"])
        wr = ar.ring(s, "adaw", 3, [DC, 512], BF16)
        br = ar.ring(s, "adab", 3, [512], F32)
        pbank = [2, 3]
        for bk in pbank:
            s.mark_local(("ps", bk))
        it = 0
        for l in range(2):
            for cg in range(6 * D // 512):
                w, wk = wr.next()
                bt, bk_ = br.next()
                s.add("pool", lambda e, w=w, l=l, cg=cg: e.dma_start(
                    out=w, in_=self.ada_w[l, :, cg * 512:(cg + 1) * 512].rearrange("(k p) c -> p k c", p=128)),
                    W=[wk], dma=f"adaw{wk[1]}")
                s.add("sp", lambda e, bt=bt, l=l, cg=cg: e.dma_start(out=bt[0:1, :], in_=self.ada_b[l:l + 1, cg * 512:(cg + 1) * 512]),
                      W=[bk_], dma=f"adab{bk_[1]}")
                pb = pbank[it % 2]
                it += 1
                def mm(e, w=w, bt=bt, pb=pb):
                    for sub in range(4):
                        o = self.psb(pb, NBX, sub * NBX)
                        for k in range(DC):
                            e.matmul(o, w[:, k, sub * 128:(sub + 1) * 128], cs16[:, k, :], start=(k == 0), stop=False)
                        ins = e.matmul(o, bt[0:1, sub * 128:(sub + 1) * 128], self.ones32[0:1, 0:NBX], start=False, stop=True)
                    return ins
                s.add("pe", mm, R=[wk, bk_, "cs16", K("ones32")], W=[("ps", pb)])
                s.add("dve", lambda e, l=l, cg=cg, pb=pb: e.tensor_copy(
                    self.mods[l][:, cg * 4:(cg + 1) * 4, :], self.psb(pb, 4 * NBX).rearrange("p (a j) -> p a j", a=4)),
                    R=[("ps", pb)], W=[K("mods")])
        for l in range(2):
            for si, (gname, v) in enumerate((("gmix", 1), ("gmlp", 4))):
                for j in range(NBX):
                    dst = self.modA[:, (l * 2 + si) * NBX + j, :]
                    g = self.cols[:, colnames[(gname, l)], :]
                    sc = self.mods[l][:, v * DC:(v + 1) * DC, j]
                    s.add("dve", lambda e, dst=dst, g=g, sc=sc: e.scalar_tensor_tensor(dst, sc, 1.0, g, ALU.add, ALU.mult),
                          R=[K("mods"), K("cols")], W=[K("modA")])
        s.phase_fence()
        ar.reset()

    def A_col(self, l, si, j, k):
        return self.modA[:, (l * 2 + si) * self.cfg.NBX + j, k:k + 1]

    def make_prologue(self, psbanks, nbuf=2):
        cfg, s, ar = self.cfg, self.s, self.ar
        D, DC = cfg.D, cfg.DC
        xt = ar.ring(s, "pr_x", nbuf, [D], F32)
        xn = ar.ring(s, "pr_xn", nbuf, [D], BF16)
        st = ar.ring(s, "pr_st", 4, [2], F32)
        self.pr_xring = xt
        for bk in psbanks:
            s.mark_local(("ps", bk))
        state = dict(i=0)
        K = lambda n: ("pers", n)
        GS = min(8, DC)

        def run(src_ap, src_key, dst_fn, dst_key_fn, A_fn, sh_fn):
            x_, xk = xt.next()
            n_, nk = xn.next()
            t_, tk = st.next()
            s.add("sp", lambda e: e.dma_start(out=x_, in_=src_ap), R=[src_key], W=[xk], dma=f"prx{xk[1]}")
            s.add("act", lambda e: e.activation(n_, x_, AF.Square, accum_out=t_[:, 0:1]), R=[xk], W=[tk, nk])
            self.rstd_ops(t_[:, 0:1], t_[:, 1:2], D, RMS_EPS, R=[tk], W=[tk])
            s.add("dve", lambda e: e.tensor_scalar(n_, x_, t_[:, 1:2], None, ALU.mult), R=[xk, tk], W=[nk])
            for g in range(DC // GS):
                bk = psbanks[state["i"] % len(psbanks)]
                state["i"] += 1
                pv = self.psb16(bk)
                def tr(e, g=g, pv=pv):
                    for q in range(GS):
                        k = g * GS + q
                        ins = e.transpose(pv[:, q * 128:(q + 1) * 128], n_[:, k * 128:(k + 1) * 128], self.ident16)
                    return ins
                s.add("pe", tr, R=[nk, K("id16")], W=[("ps", bk)])
                def ev(e, g=g, pv=pv):
                    for q in range(GS):
                        k = g * GS + q
                        ins = e.activation(dst_fn(k), pv[:, q * 128:(q + 1) * 128], AF.Identity, bias=sh_fn(k), scale=A_fn(k))
                    return ins
                s.add("act", ev, R=[("ps", bk), K("mods"), K("modA")], W=[dst_key_fn(g)])
        return run, DC // GS

    def layer0_qkv(self, b):
        cfg, s, ar = self.cfg, self.s, self.ar
        D, T, C, DC, H, TT = cfg.D, cfg.T, cfg.C, cfg.DC, cfg.H, cfg.TT
        K = lambda n: ("pers", n)
        TB = cfg.TB
        hT = ar.alloc([DC, TB], BF16)
        pro, NG = self.make_prologue([0, 1])
        wr = ar.ring(s, "qkvw", 3, [DC, 256], BF16)
        cosb = ar.alloc([TB], F32)
        sinb = ar.alloc([TB], F32)
        s.mark_local("cosb"); s.mark_local("sinb")
        qs_r = ar.ring(s, "qs", 2, [512], BF16)
        t1_r = ar.ring(s, "t1", 2, [512], F32)
        t2_r = ar.ring(s, "t2", 2, [512], F32)
        so_r = ar.ring(s, "so", 3, [512], BF16)
        vo_r = ar.ring(s, "vo", 3, [256], BF16)
        mb = [2, 3, 4, 5]
        rb = [6, 7]
        for bk in mb + rb:
            s.mark_local(("ps", bk))
        mi = [0, 0]
        blocks = [(True, 0, C)] + [(False, t0, min(TB, T - t0)) for t0 in range(0, T, TB)]
        for (is_ctx, t0, nt) in blocks:
            pos0 = t0 if is_ctx else C + t0
            ntile = nt // 128
            for g in range(NG):
                for i in range(ntile):
                    s.mark_local(("hT", i, g))
            for i in range(ntile):
                if is_ctx:
                    src, skey, j = self.ctx[b, t0 + i * 128:t0 + (i + 1) * 128, :], ("ctxin", b), cfg.NB
                else:
                    src, skey, j = self.x[b, t0 + i * 128:t0 + (i + 1) * 128, :], ("xin", b), b
                pro(src, skey,
                    lambda k, i=i: hT[:, k, i * 128:(i + 1) * 128],
                    lambda g, i=i: ("hT", i, g),
                    lambda k, j=j: self.A_col(0, 0, j, k),
                    lambda k, j=j: self.mod_col(0, 0, j, k))
            import os as _os
            if _os.environ.get("KSTOP") == "pro":
                continue
            s.add("sp", lambda e, pos0=pos0, nt=nt: e.dma_start(out=cosb[:, 0:nt], in_=self.rope_cos[:, pos0:pos0 + nt]), W=["cosb"], dma="cosb")
            s.add("sp", lambda e, pos0=pos0, nt=nt: e.dma_start(out=sinb[:, 0:nt], in_=self.rope_sin[:, pos0:pos0 + nt]), W=["sinb"], dma="sinb")
            hkeys = lambda tl: [("hT", i, g) for i in tl for g in range(NG)]
            cgs = range(D // 256, 3 * D // 256) if is_ctx else range(3 * D // 256)
            for cg in cgs:
                w, wk = wr.next()
                s.add("pool", lambda e, w=w, cg=cg: e.dma_start(
                    out=w, in_=self.w_qkv[:, cg * 256:(cg + 1) * 256].rearrange("(k p) c -> p k c", p=128)),
                    W=[wk], dma=f"qkvw{wk[1]}")
                which = cg // (D // 256)
                if which < 2:
                    dstT = self.qT if which == 0 else self.kT
                    dname = "qT" if which == 0 else "kT"
                    p0 = t0 if which == 0 else pos0
                    for mc in range(2):
                        fr = (cg % (D // 256)) * 256 + mc * 128
                        for ts in range(0, nt, 512):
                            n = min(512, nt - ts)
                            pb = mb[mi[0] % len(mb)]; mi[0] += 1
                            pr = rb[mi[1] % len(rb)]; mi[1] += 1
                            tl = range(ts // 128, (ts + n) // 128)
                            def mm(e, w=w, mc=mc, ts=ts, n=n, pb=pb):
                                for k in range(DC):
                                    ins = e.matmul(self.psb(pb, n), w[:, k, mc * 128:(mc + 1) * 128], hT[:, k, ts:ts + n],
                                                   start=(k == 0), stop=(k == DC - 1))
                                return ins
                            s.add("pe", mm, R=[wk] + hkeys(tl), W=[("ps", pb)])
                            if _os.environ.get("KSTOP") == "mm":
                                continue
                            q_, qk = qs_r.next()
                            a_, ak = t1_r.next()
                            b_, bk2 = t2_r.next()
                            o_, ok = so_r.next()
                            s.add("act", lambda e, q_=q_, pb=pb, n=n: e.activation(q_[:, 0:n], self.psb(pb, n), AF.Identity), R=[("ps", pb)], W=[qk])
                            if _os.environ.get("KSTOP") == "rope_a":
                                continue
                            s.add("pe", lambda e, q_=q_, pr=pr, n=n: e.matmul(self.psb(pr, n), self.rot16, q_[:, 0:n], start=True, stop=True),
                                  R=[qk, K("rot16")], W=[("ps", pr)])
                            if _os.environ.get("KSTOP") == "rope_b":
                                continue
                            if _os.environ.get("KVAR") == "v1":
                                s.add("dve", lambda e, a_=a_, pb=pb, n=n, ts=ts: e.tensor_tensor(a_[:, 0:n], self.psb(pb, n), t2_r.aps[0][:, 0:n], ALU.mult),
                                      R=[("ps", pb)], W=[ak])
                            elif _os.environ.get("KVAR") == "v3":
                                s.add("act", lambda e, a_=a_, pb=pb, n=n, ts=ts: e.activation(a_[:, 0:n], self.psb(pb, n), AF.Identity),
                                      R=[("ps", pb), "cosb"], W=[ak])
                            elif _os.environ.get("KVAR") == "v5":
                                s.add("dve", lambda e, a_=a_, q_=q_, n=n, ts=ts: e.tensor_tensor(a_[:, 0:n], q_[:, 0:n], cosb[:, ts:ts + n], ALU.mult),
                                      R=[qk, "cosb"], W=[ak])
                            elif _os.environ.get("KVAR") == "v6":
                                s.add("dve", lambda e, a_=a_, pb=pb, n=n, ts=ts: e.tensor_tensor(a_[:, 0:n], self.psb(pb, n), cosb[:, ts:ts + n], ALU.mult),
                                      R=[("ps", pb), "cosb", qk], W=[ak])
                            elif _os.environ.get("KVAR") == "v2":
                                s.add("dve", lambda e, a_=a_, pb=pb, n=n, ts=ts: e.tensor_copy(a_[:, 0:n], self.psb(pb, n)),
                                      R=[("ps", pb), "cosb"], W=[ak])
                            else:
                                s.add("dve", lambda e, a_=a_, pb=pb, n=n, ts=ts: e.tensor_tensor(a_[:, 0:n], self.psb(pb, n), cosb[:, ts:ts + n], ALU.mult),
                                      R=[("ps", pb), "cosb"], W=[ak])
                            if _os.environ.get("KSTOP") == "rope_c":
                                continue
                            s.add("dve", lambda e, b_=b_, pr=pr, n=n, ts=ts: e.tensor_tensor(b_[:, 0:n], self.psb(pr, n), sinb[:, ts:ts + n], ALU.mult),
                                  R=[("ps", pr), "sinb"], W=[bk2])
                            if _os.environ.get("KSTOP") == "rope1":
                                continue
                            s.add("dve", lambda e, o_=o_, a_=a_, b_=b_, n=n: e.tensor_tensor(o_[:, 0:n], a_[:, 0:n], b_[:, 0:n], ALU.add),
                                  R=[ak, bk2], W=[ok])
                            s.add("sp", lambda e, o_=o_, dstT=dstT, fr=fr, p0=p0, ts=ts, n=n: e.dma_start(
                                out=dstT[fr:fr + 128, p0 + ts:p0 + ts + n], in_=o_[:, 0:n]),
                                R=[ok], W=[(dname, fr // 128)], dma=f"so{ok[1]}")
                else:
                    if _os.environ.get("KSTOP") in ("mm", "rope1", "nov", "rope_a", "rope_b", "rope_c"):
                        continue
                    c0 = (cg - 2 * (D // 256)) * 256
                    for i in range(ntile):
                        pb = mb[mi[0] % len(mb)]; mi[0] += 1
                        def mm(e, w=w, i=i, pb=pb):
                            for k in range(DC):
                                ins = e.matmul(self.psb(pb, 256), hT[:, k, i * 128:(i + 1) * 128], w[:, k, :],
                                               start=(k == 0), stop=(k == DC - 1))
                            return ins
                        s.add("pe", mm, R=[wk] + hkeys([i]), W=[("ps", pb)])
                        o_, ok = vo_r.next()
                        s.add("act", lambda e, o_=o_, pb=pb: e.activation(o_, self.psb(pb, 256), AF.Identity), R=[("ps", pb)], W=[ok])
                        s.add("sp", lambda e, o_=o_, i=i, c0=c0, pos0=pos0: e.dma_start(
                            out=self.vS[pos0 + i * 128:pos0 + (i + 1) * 128, c0:c0 + 256], in_=o_),
                            R=[ok], W=[("vS", c0 // 256)], dma=f"vo{ok[1]}")
        s.phase_fence()
        ar.reset()

    def attention(self, b):
        cfg, s, ar = self.cfg, self.s, self.ar
        D, T, C, DC, H, TT = cfg.D, cfg.T, cfg.C, cfg.DC, cfg.H, cfg.TT
        K = lambda n: ("pers", n)
        KT = TT // 128
        NQ = T // 128
        scale = DK ** -0.5
        PMAX = min(TT, 1024)
        kt_r = ar.ring(s, "at_k", 2, [2, TT], BF16)
        q_r = ar.ring(s, "at_q", 2, [2, T], BF16)
        v_r = ar.ring(s, "at_v", 2, [KT, DV], BF16)
        e_r = [ar.ring(s, f"at_e{c}", 2, [TT], BF16) for c in range(2)]
        tmp_r = ar.ring(s, "at_tmp", 2, [TT], BF16)
        a_r = ar.ring(s, "at_a", 2, [TT], BF16)
        aT_r = ar.ring(s, "at_aT", 2, [KT, 128], BF16)
        st_r = ar.ring(s, "at_st", 6, [16], F32)
        on_r = ar.ring(s, "at_on", 2, [DV], BF16)
        junk = ar.alloc([DV], F32)
        ot_r = ar.ring(s, "at_ot", 3, [2, 128], BF16)
        NSB = (TT + 511) // 512
        assert NSB <= 5
        sb = list(range(NSB))
        skeys = [("ps", bk) for bk in sb]
        tb = [5, 6]
        ob = 7
        for bk in range(8):
            s.mark_local(("ps", bk))
        s.mark_local(("ps", 7, 0)); s.mark_local(("ps", 7, 1))
        S = self.ps[:, 0:TT]
        ti = [0]
        oi = [0]
        GT = 6
        heads = {}

        def load_head(h):
            kt, kk = kt_r.next()
            qt, qk = q_r.next()
            vt, vk = v_r.next()
            for c in range(2):
                fr = h * 256 + c * 128
                s.add("sp", lambda e, kt=kt, c=c, fr=fr: e.dma_start(out=kt[:, c, :], in_=self.kT[fr:fr + 128, :]),
                      R=[("kT", fr // 128)], W=[kk], dma=f"atk{kk[1]}_{c}")
                s.add("sp", lambda e, qt=qt, c=c, fr=fr: e.dma_start(out=qt[:, c, :], in_=self.qT[fr:fr + 128, :]),
                      R=[("qT", fr // 128)], W=[qk], dma=f"atq{qk[1]}_{c}")
            s.add("sp", lambda e, vt=vt, h=h: e.dma_start(out=vt, in_=self.vS[:, h * DV:(h + 1) * DV].rearrange("(kt p) c -> p kt c", p=128)),
                  R=[("vS", h)], W=[vk], dma=f"atv{vk[1]}")
            heads[h] = (kt, kk, qt, qk, vt, vk)

        items = [dict(h=h, i=i) for h in range(H) for i in range(NQ)]

        def stage_S(it, c):
            h, i = it["h"], it["i"]
            if h not in heads:
                load_head(h)
            kt, kk, qt, qk, vt, vk = heads[h]
            if c == 0:
                it["st"], it["sk"] = st_r.next()
                it["es"] = []
            st, sk = it["st"], it["sk"]
            def mmS(e):
                for j0 in range(0, TT, 512):
                    n = min(512, TT - j0)
                    ins = e.matmul(S[:, j0:j0 + n], qt[:, c, i * 128:(i + 1) * 128], kt[:, c, j0:j0 + n], start=True, stop=True)
                return ins
            s.add("pe", mmS, R=[qk, kk], W=skeys)
            s.add("dve", lambda e: e.tensor_reduce(st[:, c:c + 1], S[:, 0:PMAX], AX.X, ALU.max), R=skeys, W=[sk])
            s.add("dve", lambda e: e.tensor_scalar(st[:, 2 + c:3 + c], st[:, c:c + 1], -scale, None, ALU.mult), R=[sk], W=[sk])
            ev, ek = e_r[c].next()
            it["es"].append((ev, ek))
            s.add("act", lambda e: e.activation(ev, S, AF.Exp, bias=st[:, 2 + c:3 + c], scale=scale, accum_out=st[:, 4 + c:5 + c]),
                  R=skeys + [sk], W=[ek, sk])

        def stage_comb(it):
            st, sk, es = it["st"], it["sk"], it["es"]
            s.add("dve", lambda e: e.reciprocal(st[:, 6:8], st[:, 4:6]), R=[sk], W=[sk])
            s.add("dve", lambda e: e.tensor_tensor(st[:, 8:9], st[:, 7:8], self.lamc[:, 1:2], ALU.mult), R=[sk, K("lam")], W=[sk])
            tm, tk = tmp_r.next()
            s.add("act", lambda e: e.activation(tm, es[1][0], AF.Identity, scale=st[:, 8:9]), R=[es[1][1], sk], W=[tk])
            av, ak = a_r.next()
            s.add("dve", lambda e: e.scalar_tensor_tensor(av, es[0][0], st[:, 6:7], tm, ALU.mult, ALU.add), R=[es[0][1], tk, sk], W=[ak])
            it["a"] = (av, ak)

        def stage_T(it):
            av, ak = it["a"]
            aT, aTk = aT_r.next()
            it["aT"] = (aT, aTk)
            for g0 in range(0, KT, GT):
                ng = min(GT, KT - g0)
                bk = tb[ti[0] % 2]; ti[0] += 1
                pv = self.psb16(bk)
                def tr(e, g0=g0, ng=ng, pv=pv):
                    for q in range(ng):
                        ins = e.transpose(pv[:, q * 128:(q + 1) * 128], av[:, (g0 + q) * 128:(g0 + q + 1) * 128], self.ident16)
                    return ins
                s.add("pe", tr, R=[ak, K("id16")], W=[("ps", bk)])
                s.add("act", lambda e, g0=g0, ng=ng, pv=pv: e.activation(
                    aT[:, g0:g0 + ng, :], pv[:, 0:ng * 128].rearrange("p (a t) -> p a t", a=ng), AF.Identity),
                    R=[("ps", bk)], W=[aTk])

        def stage_AV(it):
            h, i = it["h"], it["i"]
            kt, kk, qt, qk, vt, vk = heads[h]
            st, sk = it["st"], it["sk"]
            aT, aTk = it["aT"]
            oh = oi[0] % 2; oi[0] += 1
            O = self.psb(ob, DV, oh * DV)
            def mmO(e):
                for j in range(KT):
                    ins = e.matmul(O, aT[:, j, :], vt[:, j, :], start=(j == 0), stop=(j == KT - 1))
                return ins
            s.add("pe", mmO, R=[aTk, vk], W=[("ps", 7, oh)])
            it["O"] = (O, oh)

        def stage_F(it):
            h, i = it["h"], it["i"]
            st, sk = it["st"], it["sk"]
            O, oh = it["O"]
            s.add("act", lambda e: e.activation(junk, O, AF.Square, accum_out=st[:, 9:10]), R=[("ps", 7, oh)], W=[sk, ("ps", 7, oh)])
            self.rstd_ops(st[:, 9:10], st[:, 10:11], DV, RMS_EPS, R=[sk], W=[sk])
            on, onk = on_r.next()
            s.add("dve", lambda e: e.scalar_tensor_tensor(on, O, st[:, 10:11], self.gsub, ALU.mult, ALU.mult),
                  R=[("ps", 7, oh), sk, K("gsub")], W=[onk, ("ps", 7, oh)])
            bk = tb[ti[0] % 2]; ti[0] += 1
            pv = self.psb16(bk)
            def tr2(e):
                for q in range(2):
                    ins = e.transpose(pv[:, q * 128:(q + 1) * 128], on[:, q * 128:(q + 1) * 128], self.ident16)
                return ins
            s.add("pe", tr2, R=[onk, K("id16")], W=[("ps", bk)])
            ot, otk = ot_r.next()
            s.add("act", lambda e: e.activation(ot, pv[:, 0:256].rearrange("p (a t) -> p a t", a=2), AF.Identity),
                  R=[("ps", bk)], W=[otk])
            s.add("sp", lambda e: e.dma_start(out=self.oTs[i, :, 2 * h:2 * h + 2, :], in_=ot),
                  R=[otk], W=[("oTs", i)], dma=f"atot{otk[1]}")

        N = len(items)
        stage_S(items[0], 0); stage_S(items[0], 1); stage_comb(items[0])
        self.preconvert_w1(0)
        for n in range(N):
            if n + 1 < N:
                stage_S(items[n + 1], 0)
            stage_T(items[n])
            if n + 1 < N:
                stage_S(items[n + 1], 1)
                stage_comb(items[n + 1])
            stage_AV(items[n])
            stage_F(items[n])
        s.phase_fence()
        ar.reset()

    def bcast_row(self, dst, col_fn, key_w, nchunks, psbanks):
        s = self.s
        K = lambda n: ("pers", n)
        gt, gk = self._bc_g, "bc_g"
        for k0 in range(0, nchunks, 4):
            nk = min(4, nchunks - k0)
            bk = psbanks[(k0 // 4) % len(psbanks)]
            def mk(e, k0=k0, nk=nk):
                for q in range(nk):
                    ins = e.activation(gt[:, q, :], self._bc_z, AF.Identity, bias=col_fn(k0 + q), scale=1.0)
                return ins
            s.add("act", mk, R=[K("mods"), "bc_z"], W=[gk])
            def mm(e, nk=nk, bk=bk):
                for q in range(nk):
                    ins = e.matmul(self.psb(bk, 128, q * 128), gt[:, q, :], self.ident32, start=True, stop=True)
                return ins
            s.add("pe", mm, R=[gk, K("id32")], W=[("ps", bk)])
            s.add("dve", lambda e, k0=k0, nk=nk, bk=bk: e.tensor_copy(dst[:, k0 * 128:(k0 + nk) * 128], self.psb(bk, nk * 128)),
                  R=[("ps", bk)], W=[key_w])

    def preconvert_w1(self, l):
        cfg, s = self.cfg, self.s
        DC = cfg.DC
        for fi in range(cfg.DFF // 128):
            s.add("pool", lambda e, fi=fi: e.dma_start(
                out=self.w1s[fi].rearrange("p (k c) -> p k c", k=DC),
                in_=self.mlp_w1[l, :, fi * 128:(fi + 1) * 128].rearrange("(k p) c -> p k c", p=128)),
                W=[("w1s", fi), "w1s_chain"], dma="w1pc")

    def alloc_bcast(self):
        s, ar = self.s, self.ar
        self._bc_g = ar.alloc([4, 128], F32)
        self._bc_z = ar.alloc([128], F32)
        s.mark_local("bc_g"); s.mark_local("bc_z")
        s.add("dve", lambda e: e.memset(self._bc_z, 0.0), W=["bc_z"])

    def proj_tm(self, b, aTs, aname, w, bias, xin, xin_name, xout, xout_name, gate, tag):
        cfg, s, ar = self.cfg, self.s, self.ar
        D, T, DC = cfg.D, cfg.T, cfg.DC
        K = lambda n: ("pers", n)
        NT = T // 128
        l, v, bb = gate
        self.alloc_bcast()
        gbc = ar.alloc([D], F32)
        s.mark_local("gbc")
        for bk in range(8):
            s.mark_local(("ps", bk))
        self.bcast_row(gbc, lambda k: self.mod_col(l, v, bb, k), "gbc", DC, [6, 7])
        if bias is not None:
            bbc = ar.alloc([D], F32)
            s.mark_local("bbc")
            s.add("sp", lambda e: e.dma_start(out=bbc, in_=bias.partition_broadcast(128)), W=["bbc"], dma="bbc")
        CW = min(512, D)
        wr = ar.ring(s, "tmw", 2, [DC, CW], BF16)
        a_r = ar.ring(s, "tma", 4, [DC, 128], BF16)
        x_r = ar.ring(s, "tmx", 4, [CW], F32)
        o_r = ar.ring(s, "tmo", 3, [CW], F32)
        mb = [0, 1, 2, 3]
        mi = 0
        xk_of = (lambda i: xin_name) if isinstance(xin_name, tuple) else (lambda i: (xin_name, i))
        tiles = [(n0, i) for n0 in range(0, D, CW) for i in range(NT)]
        ld = {}
        wcur = {}
        def load(j):
            n0, i = tiles[j]
            if i == 0:
                wt, wk = wr.next()
                s.add("pool", lambda e, wt=wt, n0=n0: e.dma_start(out=wt, in_=w[:, n0:n0 + CW].rearrange("(k p) c -> p k c", p=128)),
                      W=[wk], dma=f"tmw{wk[1]}")
                wcur[n0] = (wt, wk)
            at, ak = a_r.next()
            s.add("sp", lambda e, at=at, i=i: e.dma_start(out=at, in_=aTs[i]), R=[(aname, i)], W=[ak], dma=f"tma{ak[1]}")
            xt, xk = x_r.next()
            s.add("sp", lambda e, xt=xt, i=i, n0=n0: e.dma_start(out=xt, in_=xin[i * 128:(i + 1) * 128, n0:n0 + CW]),
                  R=[xk_of(i)], W=[xk], dma=f"tmx{xk[1]}")
            ld[j] = (at, ak, xt, xk)
        PF = 2
        for j in range(min(PF, len(tiles))):
            load(j)
        for j, (n0, i) in enumerate(tiles):
            if j + PF < len(tiles):
                load(j + PF)
            at, ak, xt, xk = ld.pop(j)
            wt, wk = wcur[n0]
            pb = mb[mi % 4]; mi += 1
            def mm(e, at=at, wt=wt, pb=pb):
                for k in range(DC):
                    ins = e.matmul(self.psb(pb, CW), at[:, k, :], wt[:, k, :], start=(k == 0), stop=(k == DC - 1))
                return ins
            s.add("pe", mm, R=[ak, wk], W=[("ps", pb)])
            ot, ok = o_r.next()
            if bias is not None:
                s.add("dve", lambda e, ot=ot, pb=pb, n0=n0: e.tensor_tensor(ot, self.psb(pb, CW), bbc[:, n0:n0 + CW], ALU.add),
                      R=[("ps", pb), "bbc"], W=[ok])
                s.add("dve", lambda e, ot=ot, n0=n0: e.tensor_tensor(ot, ot, gbc[:, n0:n0 + CW], ALU.mult), R=[ok, "gbc"], W=[ok])
            else:
                s.add("dve", lambda e, ot=ot, pb=pb, n0=n0: e.tensor_tensor(ot, self.psb(pb, CW), gbc[:, n0:n0 + CW], ALU.mult),
                      R=[("ps", pb), "gbc"], W=[ok])
            s.add("dve", lambda e, ot=ot, xt=xt: e.tensor_tensor(ot, ot, xt, ALU.add), R=[ok, xk], W=[ok])
            s.add("sp", lambda e, ot=ot, i=i, n0=n0: e.dma_start(out=xout[i * 128:(i + 1) * 128, n0:n0 + CW], in_=ot),
                  R=[ok], W=[(xout_name, i)], dma=f"tmo{ok[1]}")
        s.phase_fence()
        ar.reset()

    def mlp(self, b, l, xin, xin_name, xout, xout_name):
        cfg, s, ar = self.cfg, self.s, self.ar
        D, T, DC, DFF, G, MT = cfg.D, cfg.T, cfg.DC, cfg.DFF, cfg.G, cfg.MT
        K = lambda n: ("pers", n)
        NG = DFF // G
        GK = G // 128
        NS = MT // 128
        CW = min(512, D)
        self.alloc_bcast()
        gbc = ar.alloc([D], F32)
        s.mark_local("gbc")
        for bk in range(8):
            s.mark_local(("ps", bk))
        self.bcast_row(gbc, lambda k: self.mod_col(l, 5, b, k), "gbc", DC, [6, 7])
        hT = ar.alloc([DC, MT], BF16)
        pro, NPG = self.make_prologue([0, 1], nbuf=1)
        hid_r = ar.ring(s, "hid", 2, [GK, MT], BF16)
        acc = ar.alloc([NS, D], F32)
        w1_r = ar.ring(s, "w1", 2, [DC, 128], BF16)
        sq_r = ar.ring(s, "msq", 2, [MT], F32)
        w2_r = ar.ring(s, "w2", 2, [GK, CW], BF16)
        for i in range(NS):
            for g in range(NPG):
                s.mark_local(("hT", i, g))
            for n in range(D // CW):
                s.mark_local(("acc", i, n))
        hkeys = [("hT", i, g) for i in range(NS) for g in range(NPG)]
        mb1 = [2, 3]
        mb2 = [4, 5, 6, 7]
        m1 = [0]; m2 = [0]
        w1 = self.mlp_w1
        w2 = self.mlp_w2
        xt_r = self.pr_xring
        for tt in range(T // MT):
            tok0 = tt * MT
            for i in range(NS):
                pro(xin[tok0 + i * 128:tok0 + (i + 1) * 128, :], (xin_name, (tok0 // 128) + i),
                    lambda k, i=i: hT[:, k, i * 128:(i + 1) * 128],
                    lambda g, i=i: ("hT", i, g),
                    lambda k: self.A_col(l, 1, b, k),
                    lambda k: self.mod_col(l, 3, b, k))
            def m1_group(g):
                hd, hk = hid_r.next()
                for fc in range(GK):
                    f0 = g * G + fc * 128
                    wt, wk = w1_r.next()
                    fi = f0 // 128
                    if True:
                        s.add("sp", lambda e, wt=wt, fi=fi: e.dma_start(out=wt.rearrange("p k c -> p (k c)"), in_=self.w1s[fi]),
                              R=[("w1s", fi)], W=[wk], dma=f"w1h_{wk[1]}")
                    pb = mb1[m1[0] % 2]; m1[0] += 1
                    def mm(e, wt=wt, pb=pb):
                        for k in range(DC):
                            ins = e.matmul(self.psb(pb, MT), wt[:, k, :], hT[:, k, :], start=(k == 0), stop=(k == DC - 1))
                        return ins
                    s.add("pe", mm, R=[wk] + hkeys, W=[("ps", pb)])
                    sq, sqk = sq_r.next()
                    s.add("act", lambda e, sq=sq, pb=pb: e.activation(sq, self.psb(pb, MT), AF.Square), R=[("ps", pb)], W=[sqk])
                    s.add("dve", lambda e, hd=hd, fc=fc, pb=pb, sq=sq: e.scalar_tensor_tensor(hd[:, fc, :], self.psb(pb, MT), 0.0, sq, ALU.is_gt, ALU.mult),
                          R=[("ps", pb), sqk], W=[hk])
                return hd, hk
            def m2_group(g, hd, hk):
                for n in range(D // CW):
                    wt, wk = w2_r.next()
                    wi = g * (D // CW) + n
                    if tt == 0:
                        s.add("pool", lambda e, wt=wt, g=g, n=n: e.dma_start(
                            out=wt, in_=w2[l, g * G:(g + 1) * G, n * CW:(n + 1) * CW].rearrange("(k p) c -> p k c", p=128)),
                            W=[wk], dma=f"w2_{wk[1]}")
                        if T // MT > 1:
                            s.add("sp", lambda e, wt=wt, wi=wi: e.dma_start(out=self.w2s[wi], in_=wt.rearrange("p k c -> p (k c)")),
                                  R=[wk], W=[("w2s", wi)], dma=f"w2st{wk[1]}")
                    else:
                        s.add("sp", lambda e, wt=wt, wi=wi: e.dma_start(out=wt.rearrange("p k c -> p (k c)"), in_=self.w2s[wi]),
                              R=[("w2s", wi)], W=[wk], dma=f"w2h_{wk[1]}")
                    for i in range(NS):
                        pb = mb2[m2[0] % 4]; m2[0] += 1
                        def mm(e, wt=wt, hd=hd, i=i, pb=pb):
                            for k in range(GK):
                                ins = e.matmul(self.psb(pb, CW), hd[:, k, i * 128:(i + 1) * 128], wt[:, k, :], start=(k == 0), stop=(k == GK - 1))
                            return ins
                        s.add("pe", mm, R=[wk, hk], W=[("ps", pb)])
                        dst = acc[:, i, n * CW:(n + 1) * CW]
                        if g == 0:
                            s.add("act", lambda e, dst=dst, pb=pb: e.activation(dst, self.psb(pb, CW), AF.Identity), R=[("ps", pb)], W=[("acc", i, n)])
                        else:
                            s.add("dve", lambda e, dst=dst, pb=pb: e.tensor_tensor(dst, dst, self.psb(pb, CW), ALU.add),
                                  R=[("ps", pb), ("acc", i, n)], W=[("acc", i, n)])
            prev = m1_group(0)
            for g in range(NG):
                nxt = m1_group(g + 1) if g + 1 < NG else None
                m2_group(g, *prev)
                prev = nxt
            for i in range(NS):
                ti = tok0 // 128 + i
                xt, xk = xt_r.next()
                s.add("sp", lambda e, xt=xt, ti=ti: e.dma_start(out=xt, in_=xin[ti * 128:(ti + 1) * 128, :]), R=[(xin_name, ti)], W=[xk], dma="ep_x")
                akeys = [("acc", i, n) for n in range(D // CW)]
                s.add("dve", lambda e, i=i: e.tensor_tensor(acc[:, i, :], acc[:, i, :], gbc, ALU.mult), R=akeys + ["gbc"], W=akeys)
                s.add("dve", lambda e, i=i, xt=xt: e.tensor_tensor(acc[:, i, :], acc[:, i, :], xt, ALU.add), R=akeys + [xk], W=akeys)
                s.add("sp", lambda e, i=i, ti=ti: e.dma_start(out=xout[ti * 128:(ti + 1) * 128, :], in_=acc[:, i, :]),
                      R=akeys, W=[(xout_name, ti)] + akeys, dma=f"ep_o{i}")
        s.phase_fence()
        ar.reset()

    def conv_block(self, b):
        cfg, s, ar = self.cfg, self.s, self.ar
        D, T, DC, TB = cfg.D, cfg.T, cfg.DC, cfg.TB
        K = lambda n: ("pers", n)
        cn = self.colnames
        hT = ar.alloc([DC, TB], BF16)
        pro, NG = self.make_prologue([0, 1])
        wr = ar.ring(s, "pw1w", 4, [DC, 128], BF16)
        sg_r = ar.ring(s, "sg", 2, [512], F32)
        u_r = ar.ring(s, "uo", 3, [512], F32)
        mb = [2, 3, 4, 5, 6, 7]
        for bk in mb:
            s.mark_local(("ps", bk))
        mi = 0
        for t0 in range(0, T, TB):
            nt = min(TB, T - t0)
            ntile = nt // 128
            for g in range(NG):
                for i in range(ntile):
                    s.mark_local(("hT", i, g))
            for i in range(ntile):
                pro(self.x2[t0 + i * 128:t0 + (i + 1) * 128, :], ("x2", t0 // 128 + i),
                    lambda k, i=i: hT[:, k, i * 128:(i + 1) * 128],
                    lambda g, i=i: ("hT", i, g),
                    lambda k: self.A_col(1, 0, b, k),
                    lambda k: self.mod_col(1, 0, b, k))
            hkeys = lambda tl: [("hT", i, g) for i in tl for g in range(NG)]
            for fc in range(DC):
                wa, wak = wr.next()
                wg, wgk = wr.next()
                s.add("pool", lambda e, wa=wa, fc=fc: e.dma_start(out=wa, in_=self.pw1_w[:, fc * 128:(fc + 1) * 128].rearrange("(k p) c -> p k c", p=128)),
                      W=[wak], dma=f"pw1w{wak[1]}")
                s.add("pool", lambda e, wg=wg, fc=fc: e.dma_start(out=wg, in_=self.pw1_w[:, D + fc * 128:D + (fc + 1) * 128].rearrange("(k p) c -> p k c", p=128)),
                      W=[wgk], dma=f"pw1w{wgk[1]}")
                for ts in range(0, nt, 512):
                    n = min(512, nt - ts)
                    tl = range(ts // 128, (ts + n) // 128)
                    pa = mb[mi % 6]; mi += 1
                    pg = mb[mi % 6]; mi += 1
                    for (wt, wk, pb) in ((wa, wak, pa), (wg, wgk, pg)):
                        def mm(e, wt=wt, pb=pb, ts=ts, n=n):
                            for k in range(DC):
                                ins = e.matmul(self.psb(pb, n), wt[:, k, :], hT[:, k, ts:ts + n], start=(k == 0), stop=(k == DC - 1))
                            return ins
                        s.add("pe", mm, R=[wk] + hkeys(tl), W=[("ps", pb)])
                    sg, sgk = sg_r.next()
                    uo, uk = u_r.next()
                    s.add("act", lambda e, sg=sg, pg=pg, n=n, fc=fc: e.activation(sg[:, 0:n], self.psb(pg, n), AF.Sigmoid, bias=self.colvec(cn[("pw1b", 1)], fc), scale=1.0),
                          R=[("ps", pg), K("cols")], W=[sgk])
                    s.add("dve", lambda e, uo=uo, sg=sg, pa=pa, n=n, fc=fc: e.scalar_tensor_tensor(uo[:, 0:n], self.psb(pa, n), self.colvec(cn[("pw1b", 0)], fc), sg[:, 0:n], ALU.add, ALU.mult),
                          R=[("ps", pa), sgk, K("cols")], W=[uk])
                    s.add("sp", lambda e, uo=uo, fc=fc, t0=t0, ts=ts, n=n: e.dma_start(out=self.uT[fc * 128:(fc + 1) * 128, t0 + ts:t0 + ts + n], in_=uo[:, 0:n]),
                          R=[uk], W=[("uT", fc)], dma=f"uo{uk[1]}")
        s.phase_fence()
        ar.reset()
        ub_r = ar.ring(s, "cv_u", 2, [T + 2 * CONV_PAD], F32)
        ac_r = ar.ring(s, "cv_a", 2, [T], F32)
        a2_r = ar.ring(s, "cv_a2", 2, [T], F32)
        cb_r = ar.ring(s, "cv_c", 2, [T], BF16)
        sq_r = ar.ring(s, "cv_s", 2, [T], BF16)
        for bk in range(8):
            s.mark_local(("ps", bk))
        NTB = (T + 511) // 512
        assert 2 * NTB <= 8
        for j in range(2):
            ub, ubk = ub_r.next()
            s.add("dve", lambda e, ub=ub: e.memset(ub, 0.0), W=[ubk])
        self.preconvert_w1(1)
        for fc in range(DC):
            ub, ubk = ub_r.next()
            s.add("sp", lambda e, ub=ub, fc=fc: e.dma_start(out=ub[:, CONV_PAD:CONV_PAD + T], in_=self.uT[fc * 128:(fc + 1) * 128, :]),
                  R=[("uT", fc)], W=[ubk], dma=f"cvu{ubk[1]}")
            ac, ack = ac_r.next()
            a2, a2k = a2_r.next()
            s.add("dve", lambda e, ub=ub, ac=ac, fc=fc: e.tensor_scalar(ac, ub[:, 0:T], self.colvec(cn[("dw", 0)], fc), self.colvec(cn["dwb"], fc), ALU.mult, ALU.add),
                  R=[ubk, K("cols")], W=[ack])
            s.add("dve", lambda e, ub=ub, a2=a2, fc=fc: e.tensor_scalar(a2, ub[:, 1:1 + T], self.colvec(cn[("dw", 1)], fc), None, ALU.mult),
                  R=[ubk, K("cols")], W=[a2k])
            for j in range(2, CONV_W):
                tgt, tgk = (ac, ack) if j % 2 == 0 else (a2, a2k)
                s.add("dve", lambda e, ub=ub, tgt=tgt, fc=fc, j=j: e.scalar_tensor_tensor(tgt, ub[:, j:j + T], self.colvec(cn[("dw", j)], fc), tgt, ALU.mult, ALU.add),
                      R=[ubk, K("cols"), tgk], W=[tgk])
            s.add("dve", lambda e, ac=ac, a2=a2: e.tensor_tensor(ac, ac, a2, ALU.add), R=[ack, a2k], W=[ack])
            cb, cbk = cb_r.next()
            sq, sqk = sq_r.next()
            s.add("act", lambda e, cb=cb, ac=ac: e.activation(cb, ac, AF.Identity), R=[ack], W=[cbk])
            s.add("act", lambda e, sq=sq, cb=cb: e.activation(sq, cb, AF.Square), R=[cbk], W=[sqk])
            def st(e, cb=cb, sq=sq, fc=fc):
                for tb in range(NTB):
                    n = min(512, T - tb * 512)
                    e.matmul(self.psb(tb, n), self.ones16, cb[:, tb * 512:tb * 512 + n], start=(fc == 0), stop=(fc == DC - 1))
                    ins = e.matmul(self.psb(NTB + tb, n), self.ones16, sq[:, tb * 512:tb * 512 + n], start=(fc == 0), stop=(fc == DC - 1))
                return ins
            s.add("pe", st, R=[cbk, sqk, K("ones16")], W=[("ps", bk) for bk in range(2 * NTB)])
            s.add("sp", lambda e, cb=cb, fc=fc: e.dma_start(out=self.cT[fc * 128:(fc + 1) * 128, :], in_=cb), R=[cbk], W=[("cT", fc)], dma=f"cvc{cbk[1]}")
        mean = ar.alloc([T], F32)
        rstd = ar.alloc([T], F32)
        s.mark_local("lnstat")
        tmpm = ar.alloc([T], F32)
        S1 = self.ps[:, 0:T]
        S2 = self.ps[:, NTB * 512:NTB * 512 + T]
        pk = [("ps", bk) for bk in range(2 * NTB)]
        s.mark_local("tmpm")
        s.add("dve", lambda e: e.tensor_scalar(mean, S1, 1.0 / D, None, ALU.mult), R=pk, W=["lnstat"])
        s.add("dve", lambda e: e.tensor_tensor(tmpm, mean, mean, ALU.mult), R=["lnstat"], W=["tmpm"])
        s.add("dve", lambda e: e.scalar_tensor_tensor(rstd, S2, 1.0 / D, tmpm, ALU.mult, ALU.subtract), R=pk + ["tmpm"], W=["lnstat"])
        s.add("dve", lambda e: e.tensor_scalar(rstd, rstd, LN_EPS, None, ALU.add), R=["lnstat"], W=["lnstat"])
        s.add("act", lambda e: e.activation(rstd, rstd, AF.Sqrt), R=["lnstat"], W=["lnstat"])
        s.add("dve", lambda e: e.reciprocal(rstd, rstd), R=["lnstat"], W=["lnstat"])
        cl_r = ar.ring(s, "ln_c", 2, [T], BF16)
        t_r = ar.ring(s, "ln_t", 2, [T], F32)
        v_r = ar.ring(s, "ln_v", 2, [T], BF16)
        for fc in range(DC):
            cl, clk = cl_r.next()
            s.add("sp", lambda e, cl=cl, fc=fc: e.dma_start(out=cl, in_=self.cT[fc * 128:(fc + 1) * 128, :]), R=[("cT", fc)], W=[clk], dma=f"lnc{clk[1]}")
            tt, tk = t_r.next()
            s.add("dve", lambda e, tt=tt, cl=cl: e.tensor_tensor(tt, cl, mean, ALU.subtract), R=[clk, "lnstat"], W=[tk])
            s.add("dve", lambda e, tt=tt: e.tensor_tensor(tt, tt, rstd, ALU.mult), R=[tk, "lnstat"], W=[tk])
            vv, vk = v_r.next()
            s.add("act", lambda e, vv=vv, tt=tt, fc=fc: e.activation(vv, tt, AF.Silu, bias=self.colvec(cn["lnb"], fc), scale=self.colvec(cn["lng"], fc)),
                  R=[tk, K("cols")], W=[vk])
            s.add("sp", lambda e, vv=vv, fc=fc: e.dma_start(out=self.vTs[:, :, fc, :].rearrange("i p t -> p i t"), in_=vv.rearrange("p (i t) -> p i t", t=128)),
                  R=[vk], W=[("vTs", i) for i in range(T // 128)], dma=f"lnv{vk[1]}")
        s.phase_fence()
        ar.reset()

    def final_norm(self, b):
        cfg, s, ar = self.cfg, self.s, self.ar
        D, T = cfg.D, cfg.T
        gf = ar.alloc([D], F32)
        s.mark_local("gf")
        s.add("sp", lambda e: e.dma_start(out=gf, in_=self.norm_final_g.partition_broadcast(128)), W=["gf"], dma="gf")
        x_r = ar.ring(s, "fn_x", 3, [D], F32)
        junk = ar.alloc([D], BF16)
        st_r = ar.ring(s, "fn_s", 3, [2], F32)
        for i in range(T // 128):
            xt, xk = x_r.next()
            st, sk = st_r.next()
            s.add("sp", lambda e, xt=xt, i=i: e.dma_start(out=xt, in_=self.x4[i * 128:(i + 1) * 128, :]), R=[("x4", i)], W=[xk], dma=f"fnx{xk[1]}")
            s.add("act", lambda e, xt=xt, st=st: e.activation(junk, xt, AF.Square, accum_out=st[:, 0:1]), R=[xk], W=[sk])
            self.rstd_ops(st[:, 0:1], st[:, 1:2], D, RMS_EPS, R=[sk], W=[sk])
            s.add("dve", lambda e, xt=xt, st=st: e.scalar_tensor_tensor(xt, xt, st[:, 1:2], gf, ALU.mult, ALU.mult), R=[xk, sk, "gf"], W=[xk])
            s.add("sp", lambda e, xt=xt, i=i: e.dma_start(out=self.out[b, i * 128:(i + 1) * 128, :], in_=xt), R=[xk], W=[("out", b, i)], dma=f"fno{xk[1]}")
        s.phase_fence()
        ar.reset()


def rope_tables(cfg):
    T, C, GW = cfg.T, cfg.C, cfg.GRID_W
    rows = T // GW
    row = np.repeat(np.arange(rows, dtype=np.float32), GW)
    col = np.tile(np.arange(GW, dtype=np.float32), rows)
    n = 32
    inv = (np.float32(ROPE_BASE) ** (-np.arange(n, dtype=np.float32) / np.float32(n))).astype(np.float32)
    ar_ = row[:, None] * inv
    ac_ = col[:, None] * inv
    ang = np.concatenate([ar_, ar_, ac_, ac_], axis=-1).astype(np.float32)
    cos = np.ones((128, C + T), np.float32)
    sin = np.zeros((128, C + T), np.float32)
    cos[:, C:] = np.cos(ang).T
    sin[:, C:] = np.sin(ang).T
    return cos, sin


def rot_matrix():
    R = np.zeros((128, 128), np.float32)
    for m in range(128):
        blk = (m // 32) % 2
        if blk == 0:
            R[m + 32, m] = -1.0
        else:
            R[m - 32, m] = 1.0
    return R


def make_in_maps(cfg, ncores, inp):
    NB = cfg.NB
    cos, sin = rope_tables(cfg)
    common = dict(
        ada_w=inp["ada_w"], ada_b=inp["ada_b"], norm_mix_g=inp["norm_mix_g"], norm_mlp_g=inp["norm_mlp_g"],
        norm_final_g=inp["norm_final_g"].reshape(1, -1), attn_w_qkv=inp["attn_w_qkv"][0], attn_w_o=inp["attn_w_o"][0],
        lam_in=np.concatenate([inp["lambda_q1"], inp["lambda_k1"], inp["lambda_q2"], inp["lambda_k2"]], 0),
        attn_subln_g=inp["attn_subln_g"], conv_pw1_w=inp["conv_pw1_w"][0], conv_pw1_b=inp["conv_pw1_b"][0].reshape(2, -1),
        conv_dw_w=inp["conv_dw_w"][0], conv_dw_b=inp["conv_dw_b"], conv_ln_g=inp["conv_ln_g"], conv_ln_b=inp["conv_ln_b"],
        conv_pw2_w=inp["conv_pw2_w"][0], conv_pw2_b=inp["conv_pw2_b"], mlp_w1=inp["mlp_w1"], mlp_w2=inp["mlp_w2"],
        rope_cos=cos, rope_sin=sin, rot_m=rot_matrix(), ident=np.eye(128, dtype=np.float32),
    )
    common = {k: np.ascontiguousarray(np.asarray(v, dtype=np.float32)) for k, v in common.items()}
    maps = []
    for c in range(ncores):
        sl = slice(c * NB, (c + 1) * NB)
        m = dict(common)
        m["x"] = np.ascontiguousarray(inp["x"][sl])
        m["ctx"] = np.ascontiguousarray(inp["ctx"][sl])
        m["cvec"] = np.ascontiguousarray(np.concatenate([inp["c"][sl], inp["c_ctx"].reshape(1, -1)], 0))
        maps.append(m)
    return maps


NCORES = 8


def kernel(**inputs):
    inp = {k: np.asarray(v) for k, v in inputs.items()}
    B = inp["x"].shape[0]
    cfg = Cfg(D=inp["x"].shape[2], T=inp["x"].shape[1], C=inp["ctx"].shape[1], DFF=inp["mlp_w1"].shape[2], NB=B // NCORES)
    bld = Builder(cfg)
    nc = bld.build()
    maps = make_in_maps(cfg, NCORES, inp)
    res = run_bass_kernel_spmd(nc, maps, core_ids=list(range(NCORES)))
    outs = [np.asarray(r["out"]) for r in res.results]
    return np.concatenate(outs, axis=0).astype(np.float32)
```

```python
import math
from contextlib import ExitStack

import numpy as np
import concourse.bass as bass
import concourse.mybir as mybir
from concourse.bass_utils import run_bass_kernel_spmd

F32 = mybir.dt.float32
BF16 = mybir.dt.bfloat16
AF = mybir.ActivationFunctionType
ALU = mybir.AluOpType
AX = mybir.AxisListType

RMS_EPS = 1e-6
LN_EPS = 1e-5
CONV_W = 31
CONV_PAD = 15
ROPE_BASE = 10000.0
DK = 128
DV = 256


class Cfg:
    def __init__(self, D=4096, T=2048, C=256, DFF=16384, NB=1, GRID_W=64, debug=(), nphase=9):
        self.D, self.T, self.C, self.DFF, self.NB, self.GRID_W = D, T, C, DFF, NB, GRID_W
        self.DC = D // 128
        self.H = D // (2 * DK)
        self.TT = C + T
        self.NBX = NB + 1
        self.FC = DFF // 128
        self.debug = tuple(debug)
        self.nphase = nphase
        self.G = min(1024, DFF)
        self.MT = min(512, T)
        self.TB = min(1024, T)


class Sched:
    STREAMS = ("pe", "act", "dve", "pool", "sp")

    def __init__(self):
        self.ops = []
        self.lastw = {}
        self.readers = {}
        self.seen = set()
        self.fence = {}
        self.local = set()

    def add(self, stream, emit, R=(), W=(), dma=None, inc=16):
        i = len(self.ops)
        W = tuple(W) + tuple(k for k in R if isinstance(k, tuple) and k[0] == "ps")
        deps = set()
        for k in tuple(R) + tuple(W):
            if k not in self.seen:
                self.seen.add(k)
                deps.update(self.fence.values())
            w = self.lastw.get(k)
            if w is not None:
                deps.add(w)
        for k in W:
            deps.update(self.readers.get(k, ()))
        for k in R:
            self.readers.setdefault(k, []).append(i)
        for k in W:
            self.lastw[k] = i
            self.readers[k] = []
        deps.discard(i)
        self.ops.append(dict(stream=stream, emit=emit, deps=deps, dma=dma, inc=inc))
        return i

    def mark_local(self, key):
        self.local.add(key)

    def phase_fence(self):
        for k in list(self.local):
            ids = []
            if k in self.lastw:
                ids.append(self.lastw.pop(k))
            ids.extend(self.readers.pop(k, ()))
            self.seen.discard(k)
            for i in ids:
                o = self.ops[i]
                fk = ("dma", o["dma"]) if o["dma"] is not None else ("eng", o["stream"])
                if self.fence.get(fk, -1) < i:
                    self.fence[fk] = i
        self.local = set()

    def finalize(self):
        ops = self.ops
        for o in ops:
            o["signal"] = o["dma"] is not None
        for o in ops:
            for d in o["deps"]:
                p = ops[d]
                if p["dma"] is None:
                    if p["stream"] == "pe" and o["stream"] == "pe" and o["dma"] is None:
                        continue
                    p["signal"] = True
        cnt = {}
        for o in ops:
            key = ("dma", o["dma"]) if o["dma"] is not None else ("eng", o["stream"])
            o["semkey"] = key
            if o["signal"]:
                cnt[key] = cnt.get(key, 0) + (o["inc"] if o["dma"] is not None else 1)
                o["val"] = cnt[key]
        self.semkeys = sorted(cnt.keys(), key=str)
        self.by_stream = {s: [o for o in ops if o["stream"] == s] for s in self.STREAMS}

    def emit_stream(self, stream, eng, sems):
        waited = {}
        ops = self.ops
        for o in self.by_stream[stream]:
            need = {}
            for d in o["deps"]:
                p = ops[d]
                if not p["signal"]:
                    continue
                if p["dma"] is None and p["stream"] == "pe" and stream == "pe" and o["dma"] is None:
                    continue
                k = p["semkey"]
                if need.get(k, 0) < p["val"]:
                    need[k] = p["val"]
            for k, v in need.items():
                if waited.get(k, 0) >= v:
                    continue
                eng.wait_ge(sems[k], v)
                waited[k] = v
            ins = o["emit"](eng) if o["emit"] is not None else None
            if o["signal"]:
                assert ins is not None
                ins.then_inc(sems[o["semkey"]], o["inc"] if o["dma"] is not None else 1)


class Ring:
    def __init__(self, sched, name, aps):
        self.name, self.aps, self.i = name, aps, 0
        for j in range(len(aps)):
            sched.mark_local((name, j))

    def next(self):
        j = self.i % len(self.aps)
        self.i += 1
        return self.aps[j], (self.name, j)


class Arena:
    def __init__(self, ap, words):
        self.ap, self.words, self.off = ap, words, 0

    def reset(self):
        self.off = 0

    def alloc(self, shape, dtype):
        n = 1
        for s in shape:
            n *= s
        words = n if dtype == F32 else (n + 1) // 2
        words = (words + 15) // 16 * 16
        assert self.off + words <= self.words, ("SBUF arena overflow", self.off, words, self.words)
        v = self.ap[:, self.off:self.off + words]
        self.off += words
        if dtype == BF16:
            v = v.bitcast(BF16)
        v = v[:, 0:n]
        if len(shape) == 2:
            v = v.rearrange("p (a b) -> p a b", a=shape[0])
        elif len(shape) == 3:
            v = v.rearrange("p (a b c) -> p a b c", a=shape[0], b=shape[1])
        return v

    def ring(self, sched, name, n, shape, dtype):
        return Ring(sched, name, [self.alloc(shape, dtype) for _ in range(n)])


class Builder:
    def __init__(self, cfg):
        self.cfg = cfg
        self.nc = bass.Bass("TRN2", target_bir_lowering=False)
        self.s = Sched()
        self.dram = {}

    def din(self, name, shape, dtype=F32):
        t = self.nc.dram_tensor(name, list(shape), dtype, kind="ExternalInput").ap()
        self.dram[name] = t
        return t

    def dscratch(self, name, shape, dtype):
        kind = "ExternalOutput" if name in self.cfg.debug else "Internal"
        t = self.nc.dram_tensor(name, list(shape), dtype, kind=kind).ap()
        self.dram[name] = t
        return t

    def psb(self, b, n=512, off=0):
        return self.ps[:, b * 512 + off:b * 512 + off + n]

    def psb16(self, b):
        return self.ps[:, b * 512:(b + 1) * 512].bitcast(BF16)

    def rstd_ops(self, ss, out, n, eps, R, W):
        self.s.add("dve", lambda e: e.tensor_scalar(out, ss, 1.0 / n, eps, ALU.mult, ALU.add), R=R, W=W)
        self.s.add("act", lambda e: e.activation(out, out, AF.Sqrt), R=W, W=W)
        self.s.add("dve", lambda e: e.reciprocal(out, out), R=W, W=W)

    def build(self):
        cfg, nc, s = self.cfg, self.nc, self.s
        D, T, C, DFF, NB, DC, H, TT, NBX = cfg.D, cfg.T, cfg.C, cfg.DFF, cfg.NB, cfg.DC, cfg.H, cfg.TT, cfg.NBX
        x = self.din("x", [NB, T, D])
        ctx = self.din("ctx", [NB, C, D])
        cvec = self.din("cvec", [NBX, D])
        ada_w = self.din("ada_w", [2, D, 6 * D])
        ada_b = self.din("ada_b", [2, 6 * D])
        norm_mix_g = self.din("norm_mix_g", [2, D])
        norm_mlp_g = self.din("norm_mlp_g", [2, D])
        norm_final_g = self.din("norm_final_g", [1, D])
        w_qkv = self.din("attn_w_qkv", [D, 3 * D])
        w_o = self.din("attn_w_o", [D, D])
        lam_in = self.din("lam_in", [4, DK])
        subln_g = self.din("attn_subln_g", [1, DV])
        pw1_w = self.din("conv_pw1_w", [D, 2 * D])
        pw1_b = self.din("conv_pw1_b", [2, D])
        dw_w = self.din("conv_dw_w", [CONV_W, D])
        dw_b = self.din("conv_dw_b", [1, D])
        ln_g = self.din("conv_ln_g", [1, D])
        ln_b = self.din("conv_ln_b", [1, D])
        pw2_w = self.din("conv_pw2_w", [D, D])
        pw2_b = self.din("conv_pw2_b", [1, D])
        mlp_w1 = self.din("mlp_w1", [2, D, DFF])
        mlp_w2 = self.din("mlp_w2", [2, DFF, D])
        rope_cos = self.din("rope_cos", [128, TT])
        rope_sin = self.din("rope_sin", [128, TT])
        rot_m = self.din("rot_m", [128, 128])
        ident_in = self.din("ident", [128, 128])
        out = self.nc.dram_tensor("out", [NB, T, D], F32, kind="ExternalOutput").ap()
        qT = self.dscratch("qT", [D, T], BF16)
        kT = self.dscratch("kT", [D, TT], BF16)
        vS = self.dscratch("vS", [TT, D], BF16)
        oTs = self.dscratch("oTs", [T // 128, 128, DC, 128], BF16)
        x1 = self.dscratch("x1", [T, D], F32)
        x2 = self.dscratch("x2", [T, D], F32)
        x3 = self.dscratch("x3", [T, D], F32)
        x4 = self.dscratch("x4", [T, D], F32)
        uT = self.dscratch("uT", [D, T], F32)
        cT = self.dscratch("cT", [D, T], BF16)
        vTs = self.dscratch("vTs", [T // 128, 128, DC, 128], BF16)
        w1s = self.dscratch("w1s", [DFF // 128, 128, DC * 128], BF16)
        w2s = self.dscratch("w2s", [(DFF // cfg.G) * (D // min(512, D)), 128, (cfg.G // 128) * min(512, D)], BF16)
        self.__dict__.update(locals())

        ARENA_WORDS = 48 * 1024
        with ExitStack() as es:
            arena_t = es.enter_context(nc.sbuf_tensor("arena", [128, ARENA_WORDS], F32))
            self.ar = Arena(arena_t[:, :], ARENA_WORDS)
            NCOLV = CONV_W + 3 + 4 + 2 + 1
            pers_words = 128 * 3 + 128 + 2 * (6 * DC * NBX) + 8 * NBX * DC + NCOLV * DC + 64 + DV + 64
            pers_t = es.enter_context(nc.sbuf_tensor("pers", [128, pers_words], F32))
            self.pers = Arena(pers_t[:, :], pers_words)
            self.ps = es.enter_context(nc.psum_tensor("ps", [128, 4096], F32))[:, :]

            NP = cfg.nphase
            self.setup_phase()
            for b in range(NB):
                if NP >= 2: self.layer0_qkv(b)
                if NP >= 3: self.attention(b)
                if NP >= 4: self.proj_tm(b, oTs, "oTs", w_o, None, x[b], ("xin", b), x1, "x1", self.gate_cols(0, 2, b), "wo")
                if NP >= 5: self.mlp(b, 0, x1, "x1", x2, "x2")
                if NP >= 6: self.conv_block(b)
                if NP >= 7: self.proj_tm(b, vTs, "vTs", pw2_w, pw2_b, x2, "x2", x3, "x3", self.gate_cols(1, 2, b), "pw2")
                if NP >= 8: self.mlp(b, 1, x3, "x3", x4, "x4")
                if NP >= 9: self.final_norm(b)
            lastdma = {}
            for i, o in enumerate(s.ops):
                if o["dma"] is not None:
                    lastdma[o["dma"]] = i
            fin = s.add("sp", None)
            s.ops[fin]["deps"].update(lastdma.values())
            s.finalize()
            sems = {}
            for k in s.semkeys:
                sems[k] = es.enter_context(nc.semaphore("s_" + "_".join(str(z) for z in (k if isinstance(k, tuple) else (k,))).replace(" ", "")[:40]))
            self.nsems = len(sems)
            with nc.Block() as block:
                @block.tensor
                def _(e):
                    s.emit_stream("pe", e, sems)

                @block.scalar
                def _(e):
                    s.emit_stream("act", e, sems)

                @block.vector
                def _(e):
                    s.emit_stream("dve", e, sems)

                @block.gpsimd
                def _(e):
                    s.emit_stream("pool", e, sems)

                @block.sync
                def _(e):
                    s.emit_stream("sp", e, sems)
        return nc

    def mod_col(self, l, v, j, k):
        return self.mods[l][:, v * self.cfg.DC + k, j:j + 1]

    def gate_cols(self, l, v, b):
        return (l, v, b)

    def colvec(self, idx, k):
        return self.cols[:, idx, k:k + 1]

    def setup_phase(self):
        cfg, s, P, ar = self.cfg, self.s, self.pers, self.ar
        D, DC, NBX, NB = cfg.D, cfg.DC, cfg.NBX, cfg.NB
        self.ident16 = P.alloc([128], BF16)
        self.ident32 = P.alloc([128], F32)
        self.rot16 = P.alloc([128], BF16)
        self.ones16 = P.alloc([128], BF16)
        self.ones32 = P.alloc([128], F32)
        self.mods = [P.alloc([6 * DC, NBX], F32) for _ in range(2)]
        self.modA = P.alloc([4 * NBX, DC], F32)
        NCOLV = CONV_W + 3 + 4 + 2 + 1
        self.cols = P.alloc([NCOLV, DC], F32)
        self.lamc = P.alloc([8], F32)
        self.gsub = P.alloc([DV], F32)
        K = lambda n: ("pers", n)
        s.add("sp", lambda e: e.dma_start(out=self.ident32, in_=self.ident_in), W=[K("id32")], dma="c_id32")
        s.add("pool", lambda e: e.dma_start(out=self.ident16, in_=self.ident_in), W=[K("id16")], dma="c_id16")
        s.add("pool", lambda e: e.dma_start(out=self.rot16, in_=self.rot_m), W=[K("rot16")], dma="c_rot")
        s.add("dve", lambda e: e.memset(self.ones16, 1.0), W=[K("ones16")])
        s.add("dve", lambda e: e.memset(self.ones32, 1.0), W=[K("ones32")])
        s.add("sp", lambda e: e.dma_start(out=self.gsub, in_=self.subln_g.partition_broadcast(128)), W=[K("gsub")], dma="c_gsub")
        lam_init0 = 0.8 - 0.6 * math.exp(-0.3 * 0)
        s.add("dve", lambda e: e.tensor_scalar(self.gsub, self.gsub, 1.0 - lam_init0, None, ALU.mult), R=[K("gsub")], W=[K("gsub")])

        VPT = max(1, 128 // DC)
        stage = ar.ring(s, "cl_stage", 2, [128], F32)
        vec_srcs = []
        def rowsrc(ap_row):
            return ap_row.rearrange("o (k p) -> (o k) p", p=128)
        colnames = {}
        ci = 0
        for j in range(CONV_W):
            vec_srcs.append((self.cols[:, ci, :], rowsrc(self.dw_w[j:j + 1, :]))); colnames[("dw", j)] = ci; ci += 1
        for nm, src in (("dwb", self.dw_b), ("lng", self.ln_g), ("lnb", self.ln_b)):
            vec_srcs.append((self.cols[:, ci, :], rowsrc(src))); colnames[nm] = ci; ci += 1
        for l in range(2):
            vec_srcs.append((self.cols[:, ci, :], rowsrc(self.norm_mix_g[l:l + 1, :]))); colnames[("gmix", l)] = ci; ci += 1
            vec_srcs.append((self.cols[:, ci, :], rowsrc(self.norm_mlp_g[l:l + 1, :]))); colnames[("gmlp", l)] = ci; ci += 1
        for h in range(2):
            vec_srcs.append((self.cols[:, ci, :], rowsrc(self.pw1_b[h:h + 1, :]))); colnames[("pw1b", h)] = ci; ci += 1
        self.colnames = colnames
        cs32 = ar.alloc([NBX, DC], F32)
        s.mark_local("cs32")
        for j in range(NBX):
            vec_srcs.append((cs32[:, j, :], rowsrc(self.cvec[j:j + 1, :])))
        for g0 in range(0, len(vec_srcs), VPT):
            grp = vec_srcs[g0:g0 + VPT]
            st, stk = stage.next()
            s.add("dve", lambda e, st=st: e.memset(st, 0.0), W=[stk])
            for i, (dst, src) in enumerate(grp):
                s.add("sp", lambda e, st=st, i=i, src=src: e.dma_start(out=st[i * DC:(i + 1) * DC, :], in_=src),
                      W=[stk], dma=f"cl{stk[1]}")
            s.mark_local(("ps", 0))
            pst = self.psb(0, 128)
            s.add("pe", lambda e, st=st, pst=pst: e.transpose(pst, st, self.ident32), R=[stk, K("id32")], W=[("ps", 0)])
            for i, (dst, src) in enumerate(grp):
                s.add("dve", lambda e, dst=dst, i=i, pst=pst: e.tensor_copy(dst, pst[:, i * DC:(i + 1) * DC]),
                      R=[("ps", 0)], W=[K("cols"), "cs32"])
        lamt = ar.alloc([8], F32)
        s.mark_local("lamt")
        s.add("sp", lambda e: e.dma_start(out=lamt[:, 0:4], in_=self.lam_in.rearrange("a d -> d a"), allow_slow_non_contiguous=True),
              W=["lamt"], dma="c_lam")
        def lam_a(e):
            e.tensor_tensor(lamt[:, 4:5], lamt[:, 0:1], lamt[:, 1:2], ALU.mult)
            return e.tensor_tensor(lamt[:, 5:6], lamt[:, 2:3], lamt[:, 3:4], ALU.mult)
        s.add("dve", lam_a, R=["lamt"], W=["lamt"])
        s.mark_local(("ps", 1))
        s.add("pe", lambda e: e.matmul(self.psb(1, 2), self.ones32, lamt[:, 4:6], start=True, stop=True),
              R=["lamt", K("ones32")], W=[("ps", 1)])
        s.add("act", lambda e: e.activation(lamt[:, 6:8], self.psb(1, 2), AF.Exp), R=[("ps", 1)], W=["lamt"])
        s.add("dve", lambda e: e.scalar_tensor_tensor(self.lamc[:, 0:1], lamt[:, 6:7], lam_init0, lamt[:, 7:8], ALU.add, ALU.subtract),
              R=["lamt"], W=[K("lam")])
        s.add("dve", lambda e: e.tensor_scalar(self.lamc[:, 1:2], self.lamc[:, 0:1], -1.0, None, ALU.mult), R=[K("lam")], W=[K("lam")])

        cs16 = ar.alloc([DC, NBX], BF16)
        s.mark_local("cs16")
        s.add("act", lambda e: e.activation(cs16.rearrange("p k j -> p j k"), cs32, AF.Silu), R=["cs32"], W=["cs16"])
        wr = ar.ring(s, "adaw", 3, [DC, 512], BF16)
        br = ar.ring(s, "adab", 3, [512], F32)
        pbank = [2, 3]
        for bk in pbank:
            s.mark_local(("ps", bk))
        it = 0
        for l in range(2):
            for cg in range(6 * D // 512):
                w, wk = wr.next()
                bt, bk_ = br.next()
                s.add("pool", lambda e, w=w, l=l, cg=cg: e.dma_start(
                    out=w, in_=self.ada_w[l, :, cg * 512:(cg + 1) * 512].rearrange("(k p) c -> p k c", p=128)),
                    W=[wk], dma=f"adaw{wk[1]}")
                s.add("sp", lambda e, bt=bt, l=l, cg=cg: e.dma_start(out=bt[0:1, :], in_=self.ada_b[l:l + 1, cg * 512:(cg + 1) * 512]),
                      W=[bk_], dma=f"adab{bk_[1]}")
                pb = pbank[it % 2]
                it += 1
                def mm(e, w=w, bt=bt, pb=pb):
                    for sub in range(4):
                        o = self.psb(pb, NBX, sub * NBX)
                        for k in range(DC):
                            e.matmul(o, w[:, k, sub * 128:(sub + 1) * 128], cs16[:, k, :], start=(k == 0), stop=False)
                        ins = e.matmul(o, bt[0:1, sub * 128:(sub + 1) * 128], self.ones32[0:1, 0:NBX], start=False, stop=True)
                    return ins
                s.add("pe", mm, R=[wk, bk_, "cs16", K("ones32")], W=[("ps", pb)])
                s.add("dve", lambda e, l=l, cg=cg, pb=pb: e.tensor_copy(
                    self.mods[l][:, cg * 4:(cg + 1) * 4, :], self.psb(pb, 4 * NBX).rearrange("p (a j) -> p a j", a=4)),
                    R=[("ps", pb)], W=[K("mods")])
        for l in range(2):
            for si, (gname, v) in enumerate((("gmix", 1), ("gmlp", 4))):
                for j in range(NBX):
                    dst = self.modA[:, (l * 2 + si) * NBX + j, :]
                    g = self.cols[:, colnames[(gname, l)], :]
                    sc = self.mods[l][:, v * DC:(v + 1) * DC, j]
                    s.add("dve", lambda e, dst=dst, g=g, sc=sc: e.scalar_tensor_tensor(dst, sc, 1.0, g, ALU.add, ALU.mult),
                          R=[K("mods"), K("cols")], W=[K("modA")])
        s.phase_fence()
        ar.reset()

    def A_col(self, l, si, j, k):
        return self.modA[:, (l * 2 + si) * self.cfg.NBX + j, k:k + 1]

    def make_prologue(self, psbanks, nbuf=2):
        cfg, s, ar = self.cfg, self.s, self.ar
        D, DC = cfg.D, cfg.DC
        xt = ar.ring(s, "pr_x", nbuf, [D], F32)
        xn = ar.ring(s, "pr_xn", nbuf, [D], BF16)
        st = ar.ring(s, "pr_st", 4, [2], F32)
        self.pr_xring = xt
        for bk in psbanks:
            s.mark_local(("ps", bk))
        state = dict(i=0)
        K = lambda n: ("pers", n)
        GS = min(8, DC)

        def run(src_ap, src_key, dst_fn, dst_key_fn, A_fn, sh_fn):
            x_, xk = xt.next()
            n_, nk = xn.next()
            t_, tk = st.next()
            s.add("sp", lambda e: e.dma_start(out=x_, in_=src_ap), R=[src_key], W=[xk], dma=f"prx{xk[1]}")
            s.add("act", lambda e: e.activation(n_, x_, AF.Square, accum_out=t_[:, 0:1]), R=[xk], W=[tk, nk])
            self.rstd_ops(t_[:, 0:1], t_[:, 1:2], D, RMS_EPS, R=[tk], W=[tk])
            s.add("dve", lambda e: e.tensor_scalar(n_, x_, t_[:, 1:2], None, ALU.mult), R=[xk, tk], W=[nk])
            for g in range(DC // GS):
                bk = psbanks[state["i"] % len(psbanks)]
                state["i"] += 1
                pv = self.psb16(bk)
                def tr(e, g=g, pv=pv):
                    for q in range(GS):
                        k = g * GS + q
                        ins = e.transpose(pv[:, q * 128:(q + 1) * 128], n_[:, k * 128:(k + 1) * 128], self.ident16)
                    return ins
                s.add("pe", tr, R=[nk, K("id16")], W=[("ps", bk)])
                def ev(e, g=g, pv=pv):
                    for q in range(GS):
                        k = g * GS + q
                        ins = e.activation(dst_fn(k), pv[:, q * 128:(q + 1) * 128], AF.Identity, bias=sh_fn(k), scale=A_fn(k))
                    return ins
                s.add("act", ev, R=[("ps", bk), K("mods"), K("modA")], W=[dst_key_fn(g)])
        return run, DC // GS

    def layer0_qkv(self, b):
        cfg, s, ar = self.cfg, self.s, self.ar
        D, T, C, DC, H, TT = cfg.D, cfg.T, cfg.C, cfg.DC, cfg.H, cfg.TT
        K = lambda n: ("pers", n)
        TB = cfg.TB
        hT = ar.alloc([DC, TB], BF16)
        pro, NG = self.make_prologue([0, 1])
        wr = ar.ring(s, "qkvw", 3, [DC, 256], BF16)
        cosb = ar.alloc([TB], F32)
        sinb = ar.alloc([TB], F32)
        s.mark_local("cosb"); s.mark_local("sinb")
        qs_r = ar.ring(s, "qs", 2, [512], BF16)
        t1_r = ar.ring(s, "t1", 2, [512], F32)
        t2_r = ar.ring(s, "t2", 2, [512], F32)
        so_r = ar.ring(s, "so", 3, [512], BF16)
        vo_r = ar.ring(s, "vo", 3, [256], BF16)
        mb = [2, 3, 4, 5]
        rb = [6, 7]
        for bk in mb + rb:
            s.mark_local(("ps", bk))
        mi = [0, 0]
        blocks = [(True, 0, C)] + [(False, t0, min(TB, T - t0)) for t0 in range(0, T, TB)]
        for (is_ctx, t0, nt) in blocks:
            pos0 = t0 if is_ctx else C + t0
            ntile = nt // 128
            for g in range(NG):
                for i in range(ntile):
                    s.mark_local(("hT", i, g))
            for i in range(ntile):
                if is_ctx:
                    src, skey, j = self.ctx[b, t0 + i * 128:t0 + (i + 1) * 128, :], ("ctxin", b), cfg.NB
                else:
                    src, skey, j = self.x[b, t0 + i * 128:t0 + (i + 1) * 128, :], ("xin", b), b
                pro(src, skey,
                    lambda k, i=i: hT[:, k, i * 128:(i + 1) * 128],
                    lambda g, i=i: ("hT", i, g),
                    lambda k, j=j: self.A_col(0, 0, j, k),
                    lambda k, j=j: self.mod_col(0, 0, j, k))
            import os as _os
            if _os.environ.get("KSTOP") == "pro":
                continue
            s.add("sp", lambda e, pos0=pos0, nt=nt: e.dma_start(out=cosb[:, 0:nt], in_=self.rope_cos[:, pos0:pos0 + nt]), W=["cosb"], dma="cosb")
            s.add("sp", lambda e, pos0=pos0, nt=nt: e.dma_start(out=sinb[:, 0:nt], in_=self.rope_sin[:, pos0:pos0 + nt]), W=["sinb"], dma="sinb")
            hkeys = lambda tl: [("hT", i, g) for i in tl for g in range(NG)]
            cgs = range(D // 256, 3 * D // 256) if is_ctx else range(3 * D // 256)
            for cg in cgs:
                w, wk = wr.next()
                s.add("pool", lambda e, w=w, cg=cg: e.dma_start(
                    out=w, in_=self.w_qkv[:, cg * 256:(cg + 1) * 256].rearrange("(k p) c -> p k c", p=128)),
                    W=[wk], dma=f"qkvw{wk[1]}")
                which = cg // (D // 256)
                if which < 2:
                    dstT = self.qT if which == 0 else self.kT
                    dname = "qT" if which == 0 else "kT"
                    p0 = t0 if which == 0 else pos0
                    for mc in range(2):
                        fr = (cg % (D // 256)) * 256 + mc * 128
                        for ts in range(0, nt, 512):
                            n = min(512, nt - ts)
                            pb = mb[mi[0] % len(mb)]; mi[0] += 1
                            pr = rb[mi[1] % len(rb)]; mi[1] += 1
                            tl = range(ts // 128, (ts + n) // 128)
                            def mm(e, w=w, mc=mc, ts=ts, n=n, pb=pb):
                                for k in range(DC):
                                    ins = e.matmul(self.psb(pb, n), w[:, k, mc * 128:(mc + 1) * 128], hT[:, k, ts:ts + n],
                                                   start=(k == 0), stop=(k == DC - 1))
                                return ins
                            s.add("pe", mm, R=[wk] + hkeys(tl), W=[("ps", pb)])
                            if _os.environ.get("KSTOP") == "mm":
                                continue
                            q_, qk = qs_r.next()
                            a_, ak = t1_r.next()
                            b_, bk2 = t2_r.next()
                            o_, ok = so_r.next()
                            s.add("act", lambda e, q_=q_, pb=pb, n=n: e.activation(q_[:, 0:n], self.psb(pb, n), AF.Identity), R=[("ps", pb)], W=[qk])
                            if _os.environ.get("KSTOP") == "rope_a":
                                continue
                            s.add("pe", lambda e, q_=q_, pr=pr, n=n: e.matmul(self.psb(pr, n), self.rot16, q_[:, 0:n], start=True, stop=True),
                                  R=[qk, K("rot16")], W=[("ps", pr)])
                            if _os.environ.get("KSTOP") == "rope_b":
                                continue
                            if _os.environ.get("KVAR") == "v1":
                                s.add("dve", lambda e, a_=a_, pb=pb, n=n, ts=ts: e.tensor_tensor(a_[:, 0:n], self.psb(pb, n), t2_r.aps[0][:, 0:n], ALU.mult),
                                      R=[("ps", pb)], W=[ak])
                            elif _os.environ.get("KVAR") == "v3":
                                s.add("act", lambda e, a_=a_, pb=pb, n=n, ts=ts: e.activation(a_[:, 0:n], self.psb(pb, n), AF.Identity),
                                      R=[("ps", pb), "cosb"], W=[ak])
                            elif _os.environ.get("KVAR") == "v5":
                                s.add("dve", lambda e, a_=a_, q_=q_, n=n, ts=ts: e.tensor_tensor(a_[:, 0:n], q_[:, 0:n], cosb[:, ts:ts + n], ALU.mult),
                                      R=[qk, "cosb"], W=[ak])
                            elif _os.environ.get("KVAR") == "v6":
                                s.add("dve", lambda e, a_=a_, pb=pb, n=n, ts=ts: e.tensor_tensor(a_[:, 0:n], self.psb(pb, n), cosb[:, ts:ts + n], ALU.mult),
                                      R=[("ps", pb), "cosb", qk], W=[ak])
                            elif _os.environ.get("KVAR") == "v2":
                                s.add("dve", lambda e, a_=a_, pb=pb, n=n, ts=ts: e.tensor_copy(a_[:, 0:n], self.psb(pb, n)),
                                      R=[("ps", pb), "cosb"], W=[ak])
                            else:
                                s.add("dve", lambda e, a_=a_, pb=pb, n=n, ts=ts: e.tensor_tensor(a_[:, 0:n], self.psb(pb, n), cosb[:, ts:ts + n], ALU.mult),
                                      R=[("ps", pb), "cosb"], W=[ak])
                            if _os.environ.get("KSTOP") == "rope_c":
                                continue
                            s.add("dve", lambda e, b_=b_, pr=pr, n=n, ts=ts: e.tensor_tensor(b_[:, 0:n], self.psb(pr, n), sinb[:, ts:ts + n], ALU.mult),
                                  R=[("ps", pr), "sinb"], W=[bk2])
                            if _os.environ.get("KSTOP") == "rope1":
                                continue
                            s.add("dve", lambda e, o_=o_, a_=a_, b_=b_, n=n: e.tensor_tensor(o_[:, 0:n], a_[:, 0:n], b_[:, 0:n], ALU.add),
                                  R=[ak, bk2], W=[ok])
                            s.add("sp", lambda e, o_=o_, dstT=dstT, fr=fr, p0=p0, ts=ts, n=n: e.dma_start(
                                out=dstT[fr:fr + 128, p0 + ts:p0 + ts + n], in_=o_[:, 0:n]),
                                R=[ok], W=[(dname, fr // 128)], dma=f"so{ok[1]}")
                else:
                    if _os.environ.get("KSTOP") in ("mm", "rope1", "nov", "rope_a", "rope_b", "rope_c"):
                        continue
                    c0 = (cg - 2 * (D // 256)) * 256
                    for i in range(ntile):
                        pb = mb[mi[0] % len(mb)]; mi[0] += 1
                        def mm(e, w=w, i=i, pb=pb):
                            for k in range(DC):
                                ins = e.matmul(self.psb(pb, 256), hT[:, k, i * 128:(i + 1) * 128], w[:, k, :],
                                               start=(k == 0), stop=(k == DC - 1))
                            return ins
                        s.add("pe", mm, R=[wk] + hkeys([i]), W=[("ps", pb)])
                        o_, ok = vo_r.next()
                        s.add("act", lambda e, o_=o_, pb=pb: e.activation(o_, self.psb(pb, 256), AF.Identity), R=[("ps", pb)], W=[ok])
                        s.add("sp", lambda e, o_=o_, i=i, c0=c0, pos0=pos0: e.dma_start(
                            out=self.vS[pos0 + i * 128:pos0 + (i + 1) * 128, c0:c0 + 256], in_=o_),
                            R=[ok], W=[("vS", c0 // 256)], dma=f"vo{ok[1]}")
        s.phase_fence()
        ar.reset()

    def attention(self, b):
        cfg, s, ar = self.cfg, self.s, self.ar
        D, T, C, DC, H, TT = cfg.D, cfg.T, cfg.C, cfg.DC, cfg.H, cfg.TT
        K = lambda n: ("pers", n)
        KT = TT // 128
        NQ = T // 128
        scale = DK ** -0.5
        PMAX = min(TT, 1024)
        kt_r = ar.ring(s, "at_k", 2, [2, TT], BF16)
        q_r = ar.ring(s, "at_q", 2, [2, T], BF16)
        v_r = ar.ring(s, "at_v", 2, [KT, DV], BF16)
        e_r = [ar.ring(s, f"at_e{c}", 2, [TT], BF16) for c in range(2)]
        tmp_r = ar.ring(s, "at_tmp", 2, [TT], BF16)
        a_r = ar.ring(s, "at_a", 2, [TT], BF16)
        aT_r = ar.ring(s, "at_aT", 2, [KT, 128], BF16)
        st_r = ar.ring(s, "at_st", 6, [16], F32)
        on_r = ar.ring(s, "at_on", 3, [DV], BF16)
        junk = ar.alloc([DV], F32)
        ot_r = ar.ring(s, "at_ot", 3, [2, 128], BF16)
        NSB = (TT + 511) // 512
        assert NSB <= 5
        sb = list(range(NSB))
        skeys = [("ps", bk) for bk in sb]
        tb = [5, 6]
        ob = 7
        for bk in range(8):
            s.mark_local(("ps", bk))
        s.mark_local(("ps", 7, 0)); s.mark_local(("ps", 7, 1))
        S = self.ps[:, 0:TT]
        ti = [0]
        oi = [0]
        GT = 6
        heads = {}
        last_on = [None]

        def load_head(h):
            kt, kk = kt_r.next()
            qt, qk = q_r.next()
            vt, vk = v_r.next()
            for c in range(2):
                fr = h * 256 + c * 128
                s.add("sp", lambda e, kt=kt, c=c, fr=fr: e.dma_start(out=kt[:, c, :], in_=self.kT[fr:fr + 128, :]),
                      R=[("kT", fr // 128)], W=[kk], dma=f"atk{kk[1]}_{c}")
                s.add("sp", lambda e, qt=qt, c=c, fr=fr: e.dma_start(out=qt[:, c, :], in_=self.qT[fr:fr + 128, :]),
                      R=[("qT", fr // 128)], W=[qk], dma=f"atq{qk[1]}_{c}")
            s.add("sp", lambda e, vt=vt, h=h: e.dma_start(out=vt, in_=self.vS[:, h * DV:(h + 1) * DV].rearrange("(kt p) c -> p kt c", p=128)),
                  R=[("vS", h)], W=[vk], dma=f"atv{vk[1]}")
            heads[h] = (kt, kk, qt, qk, vt, vk)

        items = [dict(h=h, i=i) for h in range(H) for i in range(NQ)]

        def stage_S(it, c):
            h, i = it["h"], it["i"]
            if h not in heads:
                load_head(h)
            kt, kk, qt, qk, vt, vk = heads[h]
            if c == 0:
                it["st"], it["sk"] = st_r.next()
                it["es"] = []
            st, sk = it["st"], it["sk"]
            def mmS(e):
                for j0 in range(0, TT, 512):
                    n = min(512, TT - j0)
                    ins = e.matmul(S[:, j0:j0 + n], qt[:, c, i * 128:(i + 1) * 128], kt[:, c, j0:j0 + n], start=True, stop=True)
                return ins
            s.add("pe", mmS, R=[qk, kk], W=skeys)
            s.add("dve", lambda e: e.tensor_reduce(st[:, c:c + 1], S[:, 0:PMAX], AX.X, ALU.max), R=skeys, W=[sk])
            s.add("dve", lambda e: e.tensor_scalar(st[:, 2 + c:3 + c], st[:, c:c + 1], -scale, None, ALU.mult), R=[sk], W=[sk])
            ev, ek = e_r[c].next()
            it["es"].append((ev, ek))
            s.add("act", lambda e: e.activation(ev, S, AF.Exp, bias=st[:, 2 + c:3 + c], scale=scale, accum_out=st[:, 4 + c:5 + c]),
                  R=skeys + [sk], W=[ek, sk])

        def stage_comb(it):
            st, sk, es = it["st"], it["sk"], it["es"]
            s.add("dve", lambda e: e.reciprocal(st[:, 6:8], st[:, 4:6]), R=[sk], W=[sk])
            s.add("dve", lambda e: e.tensor_tensor(st[:, 8:9], st[:, 7:8], self.lamc[:, 1:2], ALU.mult), R=[sk, K("lam")], W=[sk])
            tm, tk = tmp_r.next()
            s.add("act", lambda e: e.activation(tm, es[1][0], AF.Identity, scale=st[:, 8:9]), R=[es[1][1], sk], W=[tk])
            av, ak = a_r.next()
            s.add("dve", lambda e: e.scalar_tensor_tensor(av, es[0][0], st[:, 6:7], tm, ALU.mult, ALU.add), R=[es[0][1], tk, sk], W=[ak])
            it["a"] = (av, ak)

        def stage_T(it):
            av, ak = it["a"]
            aT, aTk = aT_r.next()
            it["aT"] = (aT, aTk)
            for g0 in range(0, KT, GT):
                ng = min(GT, KT - g0)
                bk = tb[ti[0] % 2]; ti[0] += 1
                pv = self.psb16(bk)
                def tr(e, g0=g0, ng=ng, pv=pv):
                    for q in range(ng):
                        ins = e.transpose(pv[:, q * 128:(q + 1) * 128], av[:, (g0 + q) * 128:(g0 + q + 1) * 128], self.ident16)
                    return ins
                s.add("pe", tr, R=[ak, K("id16")], W=[("ps", bk)])
                s.add("act", lambda e, g0=g0, ng=ng, pv=pv: e.activation(
                    aT[:, g0:g0 + ng, :], pv[:, 0:ng * 128].rearrange("p (a t) -> p a t", a=ng), AF.Identity),
                    R=[("ps", bk)], W=[aTk])

        def stage_AV(it):
            h, i = it["h"], it["i"]
            kt, kk, qt, qk, vt, vk = heads[h]
            st, sk = it["st"], it["sk"]
            aT, aTk = it["aT"]
            oh = oi[0] % 2; oi[0] += 1
            O = self.psb(ob, DV, oh * DV)
            def mmO(e):
                for j in range(KT):
                    ins = e.matmul(O, aT[:, j, :], vt[:, j, :], start=(j == 0), stop=(j == KT - 1))
                return ins
            s.add("pe", mmO, R=[aTk, vk] + ([last_on[0]] if last_on[0] is not None else []), W=[("ps", 7, oh)])
            it["O"] = (O, oh)

        def stage_F(it):
            h, i = it["h"], it["i"]
            st, sk = it["st"], it["sk"]
            O, oh = it["O"]
            s.add("act", lambda e: e.activation(junk, O, AF.Square, accum_out=st[:, 9:10]), R=[("ps", 7, oh)], W=[sk, ("ps", 7, oh)])
            self.rstd_ops(st[:, 9:10], st[:, 10:11], DV, RMS_EPS, R=[sk], W=[sk])
            on, onk = on_r.next()
            s.add("dve", lambda e: e.scalar_tensor_tensor(on, O, st[:, 10:11], self.gsub, ALU.mult, ALU.mult),
                  R=[("ps", 7, oh), sk, K("gsub")], W=[onk, ("ps", 7, oh)])
            it["on"] = (on, onk)
            last_on[0] = onk

        def stage_F2(it):
            h, i = it["h"], it["i"]
            on, onk = it["on"]
            bk = tb[ti[0] % 2]; ti[0] += 1
            pv = self.psb16(bk)
            def tr2(e):
                for q in range(2):
                    ins = e.transpose(pv[:, q * 128:(q + 1) * 128], on[:, q * 128:(q + 1) * 128], self.ident16)
                return ins
            s.add("pe", tr2, R=[onk, K("id16")], W=[("ps", bk)])
            ot, otk = ot_r.next()
            s.add("act", lambda e: e.activation(ot, pv[:, 0:256].rearrange("p (a t) -> p a t", a=2), AF.Identity),
                  R=[("ps", bk)], W=[otk])
            s.add("sp", lambda e: e.dma_start(out=self.oTs[i, :, 2 * h:2 * h + 2, :], in_=ot),
                  R=[otk], W=[("oTs", i)], dma=f"atot{otk[1]}")

        N = len(items)
        stage_S(items[0], 0); stage_S(items[0], 1); stage_comb(items[0])
        self.preconvert_w1(0)
        for n in range(N):
            if n + 1 < N:
                stage_S(items[n + 1], 0)
            stage_T(items[n])
            if n + 1 < N:
                stage_S(items[n + 1], 1)
                stage_comb(items[n + 1])
            stage_AV(items[n])
            if n >= 1:
                stage_F2(items[n - 1])
            stage_F(items[n])
        stage_F2(items[N - 1])
        s.phase_fence()
        ar.reset()

    def bcast_row(self, dst, col_fn, key_w, nchunks, psbanks):
        s = self.s
        K = lambda n: ("pers", n)
        gt, gk = self._bc_g, "bc_g"
        for k0 in range(0, nchunks, 4):
            nk = min(4, nchunks - k0)
            bk = psbanks[(k0 // 4) % len(psbanks)]
            def mk(e, k0=k0, nk=nk):
                for q in range(nk):
                    ins = e.activation(gt[:, q, :], self._bc_z, AF.Identity, bias=col_fn(k0 + q), scale=1.0)
                return ins
            s.add("act", mk, R=[K("mods"), "bc_z"], W=[gk])
            def mm(e, nk=nk, bk=bk):
                for q in range(nk):
                    ins = e.matmul(self.psb(bk, 128, q * 128), gt[:, q, :], self.ident32, start=True, stop=True)
                return ins
            s.add("pe", mm, R=[gk, K("id32")], W=[("ps", bk)])
            s.add("dve", lambda e, k0=k0, nk=nk, bk=bk: e.tensor_copy(dst[:, k0 * 128:(k0 + nk) * 128], self.psb(bk, nk * 128)),
                  R=[("ps", bk)], W=[key_w])

    def preconvert_w1(self, l):
        cfg, s = self.cfg, self.s
        DC = cfg.DC
        for fi in range(cfg.DFF // 128):
            s.add("pool", lambda e, fi=fi: e.dma_start(
                out=self.w1s[fi].rearrange("p (k c) -> p k c", k=DC),
                in_=self.mlp_w1[l, :, fi * 128:(fi + 1) * 128].rearrange("(k p) c -> p k c", p=128)),
                W=[("w1s", fi), "w1s_chain"], dma="w1pc")

    def alloc_bcast(self):
        s, ar = self.s, self.ar
        self._bc_g = ar.alloc([4, 128], F32)
        self._bc_z = ar.alloc([128], F32)
        s.mark_local("bc_g"); s.mark_local("bc_z")
        s.add("dve", lambda e: e.memset(self._bc_z, 0.0), W=["bc_z"])

    def proj_tm(self, b, aTs, aname, w, bias, xin, xin_name, xout, xout_name, gate, tag):
        cfg, s, ar = self.cfg, self.s, self.ar
        D, T, DC = cfg.D, cfg.T, cfg.DC
        K = lambda n: ("pers", n)
        NT = T // 128
        l, v, bb = gate
        self.alloc_bcast()
        gbc = ar.alloc([D], F32)
        s.mark_local("gbc")
        for bk in range(8):
            s.mark_local(("ps", bk))
        self.bcast_row(gbc, lambda k: self.mod_col(l, v, bb, k), "gbc", DC, [6, 7])
        if bias is not None:
            bbc = ar.alloc([D], F32)
            s.mark_local("bbc")
            s.add("sp", lambda e: e.dma_start(out=bbc, in_=bias.partition_broadcast(128)), W=["bbc"], dma="bbc")
        CW = min(512, D)
        wr = ar.ring(s, "tmw", 2, [DC, CW], BF16)
        a_r = ar.ring(s, "tma", 4, [DC, 128], BF16)
        x_r = ar.ring(s, "tmx", 4, [CW], F32)
        o_r = ar.ring(s, "tmo", 3, [CW], F32)
        mb = [0, 1, 2, 3]
        mi = 0
        xk_of = (lambda i: xin_name) if isinstance(xin_name, tuple) else (lambda i: (xin_name, i))
        tiles = [(n0, i) for n0 in range(0, D, CW) for i in range(NT)]
        ld = {}
        wcur = {}
        def load(j):
            n0, i = tiles[j]
            if i == 0:
                wt, wk = wr.next()
                s.add("pool", lambda e, wt=wt, n0=n0: e.dma_start(out=wt, in_=w[:, n0:n0 + CW].rearrange("(k p) c -> p k c", p=128)),
                      W=[wk], dma=f"tmw{wk[1]}")
                wcur[n0] = (wt, wk)
            at, ak = a_r.next()
            s.add("sp", lambda e, at=at, i=i: e.dma_start(out=at, in_=aTs[i]), R=[(aname, i)], W=[ak], dma=f"tma{ak[1]}")
            xt, xk = x_r.next()
            s.add("sp", lambda e, xt=xt, i=i, n0=n0: e.dma_start(out=xt, in_=xin[i * 128:(i + 1) * 128, n0:n0 + CW]),
                  R=[xk_of(i)], W=[xk], dma=f"tmx{xk[1]}")
            ld[j] = (at, ak, xt, xk)
        PF = 2
        for j in range(min(PF, len(tiles))):
            load(j)
        for j, (n0, i) in enumerate(tiles):
            if j + PF < len(tiles):
                load(j + PF)
            at, ak, xt, xk = ld.pop(j)
            wt, wk = wcur[n0]
            pb = mb[mi % 4]; mi += 1
            def mm(e, at=at, wt=wt, pb=pb):
                for k in range(DC):
                    ins = e.matmul(self.psb(pb, CW), at[:, k, :], wt[:, k, :], start=(k == 0), stop=(k == DC - 1))
                return ins
            s.add("pe", mm, R=[ak, wk], W=[("ps", pb)])
            ot, ok = o_r.next()
            if bias is not None:
                s.add("dve", lambda e, ot=ot, pb=pb, n0=n0: e.tensor_tensor(ot, self.psb(pb, CW), bbc[:, n0:n0 + CW], ALU.add),
                      R=[("ps", pb), "bbc"], W=[ok])
                s.add("dve", lambda e, ot=ot, n0=n0: e.tensor_tensor(ot, ot, gbc[:, n0:n0 + CW], ALU.mult), R=[ok, "gbc"], W=[ok])
            else:
                s.add("dve", lambda e, ot=ot, pb=pb, n0=n0: e.tensor_tensor(ot, self.psb(pb, CW), gbc[:, n0:n0 + CW], ALU.mult),
                      R=[("ps", pb), "gbc"], W=[ok])
            s.add("dve", lambda e, ot=ot, xt=xt: e.tensor_tensor(ot, ot, xt, ALU.add), R=[ok, xk], W=[ok])
            s.add("sp", lambda e, ot=ot, i=i, n0=n0: e.dma_start(out=xout[i * 128:(i + 1) * 128, n0:n0 + CW], in_=ot),
                  R=[ok], W=[(xout_name, i)], dma=f"tmo{ok[1]}")
        s.phase_fence()
        ar.reset()

    def mlp(self, b, l, xin, xin_name, xout, xout_name):
        cfg, s, ar = self.cfg, self.s, self.ar
        D, T, DC, DFF, G, MT = cfg.D, cfg.T, cfg.DC, cfg.DFF, cfg.G, cfg.MT
        K = lambda n: ("pers", n)
        NG = DFF // G
        GK = G // 128
        NS = MT // 128
        CW = min(512, D)
        self.alloc_bcast()
        gbc = ar.alloc([D], F32)
        s.mark_local("gbc")
        for bk in range(8):
            s.mark_local(("ps", bk))
        self.bcast_row(gbc, lambda k: self.mod_col(l, 5, b, k), "gbc", DC, [6, 7])
        hT = ar.alloc([DC, MT], BF16)
        pro, NPG = self.make_prologue([0, 1], nbuf=1)
        hid_r = ar.ring(s, "hid", 2, [GK, MT], BF16)
        acc = ar.alloc([NS, D], F32)
        w1_r = ar.ring(s, "w1", 2, [DC, 128], BF16)
        sq_r = ar.ring(s, "msq", 2, [MT], F32)
        w2_r = ar.ring(s, "w2", 2, [GK, CW], BF16)
        for i in range(NS):
            for g in range(NPG):
                s.mark_local(("hT", i, g))
            for n in range(D // CW):
                s.mark_local(("acc", i, n))
        hkeys = [("hT", i, g) for i in range(NS) for g in range(NPG)]
        mb1 = [2, 3]
        mb2 = [4, 5, 6, 7]
        m1 = [0]; m2 = [0]
        w1 = self.mlp_w1
        w2 = self.mlp_w2
        xt_r = self.pr_xring
        for tt in range(T // MT):
            tok0 = tt * MT
            for i in range(NS):
                pro(xin[tok0 + i * 128:tok0 + (i + 1) * 128, :], (xin_name, (tok0 // 128) + i),
                    lambda k, i=i: hT[:, k, i * 128:(i + 1) * 128],
                    lambda g, i=i: ("hT", i, g),
                    lambda k: self.A_col(l, 1, b, k),
                    lambda k: self.mod_col(l, 3, b, k))
            def m1_group(g):
                hd, hk = hid_r.next()
                for fc in range(GK):
                    f0 = g * G + fc * 128
                    wt, wk = w1_r.next()
                    fi = f0 // 128
                    if True:
                        s.add("sp", lambda e, wt=wt, fi=fi: e.dma_start(out=wt.rearrange("p k c -> p (k c)"), in_=self.w1s[fi]),
                              R=[("w1s", fi)], W=[wk], dma=f"w1h_{wk[1]}")
                    pb = mb1[m1[0] % 2]; m1[0] += 1
                    def mm(e, wt=wt, pb=pb):
                        for k in range(DC):
                            ins = e.matmul(self.psb(pb, MT), wt[:, k, :], hT[:, k, :], start=(k == 0), stop=(k == DC - 1))
                        return ins
                    s.add("pe", mm, R=[wk] + hkeys, W=[("ps", pb)])
                    sq, sqk = sq_r.next()
                    s.add("act", lambda e, sq=sq, pb=pb: e.activation(sq, self.psb(pb, MT), AF.Square), R=[("ps", pb)], W=[sqk])
                    s.add("dve", lambda e, hd=hd, fc=fc, pb=pb, sq=sq: e.scalar_tensor_tensor(hd[:, fc, :], self.psb(pb, MT), 0.0, sq, ALU.is_gt, ALU.mult),
                          R=[("ps", pb), sqk], W=[hk])
                return hd, hk
            def m2_group(g, hd, hk):
                for n in range(D // CW):
                    wt, wk = w2_r.next()
                    wi = g * (D // CW) + n
                    if tt == 0:
                        s.add("pool", lambda e, wt=wt, g=g, n=n: e.dma_start(
                            out=wt, in_=w2[l, g * G:(g + 1) * G, n * CW:(n + 1) * CW].rearrange("(k p) c -> p k c", p=128)),
                            W=[wk], dma=f"w2_{wk[1]}")
                        if T // MT > 1:
                            s.add("sp", lambda e, wt=wt, wi=wi: e.dma_start(out=self.w2s[wi], in_=wt.rearrange("p k c -> p (k c)")),
                                  R=[wk], W=[("w2s", wi)], dma=f"w2st{wk[1]}")
                    else:
                        s.add("sp", lambda e, wt=wt, wi=wi: e.dma_start(out=wt.rearrange("p k c -> p (k c)"), in_=self.w2s[wi]),
                              R=[("w2s", wi)], W=[wk], dma=f"w2h_{wk[1]}")
                    for i in range(NS):
                        pb = mb2[m2[0] % 4]; m2[0] += 1
                        def mm(e, wt=wt, hd=hd, i=i, pb=pb):
                            for k in range(GK):
                                ins = e.matmul(self.psb(pb, CW), hd[:, k, i * 128:(i + 1) * 128], wt[:, k, :], start=(k == 0), stop=(k == GK - 1))
                            return ins
                        s.add("pe", mm, R=[wk, hk], W=[("ps", pb)])
                        dst = acc[:, i, n * CW:(n + 1) * CW]
                        if g == 0:
                            s.add("act", lambda e, dst=dst, pb=pb: e.activation(dst, self.psb(pb, CW), AF.Identity), R=[("ps", pb)], W=[("acc", i, n)])
                        else:
                            s.add("dve", lambda e, dst=dst, pb=pb: e.tensor_tensor(dst, dst, self.psb(pb, CW), ALU.add),
                                  R=[("ps", pb), ("acc", i, n)], W=[("acc", i, n)])
            prev = m1_group(0)
            for g in range(NG):
                nxt = m1_group(g + 1) if g + 1 < NG else None
                m2_group(g, *prev)
                prev = nxt
            for i in range(NS):
                ti = tok0 // 128 + i
                xt, xk = xt_r.next()
                s.add("sp", lambda e, xt=xt, ti=ti: e.dma_start(out=xt, in_=xin[ti * 128:(ti + 1) * 128, :]), R=[(xin_name, ti)], W=[xk], dma="ep_x")
                akeys = [("acc", i, n) for n in range(D // CW)]
                s.add("dve", lambda e, i=i: e.tensor_tensor(acc[:, i, :], acc[:, i, :], gbc, ALU.mult), R=akeys + ["gbc"], W=akeys)
                s.add("dve", lambda e, i=i, xt=xt: e.tensor_tensor(acc[:, i, :], acc[:, i, :], xt, ALU.add), R=akeys + [xk], W=akeys)
                s.add("sp", lambda e, i=i, ti=ti: e.dma_start(out=xout[ti * 128:(ti + 1) * 128, :], in_=acc[:, i, :]),
                      R=akeys, W=[(xout_name, ti)] + akeys, dma=f"ep_o{i}")
        s.phase_fence()
        ar.reset()

    def conv_block(self, b):
        cfg, s, ar = self.cfg, self.s, self.ar
        D, T, DC, TB = cfg.D, cfg.T, cfg.DC, cfg.TB
        K = lambda n: ("pers", n)
        cn = self.colnames
        hT = ar.alloc([DC, TB], BF16)
        pro, NG = self.make_prologue([0, 1])
        wr = ar.ring(s, "pw1w", 4, [DC, 128], BF16)
        sg_r = ar.ring(s, "sg", 2, [512], F32)
        u_r = ar.ring(s, "uo", 3, [512], F32)
        mb = [2, 3, 4, 5, 6, 7]
        for bk in mb:
            s.mark_local(("ps", bk))
        mi = 0
        for t0 in range(0, T, TB):
            nt = min(TB, T - t0)
            ntile = nt // 128
            for g in range(NG):
                for i in range(ntile):
                    s.mark_local(("hT", i, g))
            for i in range(ntile):
                pro(self.x2[t0 + i * 128:t0 + (i + 1) * 128, :], ("x2", t0 // 128 + i),
                    lambda k, i=i: hT[:, k, i * 128:(i + 1) * 128],
                    lambda g, i=i: ("hT", i, g),
                    lambda k: self.A_col(1, 0, b, k),
                    lambda k: self.mod_col(1, 0, b, k))
            hkeys = lambda tl: [("hT", i, g) for i in tl for g in range(NG)]
            for fc in range(DC):
                wa, wak = wr.next()
                wg, wgk = wr.next()
                s.add("pool", lambda e, wa=wa, fc=fc: e.dma_start(out=wa, in_=self.pw1_w[:, fc * 128:(fc + 1) * 128].rearrange("(k p) c -> p k c", p=128)),
                      W=[wak], dma=f"pw1w{wak[1]}")
                s.add("pool", lambda e, wg=wg, fc=fc: e.dma_start(out=wg, in_=self.pw1_w[:, D + fc * 128:D + (fc + 1) * 128].rearrange("(k p) c -> p k c", p=128)),
                      W=[wgk], dma=f"pw1w{wgk[1]}")
                for ts in range(0, nt, 512):
                    n = min(512, nt - ts)
                    tl = range(ts // 128, (ts + n) // 128)
                    pa = mb[mi % 6]; mi += 1
                    pg = mb[mi % 6]; mi += 1
                    for (wt, wk, pb) in ((wa, wak, pa), (wg, wgk, pg)):
                        def mm(e, wt=wt, pb=pb, ts=ts, n=n):
                            for k in range(DC):
                                ins = e.matmul(self.psb(pb, n), wt[:, k, :], hT[:, k, ts:ts + n], start=(k == 0), stop=(k == DC - 1))
                            return ins
                        s.add("pe", mm, R=[wk] + hkeys(tl), W=[("ps", pb)])
                    sg, sgk = sg_r.next()
                    uo, uk = u_r.next()
                    s.add("act", lambda e, sg=sg, pg=pg, n=n, fc=fc: e.activation(sg[:, 0:n], self.psb(pg, n), AF.Sigmoid, bias=self.colvec(cn[("pw1b", 1)], fc), scale=1.0),
                          R=[("ps", pg), K("cols")], W=[sgk])
                    s.add("dve", lambda e, uo=uo, sg=sg, pa=pa, n=n, fc=fc: e.scalar_tensor_tensor(uo[:, 0:n], self.psb(pa, n), self.colvec(cn[("pw1b", 0)], fc), sg[:, 0:n], ALU.add, ALU.mult),
                          R=[("ps", pa), sgk, K("cols")], W=[uk])
                    s.add("sp", lambda e, uo=uo, fc=fc, t0=t0, ts=ts, n=n: e.dma_start(out=self.uT[fc * 128:(fc + 1) * 128, t0 + ts:t0 + ts + n], in_=uo[:, 0:n]),
                          R=[uk], W=[("uT", fc)], dma=f"uo{uk[1]}")
        s.phase_fence()
        ar.reset()
        ub_r = ar.ring(s, "cv_u", 2, [T + 2 * CONV_PAD], F32)
        ac_r = ar.ring(s, "cv_a", 2, [T], F32)
        a2_r = ar.ring(s, "cv_a2", 2, [T], F32)
        cb_r = ar.ring(s, "cv_c", 2, [T], BF16)
        sq_r = ar.ring(s, "cv_s", 2, [T], BF16)
        for bk in range(8):
            s.mark_local(("ps", bk))
        NTB = (T + 511) // 512
        assert 2 * NTB <= 8
        for j in range(2):
            ub, ubk = ub_r.next()
            s.add("dve", lambda e, ub=ub: e.memset(ub, 0.0), W=[ubk])
        self.preconvert_w1(1)
        for fc in range(DC):
            ub, ubk = ub_r.next()
            s.add("sp", lambda e, ub=ub, fc=fc: e.dma_start(out=ub[:, CONV_PAD:CONV_PAD + T], in_=self.uT[fc * 128:(fc + 1) * 128, :]),
                  R=[("uT", fc)], W=[ubk], dma=f"cvu{ubk[1]}")
            ac, ack = ac_r.next()
            a2, a2k = a2_r.next()
            s.add("dve", lambda e, ub=ub, ac=ac, fc=fc: e.tensor_scalar(ac, ub[:, 0:T], self.colvec(cn[("dw", 0)], fc), self.colvec(cn["dwb"], fc), ALU.mult, ALU.add),
                  R=[ubk, K("cols")], W=[ack])
            s.add("dve", lambda e, ub=ub, a2=a2, fc=fc: e.tensor_scalar(a2, ub[:, 1:1 + T], self.colvec(cn[("dw", 1)], fc), None, ALU.mult),
                  R=[ubk, K("cols")], W=[a2k])
            for j in range(2, CONV_W):
                tgt, tgk = (ac, ack) if j % 2 == 0 else (a2, a2k)
                s.add("dve", lambda e, ub=ub, tgt=tgt, fc=fc, j=j: e.scalar_tensor_tensor(tgt, ub[:, j:j + T], self.colvec(cn[("dw", j)], fc), tgt, ALU.mult, ALU.add),
                      R=[ubk, K("cols"), tgk], W=[tgk])
            s.add("dve", lambda e, ac=ac, a2=a2: e.tensor_tensor(ac, ac, a2, ALU.add), R=[ack, a2k], W=[ack])
            cb, cbk = cb_r.next()
            sq, sqk = sq_r.next()
            s.add("act", lambda e, cb=cb, ac=ac: e.activation(cb, ac, AF.Identity), R=[ack], W=[cbk])
            s.add("act", lambda e, sq=sq, cb=cb: e.activation(sq, cb, AF.Square), R=[cbk], W=[sqk])
            def st(e, cb=cb, sq=sq, fc=fc):
                for tb in range(NTB):
                    n = min(512, T - tb * 512)
                    e.matmul(self.psb(tb, n), self.ones16, cb[:, tb * 512:tb * 512 + n], start=(fc == 0), stop=(fc == DC - 1))
                    ins = e.matmul(self.psb(NTB + tb, n), self.ones16, sq[:, tb * 512:tb * 512 + n], start=(fc == 0), stop=(fc == DC - 1))
                return ins
            s.add("pe", st, R=[cbk, sqk, K("ones16")], W=[("ps", bk) for bk in range(2 * NTB)])
            s.add("sp", lambda e, cb=cb, fc=fc: e.dma_start(out=self.cT[fc * 128:(fc + 1) * 128, :], in_=cb), R=[cbk], W=[("cT", fc)], dma=f"cvc{cbk[1]}")
        mean = ar.alloc([T], F32)
        rstd = ar.alloc([T], F32)
        s.mark_local("lnstat")
        tmpm = ar.alloc([T], F32)
        S1 = self.ps[:, 0:T]
        S2 = self.ps[:, NTB * 512:NTB * 512 + T]
        pk = [("ps", bk) for bk in range(2 * NTB)]
        s.mark_local("tmpm")
        s.add("dve", lambda e: e.tensor_scalar(mean, S1, 1.0 / D, None, ALU.mult), R=pk, W=["lnstat"])
        s.add("dve", lambda e: e.tensor_tensor(tmpm, mean, mean, ALU.mult), R=["lnstat"], W=["tmpm"])
        s.add("dve", lambda e: e.scalar_tensor_tensor(rstd, S2, 1.0 / D, tmpm, ALU.mult, ALU.subtract), R=pk + ["tmpm"], W=["lnstat"])
        s.add("dve", lambda e: e.tensor_scalar(rstd, rstd, LN_EPS, None, ALU.add), R=["lnstat"], W=["lnstat"])
        s.add("act", lambda e: e.activation(rstd, rstd, AF.Sqrt), R=["lnstat"], W=["lnstat"])
        s.add("dve", lambda e: e.reciprocal(rstd, rstd), R=["lnstat"], W=["lnstat"])
        cl_r = ar.ring(s, "ln_c", 2, [T], BF16)
        t_r = ar.ring(s, "ln_t", 2, [T], F32)
        v_r = ar.ring(s, "ln_v", 2, [T], BF16)
        for fc in range(DC):
            cl, clk = cl_r.next()
            s.add("sp", lambda e, cl=cl, fc=fc: e.dma_start(out=cl, in_=self.cT[fc * 128:(fc + 1) * 128, :]), R=[("cT", fc)], W=[clk], dma=f"lnc{clk[1]}")
            tt, tk = t_r.next()
            s.add("dve", lambda e, tt=tt, cl=cl: e.tensor_tensor(tt, cl, mean, ALU.subtract), R=[clk, "lnstat"], W=[tk])
            s.add("dve", lambda e, tt=tt: e.tensor_tensor(tt, tt, rstd, ALU.mult), R=[tk, "lnstat"], W=[tk])
            vv, vk = v_r.next()
            s.add("act", lambda e, vv=vv, tt=tt, fc=fc: e.activation(vv, tt, AF.Silu, bias=self.colvec(cn["lnb"], fc), scale=self.colvec(cn["lng"], fc)),
                  R=[tk, K("cols")], W=[vk])
            s.add("sp", lambda e, vv=vv, fc=fc: e.dma_start(out=self.vTs[:, :, fc, :].rearrange("i p t -> p i t"), in_=vv.rearrange("p (i t) -> p i t", t=128)),
                  R=[vk], W=[("vTs", i) for i in range(T // 128)], dma=f"lnv{vk[1]}")
        s.phase_fence()
        ar.reset()

    def final_norm(self, b):
        cfg, s, ar = self.cfg, self.s, self.ar
        D, T = cfg.D, cfg.T
        gf = ar.alloc([D], F32)
        s.mark_local("gf")
        s.add("sp", lambda e: e.dma_start(out=gf, in_=self.norm_final_g.partition_broadcast(128)), W=["gf"], dma="gf")
        x_r = ar.ring(s, "fn_x", 3, [D], F32)
        junk = ar.alloc([D], BF16)
        st_r = ar.ring(s, "fn_s", 3, [2], F32)
        for i in range(T // 128):
            xt, xk = x_r.next()
            st, sk = st_r.next()
            s.add("sp", lambda e, xt=xt, i=i: e.dma_start(out=xt, in_=self.x4[i * 128:(i + 1) * 128, :]), R=[("x4", i)], W=[xk], dma=f"fnx{xk[1]}")
            s.add("act", lambda e, xt=xt, st=st: e.activation(junk, xt, AF.Square, accum_out=st[:, 0:1]), R=[xk], W=[sk])
            self.rstd_ops(st[:, 0:1], st[:, 1:2], D, RMS_EPS, R=[sk], W=[sk])
            s.add("dve", lambda e, xt=xt, st=st: e.scalar_tensor_tensor(xt, xt, st[:, 1:2], gf, ALU.mult, ALU.mult), R=[xk, sk, "gf"], W=[xk])
            s.add("sp", lambda e, xt=xt, i=i: e.dma_start(out=self.out[b, i * 128:(i + 1) * 128, :], in_=xt), R=[xk], W=[("out", b, i)], dma=f"fno{xk[1]}")
        s.phase_fence()
        ar.reset()


def rope_tables(cfg):
    T, C, GW = cfg.T, cfg.C, cfg.GRID_W
    rows = T // GW
    row = np.repeat(np.arange(rows, dtype=np.float32), GW)
    col = np.tile(np.arange(GW, dtype=np.float32), rows)
    n = 32
    inv = (np.float32(ROPE_BASE) ** (-np.arange(n, dtype=np.float32) / np.float32(n))).astype(np.float32)
    ar_ = row[:, None] * inv
    ac_ = col[:, None] * inv
    ang = np.concatenate([ar_, ar_, ac_, ac_], axis=-1).astype(np.float32)
    cos = np.ones((128, C + T), np.float32)
    sin = np.zeros((128, C + T), np.float32)
    cos[:, C:] = np.cos(ang).T
    sin[:, C:] = np.sin(ang).T
    return cos, sin


def rot_matrix():
    R = np.zeros((128, 128), np.float32)
    for m in range(128):
        blk = (m // 32) % 2
        if blk == 0:
            R[m + 32, m] = -1.0
        else:
            R[m - 32, m] = 1.0
    return R


def make_in_maps(cfg, ncores, inp):
    NB = cfg.NB
    cos, sin = rope_tables(cfg)
    common = dict(
        ada_w=inp["ada_w"], ada_b=inp["ada_b"], norm_mix_g=inp["norm_mix_g"], norm_mlp_g=inp["norm_mlp_g"],
        norm_final_g=inp["norm_final_g"].reshape(1, -1), attn_w_qkv=inp["attn_w_qkv"][0], attn_w_o=inp["attn_w_o"][0],
        lam_in=np.concatenate([inp["lambda_q1"], inp["lambda_k1"], inp["lambda_q2"], inp["lambda_k2"]], 0),
        attn_subln_g=inp["attn_subln_g"], conv_pw1_w=inp["conv_pw1_w"][0], conv_pw1_b=inp["conv_pw1_b"][0].reshape(2, -1),
        conv_dw_w=inp["conv_dw_w"][0], conv_dw_b=inp["conv_dw_b"], conv_ln_g=inp["conv_ln_g"], conv_ln_b=inp["conv_ln_b"],
        conv_pw2_w=inp["conv_pw2_w"][0], conv_pw2_b=inp["conv_pw2_b"], mlp_w1=inp["mlp_w1"], mlp_w2=inp["mlp_w2"],
        rope_cos=cos, rope_sin=sin, rot_m=rot_matrix(), ident=np.eye(128, dtype=np.float32),
    )
    common = {k: np.ascontiguousarray(np.asarray(v, dtype=np.float32)) for k, v in common.items()}
    maps = []
    for c in range(ncores):
        sl = slice(c * NB, (c + 1) * NB)
        m = dict(common)
        m["x"] = np.ascontiguousarray(inp["x"][sl])
        m["ctx"] = np.ascontiguousarray(inp["ctx"][sl])
        m["cvec"] = np.ascontiguousarray(np.concatenate([inp["c"][sl], inp["c_ctx"].reshape(1, -1)], 0))
        maps.append(m)
    return maps


NCORES = 8


def kernel(**inputs):
    inp = {k: np.asarray(v) for k, v in inputs.items()}
    B = inp["x"].shape[0]
    cfg = Cfg(D=inp["x"].shape[2], T=inp["x"].shape[1], C=inp["ctx"].shape[1], DFF=inp["mlp_w1"].shape[2], NB=B // NCORES)
    bld = Builder(cfg)
    nc = bld.build()
    maps = make_in_maps(cfg, NCORES, inp)
    res = run_bass_kernel_spmd(nc, maps, core_ids=list(range(NCORES)))
    outs = [np.asarray(r["out"]) for r in res.results]
    return np.concatenate(outs, axis=0).astype(np.float32)
```
